# Optimizing a Trainium2 kernel written in Bass

```python
import jax
import jax.numpy as jnp
from jax import lax
import numpy as np

D_MODEL = 1024
BATCH = 8
SEQ = 4096
DEPTH = 4

GRID_W = 64
CTX_LEN = 256
N_MIXERS = 2
RET_HEADS = 4
RET_QK_DIM = D_MODEL // RET_HEADS
RET_V_DIM = 2 * D_MODEL // RET_HEADS
RET_CHUNK = 128
NAT_HEAD_DIM = 32
NAT_HEADS = D_MODEL // NAT_HEAD_DIM
NAT_WIN_ROWS = 8
NAT_WIN_COLS = 16
MLP_HIDDEN = 4 * D_MODEL
ROPE_BASE = 10000.0
LN_EPS = 1e-5
DN_ALPHA = (2.0 * DEPTH) ** 0.25
DN_BETA = (8.0 * DEPTH) ** -0.25
N_RET_LAYERS = (DEPTH + 1) // 2
N_NAT_LAYERS = DEPTH // 2

kernel_name = 'hybrid_retention_natten_dit'


def layer_norm(x, g, b):
    xf = x.astype(jnp.float32)
    mu = jnp.mean(xf, axis=-1, keepdims=True)
    var = jnp.mean(jnp.square(xf - mu), axis=-1, keepdims=True)
    return ((xf - mu) * lax.rsqrt(var + LN_EPS) * g + b).astype(x.dtype)


def axial_rope_angles(n, axis_dim):
    t = jnp.arange(n)
    row = (t // GRID_W).astype(jnp.float32)
    col = (t % GRID_W).astype(jnp.float32)
    inv = 1.0 / (ROPE_BASE ** (jnp.arange(0, axis_dim, 2, dtype=jnp.float32) / axis_dim))
    return row[:, None] * inv, col[:, None] * inv


def rope_rotate(x, ang):
    cos = jnp.cos(ang).astype(x.dtype)
    sin = jnp.sin(ang).astype(x.dtype)
    x1, x2 = jnp.split(x, 2, axis=-1)
    return jnp.concatenate([x1 * cos - x2 * sin, x2 * cos + x1 * sin], axis=-1)


def axial_rope(x, ang_row, ang_col):
    xr, xc = jnp.split(x, 2, axis=-1)
    return jnp.concatenate([rope_rotate(xr, ang_row), rope_rotate(xc, ang_col)], axis=-1)


def retention_chunk_scan(q, k, v, log_gamma, state0, with_out):
    b, h, n, dk = q.shape
    dv = v.shape[-1]
    cs = RET_CHUNK
    nc = n // cs
    dt = v.dtype
    pos = jnp.arange(cs, dtype=jnp.float32)
    lg = log_gamma.astype(jnp.float32)[:, None]
    rel = pos[:, None] - pos[None, :]
    intra = jnp.where(rel >= 0, jnp.exp(lg[:, :, None] * jnp.maximum(rel, 0.0)), 0.0).astype(dt)
    q_dec = jnp.exp(lg * (pos + 1.0))[..., None].astype(dt)
    k_dec = jnp.exp(lg * (cs - 1.0 - pos))[..., None].astype(dt)
    chunk_dec = jnp.exp(lg * cs)[..., None].astype(dt)

    def to_chunks(t):
        return jnp.moveaxis(t.reshape(b, h, nc, cs, t.shape[-1]), 2, 0)

    def step(state, qkv):
        qc, kc, vc = qkv
        new_state = state * chunk_dec + jnp.einsum('bhsd,bhse->bhde', kc * k_dec, vc)
        if with_out:
            scores = jnp.einsum('bhcd,bhsd->bhcs', qc, kc) * intra
            out = (jnp.einsum('bhcs,bhse->bhce', scores, vc)
                   + jnp.einsum('bhcd,bhde->bhce', qc * q_dec, state))
            return new_state, out
        return new_state, None

    state, outs = lax.scan(step, state0, (to_chunks(q), to_chunks(k), to_chunks(v)))
    if with_out:
        outs = jnp.moveaxis(outs, 0, 2).reshape(b, h, n, dv)
    return state, outs


def retention_mixer(h_lat, h_ctx, w_in, w_o, decay_param, ang_row, ang_col, ctx_out):
    b = h_lat.shape[0]
    qk_w = RET_HEADS * RET_QK_DIM
    v_w = RET_HEADS * RET_V_DIM

    def heads(t, hd):
        return t.reshape(b, t.shape[1], RET_HEADS, hd).transpose(0, 2, 1, 3)

    def project(h, w):
        p = h @ w
        q = heads(p[..., :qk_w], RET_QK_DIM)
        k = heads(p[..., qk_w:2 * qk_w], RET_QK_DIM) * (RET_QK_DIM ** -0.5)
        v = heads(p[..., 2 * qk_w:2 * qk_w + v_w], RET_V_DIM)
        return q, k, v, p[..., 2 * qk_w + v_w:]

    def flip(t):
        return t[:, :, ::-1]

    def finish(o, g):
        of = o.astype(jnp.float32)
        mu = jnp.mean(of, axis=-1, keepdims=True)
        var = jnp.mean(jnp.square(of - mu), axis=-1, keepdims=True)
        on = ((of - mu) * lax.rsqrt(var + LN_EPS)).astype(o.dtype)
        on = on.transpose(0, 2, 1, 3).reshape(b, o.shape[2], v_w)
        return (on * jax.nn.silu(g)) @ w_o

    log_gamma = -jnp.exp(decay_param.astype(jnp.float32))
    w_ctx = w_in if ctx_out else w_in[:, :2 * qk_w + v_w]
    q_c, k_c, v_c, g_c = project(h_ctx, w_ctx)
    zeros = jnp.zeros((b, RET_HEADS, RET_QK_DIM, RET_V_DIM), v_c.dtype)
    s_f, o_cf = retention_chunk_scan(q_c, k_c, v_c, log_gamma[0], zeros, ctx_out)
    s_b, o_cb = retention_chunk_scan(flip(q_c), flip(k_c), flip(v_c), log_gamma[1], zeros, ctx_out)

    q_l, k_l, v_l, g_l = project(h_lat, w_in)
    q_l = axial_rope(q_l, ang_row, ang_col)
    k_l = axial_rope(k_l, ang_row, ang_col)
    _, o_lf = retention_chunk_scan(q_l, k_l, v_l, log_gamma[0], s_f, True)
    _, o_lb = retention_chunk_scan(flip(q_l), flip(k_l), flip(v_l), log_gamma[1], s_b, True)
    y_lat = finish(o_lf + flip(o_lb), g_l)
    y_ctx = finish(o_cf + flip(o_cb), g_c) if ctx_out else None
    return y_lat, y_ctx


def neighbourhood_mixer(h_lat, h_ctx, w_in, w_o, rpb, ctx_out):
    b, n, d = h_lat.shape
    rows = n // GRID_W
    kh = min(NAT_WIN_ROWS, rows)
    kw = NAT_WIN_COLS
    nh, hd = NAT_HEADS, NAT_HEAD_DIM
    scale = hd ** -0.5

    p = h_lat @ w_in
    q = p[..., :d].reshape(b, rows, GRID_W, nh, hd) * scale
    k = p[..., d:2 * d].reshape(b, rows, GRID_W, nh, hd)
    v = p[..., 2 * d:].reshape(b, rows, GRID_W, nh, hd)

    pc = h_ctx @ (w_in if ctx_out else w_in[:, d:])
    pc = pc[..., d:] if ctx_out else pc
    L = h_ctx.shape[1]
    k_c = pc[..., :d].reshape(b, L, nh, hd)
    v_c = pc[..., d:].reshape(b, L, nh, hd)

    row_start = jnp.clip(jnp.arange(rows) - kh // 2, 0, rows - kh)
    cols = jnp.arange(GRID_W)
    col_start = jnp.clip(cols - kw // 2, 0, GRID_W - kw)
    col_in = (cols[None, :] >= col_start[:, None]) & (cols[None, :] < col_start[:, None] + kw)
    dc_idx = jnp.clip(cols[None, :] - cols[:, None] + NAT_WIN_COLS - 1, 0, 2 * NAT_WIN_COLS - 2)
    rpb_f = rpb.astype(jnp.float32)
    band = kh * GRID_W

    def row_block(args):
        q_r, r = args
        start = row_start[r]
        k_band = lax.dynamic_slice_in_dim(k, start, kh, axis=1)
        v_band = lax.dynamic_slice_in_dim(v, start, kh, axis=1)
        dr_idx = start + jnp.arange(kh) - r + NAT_WIN_ROWS - 1
        bias = rpb_f[:, dr_idx[None, :, None], dc_idx[:, None, :]]
        bias = jnp.where(col_in[:, None, :], bias, -jnp.inf)
        s_loc = jnp.einsum('bqhd,bikhd->bhqik', q_r, k_band).astype(jnp.float32) + bias
        s_ctx = jnp.einsum('bqhd,blhd->bhql', q_r, k_c).astype(jnp.float32)
        s = jnp.concatenate([s_loc.reshape(b, nh, GRID_W, band), s_ctx], axis=-1)
        prob = jax.nn.softmax(s, axis=-1).astype(v.dtype)
        p_loc = prob[..., :band].reshape(b, nh, GRID_W, kh, GRID_W)
        return (jnp.einsum('bhqik,bikhd->bqhd', p_loc, v_band)
                + jnp.einsum('bhql,blhd->bqhd', prob[..., band:], v_c))

    o = lax.map(row_block, (jnp.moveaxis(q, 1, 0), jnp.arange(rows)))
    y_lat = jnp.moveaxis(o, 0, 1).reshape(b, n, d) @ w_o

    y_ctx = None
    if ctx_out:
        q_c = pc_q = (h_ctx @ w_in[:, :d]).reshape(b, L, nh, hd) * scale
        s = jnp.einsum('blhd,bmhd->bhlm', q_c, k_c).astype(jnp.float32)
        prob = jax.nn.softmax(s, axis=-1).astype(v_c.dtype)
        y_ctx = jnp.einsum('bhlm,bmhd->blhd', prob, v_c).reshape(b, L, d) @ w_o
    return y_lat, y_ctx


def sq_relu_mlp(h, w1, w2):
    return jnp.square(jax.nn.relu(h @ w1)) @ w2


def setup_inputs(seed: int = 0) -> dict:
    key = jax.random.key(seed)
    ks = jax.random.split(key, 17)
    f32 = jnp.float32

    def nrm(k, shape, scale):
        return jax.random.normal(k, shape, f32) * scale

    qk_w = RET_HEADS * RET_QK_DIM
    v_w = RET_HEADS * RET_V_DIM
    base_decay = jnp.log(-jnp.log(1.0 - 2.0 ** (-5.0 - jnp.arange(RET_HEADS, dtype=f32))))
    return {
        'x': nrm(ks[0], (BATCH, SEQ, D_MODEL), 1.0),
        'c': nrm(ks[1], (BATCH, D_MODEL), 1.0),
        'ctx': nrm(ks[2], (BATCH, CTX_LEN, D_MODEL), 1.0),
        'c_ctx': nrm(ks[3], (D_MODEL,), 1.0),
        'ada_w': nrm(ks[4], (DEPTH, D_MODEL, 6 * D_MODEL), 0.5 * D_MODEL ** -0.5),
        'ada_b': nrm(ks[5], (DEPTH, 6 * D_MODEL), 0.02),
        'ret_w_in': nrm(ks[6], (N_RET_LAYERS, D_MODEL, 2 * qk_w + 2 * v_w), D_MODEL ** -0.5),
        'ret_w_o': nrm(ks[7], (N_RET_LAYERS, v_w, D_MODEL), DN_BETA * v_w ** -0.5),
        'ret_decay': base_decay + nrm(ks[8], (N_RET_LAYERS, 2, RET_HEADS), 0.1),
        'nat_w_in': nrm(ks[9], (N_NAT_LAYERS, D_MODEL, 3 * D_MODEL), D_MODEL ** -0.5),
        'nat_w_o': nrm(ks[10], (N_NAT_LAYERS, D_MODEL, D_MODEL), DN_BETA * D_MODEL ** -0.5),
        'nat_rpb': nrm(ks[11], (N_NAT_LAYERS, NAT_HEADS, 2 * NAT_WIN_ROWS - 1, 2 * NAT_WIN_COLS - 1), 0.1),
        'mlp_w1': nrm(ks[12], (DEPTH, D_MODEL, MLP_HIDDEN), D_MODEL ** -0.5),
        'mlp_w2': nrm(ks[13], (DEPTH, MLP_HIDDEN, D_MODEL), DN_BETA * MLP_HIDDEN ** -0.5),
        'ln_g': 1.0 + nrm(ks[14], (DEPTH, 2, D_MODEL), 0.02),
        'ln_b': nrm(ks[15], (DEPTH, 2, D_MODEL), 0.02),
    }


def reference(x, c, ctx, c_ctx, ada_w, ada_b, ret_w_in, ret_w_o, ret_decay,
              nat_w_in, nat_w_o, nat_rpb, mlp_w1, mlp_w2, ln_g, ln_b):
    n = x.shape[1]
    ang_row, ang_col = axial_rope_angles(n, RET_QK_DIM // 2)
    silu_c = jax.nn.silu(c)
    silu_cc = jax.nn.silu(c_ctx)
    for i in range(DEPTH):
        last = i == DEPTH - 1
        j = i // N_MIXERS
        mod = (silu_c @ ada_w[i] + ada_b[i])[:, None, :]
        mod_c = silu_cc @ ada_w[i] + ada_b[i]
        sh1, sc1, g1, sh2, sc2, g2 = jnp.split(mod, 6, axis=-1)
        sh1c, sc1c, g1c, sh2c, sc2c, g2c = jnp.split(mod_c, 6, axis=-1)

        h = x * (1.0 + sc1) + sh1
        hc = ctx * (1.0 + sc1c) + sh1c
        if i % N_MIXERS == 0:
            y, yc = retention_mixer(h, hc, ret_w_in[j], ret_w_o[j], ret_decay[j],
                                    ang_row, ang_col, not last)
        else:
            y, yc = neighbourhood_mixer(h, hc, nat_w_in[j], nat_w_o[j], nat_rpb[j], not last)

        x = layer_norm(DN_ALPHA * x + g1 * y, ln_g[i, 0], ln_b[i, 0])
        h = x * (1.0 + sc2) + sh2
        x = layer_norm(DN_ALPHA * x + g2 * sq_relu_mlp(h, mlp_w1[i], mlp_w2[i]), ln_g[i, 1], ln_b[i, 1])

        if not last:
            ctx = layer_norm(DN_ALPHA * ctx + g1c * yc, ln_g[i, 0], ln_b[i, 0])
            hc = ctx * (1.0 + sc2c) + sh2c
            ctx = layer_norm(DN_ALPHA * ctx + g2c * sq_relu_mlp(hc, mlp_w1[i], mlp_w2[i]),
                             ln_g[i, 1], ln_b[i, 1])
    return x
```

```python
import contextlib
import numpy as np
import concourse.bass as bass
import concourse.mybir as mybir
from concourse.bass_utils import run_bass_kernel_spmd

F32 = mybir.dt.float32
BF16 = mybir.dt.bfloat16
I32 = mybir.dt.int32
AF = mybir.ActivationFunctionType
ALU = mybir.AluOpType

D = 1024
SEQ = 4096
CTX = 256
TOK = SEQ + CTX
NT = TOK // 128
DEPTH = 4
ALPHA = (2.0 * DEPTH) ** 0.25
EPS = 1e-5
NEG = -30000.0
NAT_SCALE = 32 ** -0.5

EPOCH = 3000
STREAMS = ("pe", "act", "dve", "pool", "sp")


class Op:
    __slots__ = ("stream", "fn", "is_dma", "deps", "signal", "cnt", "dsem", "dval")

    def __init__(self, stream, fn, is_dma):
        self.stream = stream
        self.fn = fn
        self.is_dma = is_dma
        self.deps = []
        self.signal = False
        self.cnt = None
        self.dsem = None
        self.dval = None


class Prog:
    def __init__(self, nc, n_dma_sems=16):
        self.nc = nc
        self.ops = {s: [] for s in STREAMS}
        self.last_w = {}
        self.readers = {}
        self.n_dma_sems = n_dma_sems
        self.dma_rr = {s: 0 for s in STREAMS}
        self.dma_cnt = {}
        self.dma_last = {}
        self.pending = {s: [] for s in STREAMS}

    def _need(self, op, dep):
        if dep is None or dep is op:
            return
        if not dep.is_dma:
            dep.signal = True
        op.deps.append(dep)

    def barrier(self):
        deps = []
        for s in STREAMS:
            for op in reversed(self.ops[s]):
                if not op.is_dma:
                    deps.append(op)
                    break
        deps.extend(self.dma_last.values())
        for s in STREAMS:
            self.pending[s] = list(deps)

    def add(self, stream, fn, reads=(), writes=(), dma=False):
        op = Op(stream, fn, dma)
        if self.pending[stream]:
            for dep in self.pending[stream]:
                self._need(op, dep)
            self.pending[stream] = []
        if dma:
            slot = self.dma_rr[stream]
            self.dma_rr[stream] = (slot + 1) % self.n_dma_sems
            key = (stream, slot)
            prev = self.dma_last.get(key)
            if prev is not None:
                self._need(op, prev)
            op.dsem = key
            op.dval = self.dma_cnt.get(key, 0) + 16
            self.dma_cnt[key] = op.dval
            self.dma_last[key] = op
        for r in reads:
            w = self.last_w.get(r)
            if w is not None:
                if w.stream == stream and stream == "pe" and not w.is_dma and not dma:
                    pass
                else:
                    self._need(op, w)
        for wk in writes:
            w = self.last_w.get(wk)
            if w is not None and not (w.stream == stream and not w.is_dma and not dma):
                self._need(op, w)
            for rd in self.readers.get(wk, ()):
                if rd.stream == stream and not rd.is_dma and not dma:
                    continue
                self._need(op, rd)
        for r in reads:
            lst = self.readers.setdefault(r, [])
            if not dma:
                for i_, o_ in enumerate(lst):
                    if o_.stream == stream and not o_.is_dma:
                        lst[i_] = op
                        break
                else:
                    lst.append(op)
            else:
                lst.append(op)
        for wk in writes:
            self.last_w[wk] = op
            self.readers[wk] = []
        self.ops[stream].append(op)
        return op

    def emit(self):
        nc = self.nc
        nsig = {}
        for s in STREAMS:
            c = 0
            for op in self.ops[s]:
                if op.signal and not op.is_dma:
                    op.cnt = c
                    c += 1
            nsig[s] = c
        with contextlib.ExitStack() as es:
            esem = {}
            for s in STREAMS:
                n = (nsig[s] + EPOCH - 1) // EPOCH
                esem[s] = [es.enter_context(nc.semaphore(f"e_{s}_{i}")) for i in range(max(n, 1))]
            dsem = {}
            for key in self.dma_cnt:
                dsem[key] = es.enter_context(nc.semaphore(f"d_{key[0]}_{key[1]}"))
            block = es.enter_context(nc.Block())

            def run(stream, eng):
                waited = {}
                for op in self.ops[stream]:
                    for dep in op.deps:
                        if dep.is_dma:
                            k = ("d", dep.dsem)
                            v = dep.dval
                            if waited.get(k, 0) >= v:
                                continue
                            waited[k] = v
                            eng.wait_ge(dsem[dep.dsem], v)
                        else:
                            k = ("e", dep.stream)
                            v = dep.cnt
                            if waited.get(k, -1) >= v:
                                continue
                            waited[k] = v
                            eng.wait_ge(esem[dep.stream][v // EPOCH], v % EPOCH + 1)
                    ins = op.fn(eng)
                    if op.is_dma:
                        ins.then_inc(dsem[op.dsem], 16)
                    elif op.signal:
                        ins.then_inc(esem[stream][op.cnt // EPOCH], 1)
                for key, val in self.dma_cnt.items():
                    if key[0] == stream:
                        eng.wait_ge(dsem[key], val)

            @block.tensor
            def _(e):
                run("pe", e)

            @block.scalar
            def _(e):
                run("act", e)

            @block.vector
            def _(e):
                run("dve", e)

            @block.gpsimd
            def _(e):
                run("pool", e)

            @block.sync
            def _(e):
                run("sp", e)
        return nsig


class Arena:
    def __init__(self, t, words):
        self.t = t
        self.words = words
        self.off = 0
        self.peak = 0

    def f32(self, n, dt=F32):
        assert self.off + n <= self.words, ("arena overflow", self.off, n, self.words)
        ap = self.t[:, self.off:self.off + n]
        self.off += (n + 15) // 16 * 16
        self.peak = max(self.peak, self.off)
        return ap if dt == F32 else ap.bitcast(dt)

    def bf16(self, n):
        assert n % 2 == 0
        return self.f32(n // 2).bitcast(BF16)


ARENA_WORDS = 196 * 256
NSTAGE = 4


class _Stop(Exception):
    pass


def build(n_layers=DEPTH, dbg=None, stop=None):
    nc = bass.Bass("TRN2", target_bir_lowering=False)

    def din(name, shape, dt=F32):
        return nc.dram_tensor(name, list(shape), dt, kind="ExternalInput").ap()

    def dscr(name, shape, dt=F32):
        return nc.dram_tensor(name, list(shape), dt).ap()

    xs_d = din("xs", [TOK, D])
    cvT_d = din("cvT", [128, 8, 2])
    ada_w_d = din("ada_w", [DEPTH, D, 6 * D])
    ada_b_d = din("ada_b", [DEPTH, 6 * D])
    ada_bT_d = din("ada_bT", [DEPTH, 128, 48])
    ret_w_in_d = din("ret_w_in", [2, D, 6 * D])
    ret_w_o_d = din("ret_w_o", [2, 2 * D, D])
    ret_decay_d = din("ret_decay", [2, 8])
    nat_w_in_d = din("nat_w_in", [2, D, 3 * D])
    nat_w_o_d = din("nat_w_o", [2, D, D])
    pad_d = din("pad", [2, 8, 32, 20, 127])
    colmask_d = din("colmask", [128, 128])
    rope_d = din("rope", [128, 4, SEQ])
    mlp_w1_d = din("mlp_w1", [DEPTH, D, 4 * D])
    mlp_w2_d = din("mlp_w2", [DEPTH, 4 * D, D])
    ln_g_d = din("ln_g", [DEPTH, 2, D])
    ln_b_d = din("ln_b", [DEPTH, 2, D])
    y_d = nc.dram_tensor("y", [SEQ, D], F32, kind="ExternalOutput").ap()

    XM_d = dscr("XM", [TOK, D])
    XN_d = dscr("XN", [TOK, D])
    MODR_d = dscr("MODR", [DEPTH, 2, 2, D])
    QT_d = dscr("QT", [NT, 128, 1024], BF16)
    KT_d = dscr("KT", [NT, 128, 1024], BF16)
    KD_d = dscr("KD", [2, NT, 128, 1024], BF16)
    V_d = dscr("V", [NT, 128, 2048], BF16)
    SG_d = dscr("SG", [NT, 128, 2048], BF16)
    OB_d = dscr("OB", [NT, 128, 2048], F32)
    VN_d = dscr("VN", [NT, 128, 1056], BF16)
    BIAS_d = dscr("BIAS", [5, 32, 5, 128, 128])
    dbg_outs = {}
    if dbg:
        for name, shape in dbg.items():
            dbg_outs[name] = nc.dram_tensor("dbg_" + name, list(shape), F32, kind="ExternalOutput").ap()

    P = Prog(nc)

    def ck(name):
        if stop == name:
            raise _Stop()

    es = contextlib.ExitStack()
    arena_t = es.enter_context(nc.sbuf_tensor("arena", [128, ARENA_WORDS], F32))
    ps_t = es.enter_context(nc.psum_tensor("ps", [128, 4096], F32))
    A = Arena(arena_t, ARENA_WORDS)

    def bank(b, n=1):
        return ps_t[:, b * 512:(b + n) * 512]

    def bank_bf(b, n=1):
        return ps_t[:, b * 512:(b + n) * 512].bitcast(BF16)

    def dma(q, out, in_, r=(), w=()):
        P.add(q, lambda e: e.dma_start(out=out, in_=in_), reads=r, writes=w, dma=True)

    def mm(out, lhsT, rhs, start, stop, r, w, tp=None):
        if tp is None:
            P.add("pe", lambda e: e.matmul(out, lhsT=lhsT, rhs=rhs, start=start, stop=stop), reads=r, writes=w)
        else:
            P.add("pe", lambda e: e.matmul(out, lhsT=lhsT, rhs=rhs, start=start, stop=stop, tile_position=tp), reads=r, writes=w)

    def tr(out, in_, ident, r, w):
        P.add("pe", lambda e: e.transpose(out=out, in_=in_, identity=ident), reads=list(r) + ["ident"], writes=w)

    def act(out, in_, func, r, w, scale=None, bias=None):
        kw = {}
        if scale is not None:
            kw["scale"] = scale
        if bias is not None:
            kw["bias"] = bias
        P.add("act", lambda e: e.activation(out=out, in_=in_, func=func, **kw), reads=r, writes=w)

    def tt(eng, out, in0, in1, op, r, w):
        P.add(eng, lambda e: e.tensor_tensor(out=out, in0=in0, in1=in1, op=op), reads=r, writes=w)

    def ts(eng, out, in0, s1, op0, r, w, s2=None, op1=None):
        if op1 is None:
            P.add(eng, lambda e: e.tensor_scalar(out=out, in0=in0, scalar1=s1, scalar2=None, op0=op0), reads=r, writes=w)
        else:
            P.add(eng, lambda e: e.tensor_scalar(out=out, in0=in0, scalar1=s1, scalar2=s2, op0=op0, op1=op1), reads=r, writes=w)

    def stt(eng, out, in0, scalar, in1, op0, op1, r, w):
        P.add(eng, lambda e: e.scalar_tensor_tensor(out=out, in0=in0, scalar=scalar, in1=in1, op0=op0, op1=op1), reads=r, writes=w)

    def cp(eng, out, in_, r, w):
        P.add(eng, lambda e: e.tensor_copy(out=out, in_=in_), reads=r, writes=w)

    def memset(eng, out, val, w, r=()):
        P.add(eng, lambda e: e.memset(out, val), reads=r, writes=w)

    identf = A.f32(128)
    ident = A.bf16(128)
    memset("pool", identf, 0.0, w=["identf"])
    P.add("pool", lambda e: e.affine_select(out=identf, in_=identf, pattern=[[-1, 128]], compare_op=ALU.not_equal,
                                            fill=1.0, base=0, channel_multiplier=1), reads=["identf"], writes=["identf"])
    cp("dve", ident, identf, r=["identf"], w=["ident"])
    epsc = A.f32(1)
    memset("pool", epsc, EPS, w=["epsc"])
    modT = A.f32(DEPTH * 2 * 4 * 8).rearrange("p (l v s c) -> p l v s c", l=DEPTH, v=2, s=4)
    persist_off = A.off

    def phase_mods():
        A.off = persist_off
        cv = A.f32(16)
        scT = A.f32(16)
        sig = A.f32(16)
        dma("sp", cv, cvT_d.rearrange("p c r -> p (c r)"), w=["cv"])
        act(sig, cv, AF.Sigmoid, r=["cv"], w=["sig"])
        tt("dve", scT, cv, sig, ALU.mult, r=["cv", "sig"], w=["scT"])
        scT3 = scT.rearrange("p (c r) -> p c r", r=2)
        wb = [A.f32(8 * 512) for _ in range(3)]
        abT = A.f32(DEPTH * 48)
        dma("sp", abT.rearrange("p (l c) -> p l c", l=DEPTH), ada_bT_d.rearrange("l p c -> p l c"), w=["abT"])
        abR = A.f32(DEPTH * 2 * D)
        rowo = A.f32(DEPTH * 2 * D)
        for l in range(n_layers):
            for wi, sec in enumerate((2, 5)):
                dma("sp", abR[0:2, (l * 2 + wi) * D:(l * 2 + wi + 1) * D],
                    ada_b_d[l:l + 1, sec * D:(sec + 1) * D].partition_broadcast(2), w=["abR"])
        it = 0
        for l in range(n_layers):
            for grp in range(12):
                s = it % 3
                pb_ = it % 2
                it += 1
                wv = wb[s].rearrange("p (c n) -> p c n", c=8)
                dma("sp" if it % 2 else "act", wv, ada_w_d[l].rearrange("(c p) n -> p c n", p=128)[:, :, grp * 512:(grp + 1) * 512],
                    w=[f"wb{s}"])
                sec = grp // 2
                if sec in (2, 5):
                    wi = 0 if sec == 2 else 1
                    half = grp % 2
                    for kc in range(8):
                        mm(bank(2 + pb_)[0:2, :], lhsT=scT3[:, kc, :], rhs=wv[:, kc, :], start=kc == 0, stop=kc == 7,
                           r=["scT", f"wb{s}"], w=[("ps", 2 + pb_)])
                    o = (l * 2 + wi) * D + half * 512
                    tt("dve", rowo[0:2, o:o + 512], bank(2 + pb_)[0:2, :], abR[0:2, o:o + 512], ALU.add,
                       r=[("ps", 2 + pb_), "abR"], w=["rowo"])
                else:
                    si = {0: 0, 1: 1, 3: 2, 4: 3}[sec]
                    for cc in range(4):
                        for kc in range(8):
                            mm(bank(pb_)[:, cc * 2:cc * 2 + 2], lhsT=wv[:, kc, cc * 128:(cc + 1) * 128], rhs=scT3[:, kc, :],
                               start=kc == 0, stop=kc == 7, r=["scT", f"wb{s}"], w=[("ps", pb_)])
                    for v in range(2):
                        c0 = (grp % 2) * 4
                        ab = abT[:, l * 48 + grp * 4:l * 48 + grp * 4 + 4]
                        src = bank(pb_)[:, 0:8].rearrange("p (c v) -> p c v", v=2)[:, :, v]
                        tt("dve", modT[:, l, v, si, c0:c0 + 4], src, ab, ALU.add, r=[("ps", pb_), "abT"], w=["modT"])
        for l in range(n_layers):
            for v in range(2):
                for si in (1, 3):
                    ts("dve", modT[:, l, v, si, :], modT[:, l, v, si, :], 1.0, ALU.add, r=["modT"], w=["modT"])
        for l in range(n_layers):
            for wi in range(2):
                o = (l * 2 + wi) * D
                dma("pool", MODR_d[l, :, wi, :], rowo[0:2, o:o + D], r=["rowo"], w=["MODR"])
        P.barrier()
        ck('mods')

    def load_row_bc(dst, src_row, w, r=()):
        dma("sp", dst, src_row.partition_broadcast(128), r=r, w=w)

    def make_hT(xt, xkey, hT_view, hkey, l, ver, si_sh, si_sc, tb):
        pt = bank(tb, 2)
        for c in range(8):
            tr(pt[:, c * 128:(c + 1) * 128], xt[:, c * 128:(c + 1) * 128], identf, r=[xkey], w=[("ps", tb), ("ps", tb + 1)])
        for c in range(8):
            act(hT_view[:, c, :], pt[:, c * 128:(c + 1) * 128], AF.Identity, r=[("ps", tb), ("ps", tb + 1), "modT"], w=[hkey],
                scale=modT[:, l, ver, si_sc, c:c + 1], bias=modT[:, l, ver, si_sh, c:c + 1])

    class LNctx:
        pass

    def alloc_ln(nbuf=2):
        c = LNctx()
        c.gtab = A.f32(D)
        c.lng = A.f32(D)
        c.lnb = A.f32(D)
        c.r = [A.f32(D) for _ in range(nbuf)]
        c.st = [A.f32(12) for _ in range(nbuf)]
        c.mv = [A.f32(4) for _ in range(nbuf)]
        c.n = 0
        c.nbuf = nbuf
        return c

    def load_ln_tabs(c, l, which, ver):
        load_row_bc(c.gtab, MODR_d[l, ver, which:which + 1, :], w=["gtab"], r=["MODR"])
        load_row_bc(c.lng, ln_g_d[l, which:which + 1, :], w=["lng"])
        load_row_bc(c.lnb, ln_b_d[l, which:which + 1, :], w=["lnb"])

    def res_ln(c, yb0, yb1, xt, xkey, dst_d, dst_key):
        s = c.n % c.nbuf
        c.n += 1
        r = c.r[s]
        rk = f"lnr{s}"
        for g, yb in enumerate((yb0, yb1)):
            tt("dve", r[:, g * 512:(g + 1) * 512], bank(yb), c.gtab[:, g * 512:(g + 1) * 512], ALU.mult,
               r=[("ps", yb), "gtab"], w=[rk])
        stt("dve", r, xt, ALPHA, r, ALU.mult, ALU.add, r=[xkey, rk], w=[rk])
        st = c.st[s].rearrange("p (a b) -> p a b", a=2)
        for g in range(2):
            P.add("dve", lambda e, g=g: e.bn_stats(out=st[:, g, :], in_=r[:, g * 512:(g + 1) * 512]), reads=[rk], writes=[f"lnst{s}"])
        mv = c.mv[s]
        P.add("dve", lambda e: e.bn_aggr(out=mv[:, 0:2], in_=st), reads=[f"lnst{s}"], writes=[f"lnmv{s}"])
        act(mv[:, 2:3], mv[:, 1:2], AF.Ln, r=[f"lnmv{s}"], w=[f"lnmv{s}"], bias=epsc)
        act(mv[:, 2:3], mv[:, 2:3], AF.Exp, r=[f"lnmv{s}"], w=[f"lnmv{s}"], scale=-0.5)
        ts("dve", mv[:, 3:4], mv[:, 0:1], mv[:, 2:3], ALU.mult, r=[f"lnmv{s}"], w=[f"lnmv{s}"], s2=-1.0, op1=ALU.mult)
        act(r, r, AF.Identity, r=[rk, f"lnmv{s}"], w=[rk], scale=mv[:, 2:3], bias=mv[:, 3:4])
        tt("dve", r, r, c.lng, ALU.mult, r=[rk, "lng"], w=[rk])
        tt("pool", r, r, c.lnb, ALU.add, r=[rk, "lnb"], w=[rk])
        dma("pool", dst_d, r, r=[rk], w=[dst_key])

    cast_rr = [0]

    def cast(out, in_, r, w, scale=None):
        i = cast_rr[0] % 2
        cast_rr[0] += 1
        if i == 0:
            if scale is None:
                act(out, in_, AF.Copy, r, w)
            else:
                act(out, in_, AF.Identity, r, w, scale=scale)
        else:
            if scale is None:
                cp("dve", out, in_, r, w)
            else:
                ts("dve", out, in_, scale, ALU.mult, r, w)

    def load_w(dst3, src2, nchunk, key, stage, col0=0, ncol=None, kstep=1):
        ncol_ = ncol if ncol is not None else src2.shape[1]
        srcv = src2.rearrange("(c p) n -> p c n", p=128)
        for k0 in range(0, nchunk, kstep):
            s_ = (k0 // kstep) % len(stage)
            st = stage[s_][:, 0:kstep * ncol_].rearrange("p (c n) -> p c n", c=kstep)
            dma("sp" if s_ % 2 == 0 else "act", st, srcv[:, k0:k0 + kstep, col0:col0 + ncol_], w=[f"stage{s_}"])
            cast(dst3[:, k0:k0 + kstep, :], st, r=[f"stage{s_}"], w=[key])

    def bias_expansion_thunks(j):
        th = []
        for ty in range(5):
            off = (4, 0, 2, 6, 8)[ty]
            vv = ((0, 0), (1, 2), (3, 4), (0, 5), (6, 7))[ty]
            for h in range(32):
                for jq in range(2):
                    base = (((j * 8 + vv[jq]) * 32 + h) * 20 + (10 - jq - off)) * 127 + 63
                    src = bass.AP(pad_d.tensor, base, [[2 * 127, 5], [127, 2], [-1, 64], [1, 64]])
                    dst = BIAS_d[ty, h].rearrange("j (i k) (a q) -> j i k a q", i=2, a=2)[:, :, :, jq, :]
                    th.append((dst, src, ("BIAS", ty, h)))
        return th

    def phase_mlp(l, src_d, src_name, dst_d, dst_name, last, extra=()):
        A.off = persist_off
        w1 = A.bf16(8 * 4096).rearrange("p (c n) -> p c n", c=8)
        w2 = A.bf16(32 * 1024).rearrange("p (c n) -> p c n", c=32)
        o1 = A.off
        stage = [A.f32(4096) for _ in range(NSTAGE)]
        load_w(w1, mlp_w1_d[l], 8, "w1", stage)
        load_w(w2, mlp_w2_d[l], 32, "w2", stage, kstep=4)
        P.barrier()
        A.off = o1
        ln = alloc_ln()
        xt = [A.f32(D) for _ in range(4)]
        hT = [A.bf16(8 * 256).rearrange("p (c t) -> p c t", c=8) for _ in range(2)]
        uT = A.bf16(32 * 256)
        rl = [A.f32(512) for _ in range(2)]
        sts2 = ([] if last else [0]) + [2 + 2 * k for k in range(16)]
        cur_ver = None
        extra = list(extra)
        per = (len(extra) + len(sts2) - 1) // len(sts2) if extra else 0

        def prep(si):
            t0 = sts2[si]
            for tl in range(2):
                t = t0 + tl
                xs_ = (2 * si + tl) % 4
                dma("sp", xt[xs_], src_d[t * 128:(t + 1) * 128, :], r=[(src_name, t)], w=[f"xt{xs_}"])
                make_hT(xt[xs_], f"xt{xs_}", hT[si % 2][:, :, tl * 128:(tl + 1) * 128], f"hT{si % 2}", l,
                        1 if t < 2 else 0, 2, 3, 6)
            for (dst_, src_, key_) in extra[si * per:(si + 1) * per]:
                dma("sp", dst_, src_, w=[key_])

        prep(0)
        for si, t0 in enumerate(sts2):
            ver = 1 if t0 < 2 else 0
            if ver != cur_ver:
                load_ln_tabs(ln, l, 1, ver)
                cur_ver = ver
            s = si % 2
            for grp in range(16):
                pb = grp % 2
                for cc in range(2):
                    j = grp * 2 + cc
                    for kc in range(8):
                        mm(bank(pb)[:, cc * 256:(cc + 1) * 256], lhsT=w1[:, kc, j * 128:(j + 1) * 128], rhs=hT[s][:, kc, :],
                           start=kc == 0, stop=kc == 7, r=["w1", f"hT{s}"], w=[("ps", pb)])
                act(rl[pb], bank(pb), AF.Relu, r=[("ps", pb)], w=[f"rl{pb}"])
                tt("dve", uT[:, grp * 512:(grp + 1) * 512], rl[pb], rl[pb], ALU.mult, r=[f"rl{pb}"], w=["uT"])
            if si + 1 < len(sts2):
                prep(si + 1)
            for tl in range(2):
                t = t0 + tl
                xs_ = (2 * si + tl) % 4
                yb = 2 + 2 * tl
                for g in range(2):
                    for j in range(32):
                        mm(bank(yb + g), lhsT=uT[:, j * 256 + tl * 128:j * 256 + (tl + 1) * 128], rhs=w2[:, j, g * 512:(g + 1) * 512],
                           start=j == 0, stop=j == 31, r=["uT", "w2"], w=[("ps", yb + g)])
                if last:
                    res_ln(ln, yb, yb + 1, xt[xs_], f"xt{xs_}", y_d[(t - 2) * 128:(t - 1) * 128, :], ("y", t))
                else:
                    res_ln(ln, yb, yb + 1, xt[xs_], f"xt{xs_}", dst_d[t * 128:(t + 1) * 128, :], (dst_name, t))
        P.barrier()

    def phase_ret(l, j, src_d, src_name):
        A.off = persist_off
        dp = A.f32(8)
        lg = A.f32(8)
        load_row_bc(dp, ret_decay_d[j:j + 1, :], w=["dp"])
        act(lg, dp, AF.Exp, r=["dp"], w=["lg"])
        ts("dve", lg, lg, -1.0, ALU.mult, r=["lg"], w=["lg"])
        Di = A.f32(128, I32)
        Dq = A.f32(128)
        P.add("pool", lambda e: e.iota(Di, pattern=[[1, 128]], base=0, channel_multiplier=-1), writes=["Di"])
        cp("dve", Dq, Di, r=["Di"], w=["Dq"])
        pos_i = A.f32(128, I32)
        pos = A.f32(128)
        P.add("pool", lambda e: e.iota(pos_i, pattern=[[1, 128]], base=0, channel_multiplier=0), writes=["posi"])
        cp("dve", pos, pos_i, r=["posi"], w=["pos"])
        par_i = A.f32(1, I32)
        par = A.f32(1)
        P.add("pool", lambda e: e.iota(par_i, pattern=[[1, 1]], base=0, channel_multiplier=1), writes=["pari"])
        cp("dve", par, par_i, r=["pari"], w=["par"])
        tmpA = A.f32(128)
        tmpB = A.f32(128)
        maskT = A.f32(2 * 4 * 128).rearrange("p (d h q) -> p d h q", d=2, h=4)
        qdec = A.f32(2 * 4 * 128).rearrange("p (d h q) -> p d h q", d=2, h=4)
        kdec = A.f32(8).rearrange("p (d h) -> p d h", d=2)
        cdec = A.f32(8).rearrange("p (d h) -> p d h", d=2)
        for d in range(2):
            if d == 0:
                ts("dve", tmpA, Dq, 0.0, ALU.max, r=["Dq"], w=["tmpA"])
                ts("dve", tmpB, Dq, 0.0, ALU.is_ge, r=["Dq"], w=["tmpB"])
            else:
                ts("dve", tmpA, Dq, -1.0, ALU.mult, r=["Dq"], w=["tmpA"], s2=0.0, op1=ALU.max)
                ts("dve", tmpB, Dq, 0.0, ALU.is_le, r=["Dq"], w=["tmpB"])
            for h in range(4):
                sc = lg[:, d * 4 + h:d * 4 + h + 1]
                act(maskT[:, d, h, :], tmpA, AF.Exp, r=["tmpA", "lg"], w=["maskT"], scale=sc)
                tt("dve", maskT[:, d, h, :], maskT[:, d, h, :], tmpB, ALU.mult, r=["maskT", "tmpB"], w=["maskT"])
            tq = A.f32(128)
            tk = A.f32(1)
            if d == 0:
                ts("dve", tq, pos, 1.0, ALU.add, r=["pos"], w=[f"tq{d}"])
                ts("dve", tk, par, -1.0, ALU.mult, r=["par"], w=[f"tk{d}"], s2=127.0, op1=ALU.add)
            else:
                ts("dve", tq, pos, -1.0, ALU.mult, r=["pos"], w=[f"tq{d}"], s2=128.0, op1=ALU.add)
                cp("dve", tk, par, r=["par"], w=[f"tk{d}"])
            for h in range(4):
                sc = lg[:, d * 4 + h:d * 4 + h + 1]
                act(qdec[:, d, h, :], tq, AF.Exp, r=[f"tq{d}", "lg"], w=["qdec"], scale=sc)
                act(kdec[:, d, h:h + 1], tk, AF.Exp, r=[f"tk{d}", "lg"], w=["kdec"], scale=sc)
                act(cdec[:, d, h:h + 1], sc, AF.Exp, r=["lg"], w=["cdec"], scale=128.0)
        ret_off = A.off
        ck('consts')

        wqk = A.bf16(8 * 4096).rearrange("p (c n) -> p c n", c=8)
        o1 = A.off
        stage = [A.f32(2048) for _ in range(NSTAGE)]
        wsrcv = ret_w_in_d[j].rearrange("(c p) n -> p c n", p=128)
        for kc in range(8):
            s_ = kc % NSTAGE
            st = stage[s_]
            dma("sp" if s_ % 2 == 0 else "act", st, wsrcv[:, kc, 0:2048], w=[f"stage{s_}"])
            cast(wqk[:, kc, 0:1024], st[:, 0:1024], r=[f"stage{s_}"], w=["wqk"])
            cast(wqk[:, kc, 1024:2048], st[:, 1024:2048], r=[f"stage{s_}"], w=["wqk"], scale=0.0625)
            stv = st.rearrange("p (ch a b) -> p ch a b", a=2, b=64)
            dv = wqk[:, kc, 2048:4096].rearrange("p (ch a b) -> p ch a b", a=2, b=64)
            for a in range(2):
                cast(dv[:, 0:8, a, :], stv[:, 0:8, 1 - a, :], r=[f"stage{s_}"], w=["wqk"])
                cast(dv[:, 8:16, a, :], stv[:, 8:16, 1 - a, :], r=[f"stage{s_}"], w=["wqk"], scale=0.0625)
        P.barrier()
        A.off = o1
        xt = [A.f32(D) for _ in range(2)]
        ropet = [A.f32(4 * 512).rearrange("p (a t) -> p a t", a=4) for _ in range(2)]
        hT = [A.bf16(8 * 512).rearrange("p (c t) -> p c t", c=8) for _ in range(2)]
        qTo = [A.bf16(8 * 512).rearrange("p (c t) -> p c t", c=8) for _ in range(2)]
        kTo = [A.bf16(8 * 512).rearrange("p (c t) -> p c t", c=8) for _ in range(2)]
        t1 = [A.f32(512) for _ in range(2)]
        t2 = [A.f32(512) for _ in range(2)]
        kd = [[A.bf16(1024) for _ in range(2)] for _ in range(2)]
        sts = [(0, 2)] + [(2 + 4 * k, 4) for k in range(8)]
        xi = 0
        ci = 0
        ki = 0
        xi_ = [0]

        def prep_tile(si, tl):
            t0_, ntl_ = sts[si]
            s_ = si % 2
            lat_ = t0_ >= 2
            if tl == 0 and lat_:
                dma("sp", ropet[s_], rope_d[:, :, (t0_ - 2) * 128:(t0_ - 2) * 128 + 512], w=[f"rope{s_}"])
            t = t0_ + tl
            xs_ = xi_[0] % 2
            xi_[0] += 1
            dma("sp", xt[xs_], src_d[t * 128:(t + 1) * 128, :], r=[(src_name, t)], w=[f"xt{xs_}"])
            make_hT(xt[xs_], f"xt{xs_}", hT[s_][:, :, tl * 128:(tl + 1) * 128], f"hT{s_}", l, 0 if lat_ else 1, 0, 1, 6)

        for tl in range(sts[0][1]):
            prep_tile(0, tl)
        for si, (t0, ntl) in enumerate(sts):
            s = si % 2
            lat = t0 >= 2
            ver = 0 if lat else 1
            ntok = ntl * 128
            for c in range(16):
                cs = ci % 2
                ci += 1
                dst = (qTo if c < 8 else kTo)[s]
                dkey = (f"qTo{s}" if c < 8 else f"kTo{s}")
                pa, pb = 2 * cs, 2 * cs + 1
                for kc in range(8):
                    mm(bank(pa)[:, 0:ntok], lhsT=wqk[:, kc, c * 128:(c + 1) * 128], rhs=hT[s][:, kc, 0:ntok],
                       start=kc == 0, stop=kc == 7, r=["wqk", f"hT{s}"], w=[("ps", pa)])
                if lat:
                    for kc in range(8):
                        mm(bank(pb)[:, 0:ntok], lhsT=wqk[:, kc, 2048 + c * 128:2048 + (c + 1) * 128], rhs=hT[s][:, kc, 0:ntok],
                           start=kc == 0, stop=kc == 7, r=["wqk", f"hT{s}"], w=[("ps", pb)])
                    ty = c % 2
                    tt("dve", t1[cs], bank(pa), ropet[s][:, ty, :], ALU.mult, r=[("ps", pa), f"rope{s}"], w=[f"t1{cs}"])
                    tt("dve", t2[cs], bank(pb), ropet[s][:, 2 + ty, :], ALU.mult, r=[("ps", pb), f"rope{s}"], w=[f"t2{cs}"])
                    tt("pool", dst[:, c % 8, :], t1[cs], t2[cs], ALU.add, r=[f"t1{cs}", f"t2{cs}"], w=[dkey])
                else:
                    act(dst[:, c % 8, 0:ntok], bank(pa)[:, 0:ntok], AF.Copy, r=[("ps", pa)], w=[dkey])
                if si + 1 < len(sts) and c % 4 == 3 and (c // 4) < sts[si + 1][1]:
                    prep_tile(si + 1, c // 4)
            for tl in range(ntl):
                t = t0 + tl
                dma("pool", QT_d[t].rearrange("p (c t) -> p c t", c=8), qTo[s][:, :, tl * 128:(tl + 1) * 128], r=[f"qTo{s}"], w=[("QT", t)])
                dma("pool", KT_d[t].rearrange("p (c t) -> p c t", c=8), kTo[s][:, :, tl * 128:(tl + 1) * 128], r=[f"kTo{s}"], w=[("KT", t)])
                ks = ki % 2
                ki += 1
                for c in range(8):
                    tr(bank_bf(4 + ks)[:, c * 128:(c + 1) * 128], kTo[s][:, c, tl * 128:(tl + 1) * 128], ident,
                       r=[f"kTo{s}"], w=[("ps", 4 + ks)])
                for d in range(2):
                    tt("dve", kd[ks][d].rearrange("p (h x) -> p h x", h=4), bank_bf(4 + ks).rearrange("p (h x) -> p h x", h=4),
                       kdec[:, d, :].unsqueeze(2).to_broadcast([128, 4, 256]), ALU.mult,
                       r=[("ps", 4 + ks), "kdec"], w=[f"kd{ks}{d}"])
                    dma("pool", KD_d[d, t], kd[ks][d], r=[f"kd{ks}{d}"], w=[("KD", d, t)])
        P.barrier()
        ck('A1')

        A.off = ret_off
        wvg = A.bf16(8 * 4096).rearrange("p (c n) -> p c n", c=8)
        o1 = A.off
        stage = [A.f32(4096) for _ in range(NSTAGE)]
        load_w(wvg, ret_w_in_d[j], 8, "wvg", stage, col0=2048, ncol=4096)
        P.barrier()
        A.off = o1
        xt = [A.f32(D) for _ in range(2)]
        hT = [A.bf16(1024).rearrange("p (c t) -> p c t", c=8) for _ in range(2)]
        vo = [A.bf16(2048) for _ in range(2)]
        sgo = [A.bf16(2048) for _ in range(2)]
        sgt = [A.f32(512) for _ in range(2)]
        gi = 0

        def prep2(t):
            dma("sp", xt[t % 2], src_d[t * 128:(t + 1) * 128, :], r=[(src_name, t)], w=[f"xt{t % 2}"])
            make_hT(xt[t % 2], f"xt{t % 2}", hT[t % 2], f"hT{t % 2}", l, 1 if t < 2 else 0, 0, 1, 6)

        prep2(0)
        for t in range(NT):
            s = t % 2
            for g in range(8):
                if g == 4 and t + 1 < NT:
                    prep2(t + 1)
                b = gi % 4
                gi += 1
                for kc in range(8):
                    mm(bank(b), lhsT=hT[s][:, kc, :], rhs=wvg[:, kc, g * 512:(g + 1) * 512], start=kc == 0, stop=kc == 7,
                       r=[f"hT{s}", "wvg"], w=[("ps", b)])
                if g < 4:
                    act(vo[s][:, g * 512:(g + 1) * 512], bank(b), AF.Copy, r=[("ps", b)], w=[f"vo{s}"])
                else:
                    sb_ = b % 2
                    act(sgt[sb_], bank(b), AF.Sigmoid, r=[("ps", b)], w=[f"sgt{sb_}"])
                    tt("dve", sgo[s][:, (g - 4) * 512:(g - 3) * 512], bank(b), sgt[sb_], ALU.mult,
                       r=[("ps", b), f"sgt{sb_}"], w=[f"sgo{s}"])
            dma("pool", V_d[t], vo[s], r=[f"vo{s}"], w=[("V", t)])
            dma("pool", SG_d[t], sgo[s], r=[f"sgo{s}"], w=[("SG", t)])
        P.barrier()
        ck('A2')

        A.off = ret_off
        stf = A.f32(8 * 512).rearrange("p (c n) -> p c n", c=8)
        stb = A.bf16(8 * 512).rearrange("p (c n) -> p c n", c=8)
        qTt = [A.bf16(1024) for _ in range(2)]
        kTt = [A.bf16(1024) for _ in range(2)]
        kdt = [A.bf16(1024) for _ in range(2)]
        vt = [A.bf16(2048) for _ in range(2)]
        qd = [A.bf16(1024) for _ in range(2)]
        obt = [A.f32(2048) for _ in range(2)]
        pT = [A.bf16(128) for _ in range(2)]
        for d in (1, 0):
            order = [1, 0] + list(range(NT - 1, 1, -1)) if d == 1 else list(range(NT))
            memset("dve", stf, 0.0, w=[f"stf{c}" for c in range(8)])
            memset("pool", stb, 0.0, w=[f"stb{c}" for c in range(8)])
            for n, t in enumerate(order):
                s = n % 2
                dma("sp", qTt[s], QT_d[t], r=[("QT", t)], w=[f"qTt{s}"])
                dma("sp", kTt[s], KT_d[t], r=[("KT", t)], w=[f"kTt{s}"])
                dma("sp", kdt[s], KD_d[d, t], r=[("KD", d, t)], w=[f"kdt{s}"])
                dma("sp", vt[s], V_d[t], r=[("V", t)], w=[f"vt{s}"])
                if d == 0:
                    dma("sp", obt[s], OB_d[t], r=[("OB", t)], w=[f"obt{s}"])
                tt("dve", qd[s].rearrange("p (h c q) -> p h c q", h=4, c=2), qTt[s].rearrange("p (h c q) -> p h c q", h=4, c=2),
                   qdec[:, d, :, :].unsqueeze(2).to_broadcast([128, 4, 2, 128]), ALU.mult, r=[f"qTt{s}", "qdec"], w=[f"qd{s}"])
                for h in range(4):
                    hs = h % 2
                    bS, bO, bU = hs, 2 + hs, 4 + 2 * hs
                    for dc in range(2):
                        c = 2 * h + dc
                        mm(bank(bS)[:, 0:128], lhsT=kTt[s][:, c * 128:(c + 1) * 128], rhs=qTt[s][:, c * 128:(c + 1) * 128],
                           start=dc == 0, stop=dc == 1, r=[f"kTt{s}", f"qTt{s}"], w=[("ps", bS)])
                    tt("dve", pT[hs], bank(bS)[:, 0:128], maskT[:, d, h, :], ALU.mult, r=[("ps", bS), "maskT"], w=[f"pT{hs}"])
                    for dc in range(2):
                        c = 2 * h + dc
                        mm(bank(bU + dc), lhsT=kdt[s][:, c * 128:(c + 1) * 128], rhs=vt[s][:, h * 512:(h + 1) * 512],
                           start=True, stop=True, r=[f"kdt{s}", f"vt{s}"], w=[("ps", bU + dc)])
                    mm(bank(bO), lhsT=pT[hs], rhs=vt[s][:, h * 512:(h + 1) * 512], start=True, stop=False,
                       r=[f"pT{hs}", f"vt{s}"], w=[("ps", bO)])
                    for dc in range(2):
                        c = 2 * h + dc
                        mm(bank(bO), lhsT=qd[s][:, c * 128:(c + 1) * 128], rhs=stb[:, c, :], start=False, stop=dc == 1,
                           r=[f"qd{s}", f"stb{c}"], w=[("ps", bO)])
                    if d == 1:
                        act(obt[s][:, h * 512:(h + 1) * 512], bank(bO), AF.Copy, r=[("ps", bO)], w=[f"obt{s}"])
                    else:
                        tt("dve", obt[s][:, h * 512:(h + 1) * 512], bank(bO), obt[s][:, h * 512:(h + 1) * 512], ALU.add,
                           r=[("ps", bO), f"obt{s}"], w=[f"obt{s}"])
                    for dc in range(2):
                        c = 2 * h + dc
                        stt("dve", stf[:, c, :], stf[:, c, :], cdec[:, d, h:h + 1], bank(bU + dc), ALU.mult, ALU.add,
                            r=[f"stf{c}", ("ps", bU + dc), "cdec"], w=[f"stf{c}"])
                        act(stb[:, c, :], stf[:, c, :], AF.Copy, r=[f"stf{c}"], w=[f"stb{c}"])
                dma("pool", OB_d[t], obt[s], r=[f"obt{s}"], w=[("OB", t)])
        P.barrier()

        A.off = ret_off
        wo = A.bf16(16 * 1024).rearrange("p (c n) -> p c n", c=16)
        o1 = A.off
        stage = [A.f32(4096) for _ in range(NSTAGE)]
        load_w(wo, ret_w_o_d[j], 16, "wo", stage, kstep=4)
        P.barrier()
        A.off = o1
        ln = alloc_ln()
        ost = [A.f32(2048) for _ in range(2)]
        sgt_ = [A.bf16(2048) for _ in range(2)]
        xt = [A.f32(D) for _ in range(2)]
        onf = [A.f32(512) for _ in range(2)]
        onb = [A.bf16(2048) for _ in range(2)]
        onT = [A.bf16(2048) for _ in range(2)]
        gst = [A.f32(4 * 6).rearrange("p (h x) -> p h x", h=4) for _ in range(2)]
        gmv = [A.f32(4 * 2).rearrange("p (h x) -> p h x", h=4) for _ in range(2)]
        grs = [A.f32(8) for _ in range(2)]
        xt3 = xt + [A.f32(D)]

        def d_stage1(t):
            s = t % 2
            dma("sp", ost[s], OB_d[t], r=[("OB", t)], w=[f"ost{s}"])
            dma("sp", sgt_[s], SG_d[t], r=[("SG", t)], w=[f"sgt_{s}"])
            dma("sp", xt3[t % 3], src_d[t * 128:(t + 1) * 128, :], r=[(src_name, t)], w=[f"xt{t % 3}"])
            for h in range(4):
                P.add("dve", lambda e, h=h, s=s: e.bn_stats(out=gst[s][:, h, :], in_=ost[s][:, h * 512:(h + 1) * 512]),
                      reads=[f"ost{s}"], writes=[f"gst{s}"])
                P.add("dve", lambda e, h=h, s=s: e.bn_aggr(out=gmv[s][:, h, :], in_=gst[s][:, h:h + 1, :]),
                      reads=[f"gst{s}"], writes=[f"gmv{s}"])
            act(grs[s][:, 0:4], gmv[s][:, :, 1], AF.Ln, r=[f"gmv{s}"], w=[f"grs{s}"], bias=epsc)
            act(grs[s][:, 0:4], grs[s][:, 0:4], AF.Exp, r=[f"grs{s}"], w=[f"grs{s}"], scale=-0.5)
            stt("dve", grs[s][:, 4:8], gmv[s][:, :, 0], -1.0, grs[s][:, 0:4], ALU.mult, ALU.mult,
                r=[f"gmv{s}", f"grs{s}"], w=[f"grs{s}"])
            for h in range(4):
                hs = h % 2
                act(onf[hs], ost[s][:, h * 512:(h + 1) * 512], AF.Identity, r=[f"ost{s}", f"grs{s}"], w=[f"onf{hs}"],
                    scale=grs[s][:, h:h + 1], bias=grs[s][:, 4 + h:5 + h])
                tt("pool", onb[s][:, h * 512:(h + 1) * 512], onf[hs], sgt_[s][:, h * 512:(h + 1) * 512], ALU.mult,
                   r=[f"onf{hs}", f"sgt_{s}"], w=[f"onb{s}"])

        def d_stage2(t):
            s = t % 2
            tb = 2 * s
            for c in range(16):
                tr(bank_bf(tb, 2)[:, c * 128:(c + 1) * 128], onb[s][:, c * 128:(c + 1) * 128], ident, r=[f"onb{s}"],
                   w=[("ps", tb), ("ps", tb + 1)])
            act(onT[s], bank_bf(tb, 2), AF.Copy, r=[("ps", tb), ("ps", tb + 1)], w=[f"onT{s}"])
            yb = 4 + 2 * s
            for g in range(2):
                for c in range(16):
                    mm(bank(yb + g), lhsT=onT[s][:, c * 128:(c + 1) * 128], rhs=wo[:, c, g * 512:(g + 1) * 512],
                       start=c == 0, stop=c == 15, r=[f"onT{s}", "wo"], w=[("ps", yb + g)])

        cur_ver_ = [None]

        def d_stage3(t):
            s = t % 2
            ver = 1 if t < 2 else 0
            if ver != cur_ver_[0]:
                load_ln_tabs(ln, l, 0, ver)
                cur_ver_[0] = ver
            yb = 4 + 2 * s
            res_ln(ln, yb, yb + 1, xt3[t % 3], f"xt{t % 3}", XM_d[t * 128:(t + 1) * 128, :], ("XM", t))

        for i in range(NT + 2):
            if i < NT:
                d_stage1(i)
            if 0 <= i - 1 < NT:
                d_stage2(i - 1)
            if 0 <= i - 2 < NT:
                d_stage3(i - 2)
        P.barrier()

    def phase_nat(l, j, src_d, src_name, last):
        A.off = persist_off
        wn = A.bf16(8 * 3072).rearrange("p (c n) -> p c n", c=8)
        o1 = A.off
        stage = [A.f32(3072) for _ in range(NSTAGE)]
        load_w(wn, nat_w_in_d[j], 8, "wn", stage)
        P.barrier()
        A.off = o1
        xt = [A.f32(D) for _ in range(2)]
        hT = [A.bf16(8 * 512).rearrange("p (c t) -> p c t", c=8) for _ in range(2)]
        qTo = [A.bf16(8 * 512).rearrange("p (c t) -> p c t", c=8) for _ in range(2)]
        kTo = [A.bf16(8 * 512).rearrange("p (c t) -> p c t", c=8) for _ in range(2)]
        vno = [A.bf16(1056) for _ in range(2)]
        for s in range(2):
            memset("pool", vno[s], 1.0, w=[f"vno{s}"])
        sts = [(0, 2)] + [(2 + 4 * k, 4) for k in range(8)]
        ci = 0
        vi = 0
        xi_ = [0]

        def prep_tile(si, tl):
            t0_, ntl_ = sts[si]
            s_ = si % 2
            t = t0_ + tl
            xs_ = xi_[0] % 2
            xi_[0] += 1
            dma("sp", xt[xs_], src_d[t * 128:(t + 1) * 128, :], r=[(src_name, t)], w=[f"xt{xs_}"])
            make_hT(xt[xs_], f"xt{xs_}", hT[s_][:, :, tl * 128:(tl + 1) * 128], f"hT{s_}", l, 0 if t0_ >= 2 else 1, 0, 1, 6)

        for tl in range(sts[0][1]):
            prep_tile(0, tl)
        for si, (t0, ntl) in enumerate(sts):
            s = si % 2
            ver = 0 if t0 >= 2 else 1
            ntok = ntl * 128
            for c in range(16):
                b = ci % 4
                ci += 1
                for kc in range(8):
                    mm(bank(b)[:, 0:ntok], lhsT=wn[:, kc, c * 128:(c + 1) * 128], rhs=hT[s][:, kc, 0:ntok],
                       start=kc == 0, stop=kc == 7, r=["wn", f"hT{s}"], w=[("ps", b)])
                if c < 8:
                    act(qTo[s][:, c, 0:ntok], bank(b)[:, 0:ntok], AF.Identity, r=[("ps", b)], w=[f"qTo{s}"], scale=NAT_SCALE)
                else:
                    cp("dve", kTo[s][:, c - 8, 0:ntok], bank(b)[:, 0:ntok], r=[("ps", b)], w=[f"kTo{s}"])
                if si + 1 < len(sts) and c % 4 == 3 and (c // 4) < sts[si + 1][1]:
                    prep_tile(si + 1, c // 4)
            for tl in range(ntl):
                t = t0 + tl
                dma("pool", QT_d[t].rearrange("p (c t) -> p c t", c=8), qTo[s][:, :, tl * 128:(tl + 1) * 128], r=[f"qTo{s}"], w=[("QT", t)])
                dma("pool", KT_d[t].rearrange("p (c t) -> p c t", c=8), kTo[s][:, :, tl * 128:(tl + 1) * 128], r=[f"kTo{s}"], w=[("KT", t)])
                vs = vi % 2
                vi += 1
                for g in range(2):
                    b = 4 + g
                    for kc in range(8):
                        mm(bank(b), lhsT=hT[s][:, kc, tl * 128:(tl + 1) * 128], rhs=wn[:, kc, 2048 + g * 512:2048 + (g + 1) * 512],
                           start=kc == 0, stop=kc == 7, r=[f"hT{s}", "wn"], w=[("ps", b)])
                    act(vno[vs].rearrange("p (h x) -> p h x", x=33)[:, g * 16:(g + 1) * 16, 0:32],
                        bank(b).rearrange("p (h x) -> p h x", x=32), AF.Copy, r=[("ps", b)], w=[f"vno{vs}"])
                dma("pool", VN_d[t], vno[vs], r=[f"vno{vs}"], w=[("VN", t)])
        P.barrier()
        ck('NA')

        A.off = persist_off
        won = A.bf16(8 * 1024).rearrange("p (c n) -> p c n", c=8)
        o1 = A.off
        stage = [A.f32(4096), A.f32(4096)]
        load_w(won, nat_w_o_d[j], 8, "won", stage, kstep=4)
        P.barrier()
        A.off = o1
        cmask = A.f32(128)
        dma("sp", cmask, colmask_d[:, :], w=["cmask"])
        bias_int = A.bf16(32 * 5 * 128).rearrange("p (h j q) -> p h j q", h=32, j=5)
        bstage = [A.f32(5 * 128).rearrange("p (j q) -> p j q", j=5) for _ in range(2)]
        bias_b = [A.bf16(5 * 128).rearrange("p (j q) -> p j q", j=5) for _ in range(3)]
        bcount = [0]

        def expand_bias(ty, h, dst, dkey):
            s = bcount[0] % 2
            bcount[0] += 1
            dma("sp", bstage[s], BIAS_d[ty, h].rearrange("j p q -> p j q"), r=[("BIAS", ty, h)], w=[f"bstage{s}"])
            tt("dve", bstage[s], bstage[s], cmask.unsqueeze(1).to_broadcast([128, 5, 128]), ALU.add,
               r=[f"bstage{s}", "cmask"], w=[f"bstage{s}"])
            act(dst, bstage[s], AF.Exp, r=[f"bstage{s}"], w=[dkey])

        for h in range(32):
            expand_bias(0, h, bias_int[:, h, :, :], "bias_int")
        kring = [A.bf16(1024) for _ in range(8)]
        vring = [A.bf16(1056) for _ in range(8)]
        kctx = [A.bf16(1024) for _ in range(2)]
        vctx = [A.bf16(1056) for _ in range(2)]
        for c in range(2):
            dma("sp", kctx[c], KT_d[c], r=[("KT", c)], w=[f"kctx{c}"])
            dma("sp", vctx[c], VN_d[c], r=[("VN", c)], w=[f"vctx{c}"])
        qTt = [A.bf16(1024) for _ in range(2)]
        qm = [A.bf16(8 * 4 * 128).rearrange("p (c h q) -> p c h q", c=8, h=4) for _ in range(2)]
        hmask_f = A.f32(4)
        hmask = A.bf16(4)
        memset("pool", hmask_f, 1.0, w=["hmask_f"])
        P.add("pool", lambda e: e.affine_select(out=hmask_f, in_=hmask_f, pattern=[[-32, 4]], compare_op=ALU.is_ge,
                                                fill=0.0, base=0, channel_multiplier=1), reads=["hmask_f"], writes=["hmask_f"])
        P.add("pool", lambda e: e.affine_select(out=hmask_f, in_=hmask_f, pattern=[[32, 4]], compare_op=ALU.is_ge,
                                                fill=0.0, base=31, channel_multiplier=-1), reads=["hmask_f"], writes=["hmask_f"])
        cp("dve", hmask, hmask_f, r=["hmask_f"], w=["hmask"])
        xt = [A.f32(D) for _ in range(2)]
        pT = [A.bf16(896) for _ in range(3)]
        rden = A.f32(32)
        ob = A.bf16(1024)
        oT = A.bf16(1024)
        ln = alloc_ln()
        loaded = set()
        qtiles = ([] if last else [0, 1]) + list(range(2, NT))
        cur_ver_ = [None]
        PVREG = ((6, 0, 0, 15), (7, 0, 15, 15), (1, 384, 30, 2))

        def pv_dst(h):
            if h < 15:
                return bank(6)[:, 33 * h:33 * h + 33], [("ps", 6)]
            if h < 30:
                return bank(7)[:, 33 * (h - 15):33 * (h - 15) + 33], [("ps", 7)]
            o_ = 384 + 33 * (h - 30)
            return bank(1)[:, o_:o_ + 33], ["pvtail", ("ps", 1)]

        def emit_tail(n, g0):
            t = qtiles[n]
            s = n % 2
            ver = 1 if t < 2 else 0
            if ver != cur_ver_[0]:
                load_ln_tabs(ln, l, 0, ver)
                cur_ver_[0] = ver
            for (pvb, co, h0, nh) in PVREG:
                rk = ["pvtail"] if pvb == 1 else [("ps", pvb)]
                view = bank(pvb)[:, co:co + nh * 33].rearrange("p (h x) -> p h x", x=33)
                P.add("dve", lambda e, view=view, h0=h0, nh=nh: e.reciprocal(out=rden[:, h0:h0 + nh].unsqueeze(2), in_=view[:, :, 32:33]),
                      reads=rk, writes=["rden"])
                tt("dve", ob[:, h0 * 32:(h0 + nh) * 32].rearrange("p (h x) -> p h x", x=32), view[:, :, 0:32],
                   rden[:, h0:h0 + nh].unsqueeze(2).to_broadcast([128, nh, 32]), ALU.mult, r=rk + ["rden"], w=["ob"])
            for c in range(8):
                tr(bank_bf(6)[:, c * 128:(c + 1) * 128], ob[:, c * 128:(c + 1) * 128], ident, r=["ob"], w=[("ps", 6)])
            act(oT, bank_bf(6), AF.Copy, r=[("ps", 6)], w=["oT"])
            for g in range(2):
                for c in range(8):
                    mm(bank(6 + g), lhsT=oT[:, c * 128:(c + 1) * 128], rhs=won[:, c, g * 512:(g + 1) * 512], start=c == 0, stop=c == 7,
                       r=["oT", "won"], w=[("ps", 6 + g)])
            res_ln(ln, 6, 7, xt[s], f"xt{s}", XM_d[t * 128:(t + 1) * 128, :], ("XM", t))

        gcount = 0

        def tile_params(t):
            if t < 2:
                return 0, 0, 0
            T = t - 2
            ty_ = 1 if T == 0 else 2 if T == 1 else 3 if T == 30 else 4 if T == 31 else 0
            return 5, ty_, min(max(T - 2, 0), 27)

        def prefetch(n):
            t = qtiles[n]
            s_ = n % 2
            dma("sp", qTt[s_], QT_d[t], r=[("QT", t)], w=[f"qTt{s_}"])
            dma("sp", xt[s_], src_d[t * 128:(t + 1) * 128, :], r=[(src_name, t)], w=[f"xt{s_}"])
            tt("dve", qm[s_], qTt[s_].rearrange("p (c q) -> p c q", c=8).unsqueeze(2).to_broadcast([128, 8, 4, 128]),
               hmask.unsqueeze(1).unsqueeze(3).to_broadcast([128, 8, 4, 128]), ALU.mult, r=[f"qTt{s_}", "hmask"], w=[f"qm{s_}"])
            nloc_, _, wt_ = tile_params(t)
            if nloc_:
                for kt in range(wt_, wt_ + 5):
                    if kt not in loaded:
                        loaded.add(kt)
                        dma("sp", kring[kt % 8], KT_d[kt + 2], r=[("KT", kt + 2)], w=[f"kring{kt % 8}"])
                        dma("sp", vring[kt % 8], VN_d[kt + 2], r=[("VN", kt + 2)], w=[f"vring{kt % 8}"])

        prefetch(0)
        for n, t in enumerate(qtiles):
            s = n % 2
            is_ctx = t < 2
            nloc, ty, wt = tile_params(t)
            nch = nloc + 2
            g0 = gcount
            gcount += 32

            def emit_S(h):
                hp = (g0 + h) % 3
                hc, pb = h // 4, 32 * (h % 4)
                sb0 = 2 * hp
                psS = bank(sb0, 2)
                skeys = [("ps", sb0), ("ps", sb0 + 1)]
                if nloc and ty != 0:
                    expand_bias(ty, h, bias_b[hp], f"bias_b{hp}")
                for jj in range(nch):
                    if jj < nloc:
                        kt = wt + jj
                        ksrc, kkey = kring[kt % 8], f"kring{kt % 8}"
                    else:
                        ksrc, kkey = kctx[jj - nloc], f"kctx{jj - nloc}"
                    wk = [skeys[(jj * 128) // 512]]
                    mm(psS[:, jj * 128:(jj + 1) * 128], lhsT=ksrc[:, hc * 128:(hc + 1) * 128],
                       rhs=qm[s][:, hc, h % 4, :], start=True, stop=True, r=[kkey, f"qm{s}"], w=wk)
                act(pT[hp][:, 0:nch * 128], psS[:, 0:nch * 128], AF.Exp, r=skeys, w=[f"pT{hp}"])
                if nloc:
                    if ty == 0:
                        bsrc, bkey = bias_int[:, h, :, :], "bias_int"
                    else:
                        bsrc, bkey = bias_b[hp], f"bias_b{hp}"
                    pv_ = pT[hp][:, 0:640].rearrange("p (j q) -> p j q", j=5)
                    tt("dve", pv_, pv_, bsrc, ALU.mult, r=[f"pT{hp}", bkey], w=[f"pT{hp}"])

            def emit_PV(h):
                hp = (g0 + h) % 3
                dst, dkeys = pv_dst(h)
                for jj in range(nch):
                    if jj < nloc:
                        kt = wt + jj
                        vsrc, vkey = vring[kt % 8], f"vring{kt % 8}"
                    else:
                        vsrc, vkey = vctx[jj - nloc], f"vctx{jj - nloc}"
                    mm(dst, lhsT=pT[hp][:, jj * 128:(jj + 1) * 128], rhs=vsrc[:, h * 33:(h + 1) * 33],
                       start=jj == 0, stop=jj == nch - 1, r=[f"pT{hp}", vkey], w=dkeys)

            emit_S(0)
            emit_S(1)
            emit_S(2)
            if n > 0:
                emit_tail(n - 1, g0)
            for h in range(3, 32):
                emit_PV(h - 3)
                emit_S(h)
                if h == 16 and n + 1 < len(qtiles):
                    prefetch(n + 1)
            emit_PV(29)
            emit_PV(30)
            emit_PV(31)
        emit_tail(len(qtiles) - 1, gcount)
        P.barrier()

    try:
        phase_mods()
        src_d, src_name = xs_d, "xs"
        for l in range(n_layers):
            last = l == DEPTH - 1
            if l % 2 == 0:
                phase_ret(l, l // 2, src_d, src_name)
            else:
                phase_nat(l, l // 2, src_d, src_name, last)
            ck('mixer%d' % l)
            nxt_nat = (l + 1 < n_layers) and ((l + 1) % 2 == 1)
            phase_mlp(l, XM_d, "XM", XN_d, "XN", last, extra=bias_expansion_thunks((l + 1) // 2) if nxt_nat else ())
            src_d, src_name = XN_d, "XN"
    except _Stop:
        P.barrier()
    if dbg:
        A.off = persist_off
        for name, shape in dbg.items():
            src = {"XM": XM_d, "XN": XN_d}[name]
            buf = A.f32(D)
            for t in range(shape[0] // 128):
                dma("sp", buf, src[t * 128:(t + 1) * 128, :], r=[(name, t)], w=["dbgbuf"])
                dma("sp", dbg_outs[name][t * 128:(t + 1) * 128, :], buf, r=["dbgbuf"], w=[("dbg", name, t)])
    nsig = P.emit()
    es.close()
    return nc, {s: len(P.ops[s]) for s in STREAMS}, nsig, A.peak


def _rope_tables():
    f = np.arange(64, dtype=np.float32)
    inv = (1.0 / (np.float32(10000.0) ** (np.arange(0, 128, 2, dtype=np.float32) / np.float32(128)))).astype(np.float32)
    t = np.arange(SEQ)
    row = (t // 64).astype(np.float32)
    col = (t % 64).astype(np.float32)
    tab = np.zeros((128, 4, SEQ), np.float32)
    for ty, posv in enumerate((row, col)):
        ang = (posv[None, :] * inv[:, None]).astype(np.float32)
        c = np.cos(ang).astype(np.float32)
        s = np.sin(ang).astype(np.float32)
        tab[0:64, ty, :] = c
        tab[64:128, ty, :] = c
        tab[0:64, 2 + ty, :] = -s
        tab[64:128, 2 + ty, :] = s
    return tab


def _colmask():
    m = np.full((128, 128), NEG, np.float32)
    for qc in range(64):
        cs = min(max(qc - 8, 0), 48)
        for i in range(2):
            for jq in range(2):
                m[i * 64 + cs:i * 64 + cs + 16, jq * 64 + qc] = 0.0
    return m


_VAR = ((-4, 3), (0, 7), (-1, 6), (-2, 5), (-3, 4), (-5, 2), (-6, 1), (-7, 0))


def _pad_tables(rpb):
    pad = np.full((2, 8, 32, 20, 127), NEG, np.float32)
    rev = rpb[:, :, :, ::-1]
    for v, (lo, hi) in enumerate(_VAR):
        for dr in range(lo, hi + 1):
            pad[:, v, :, dr + 10, 48:79] = rev[:, :, dr + 7, :]
    return pad


_CACHE = {}


def kernel(x, c, ctx, c_ctx, ada_w, ada_b, ret_w_in, ret_w_o, ret_decay, nat_w_in, nat_w_o, nat_rpb,
           mlp_w1, mlp_w2, ln_g, ln_b):
    f = lambda a: np.ascontiguousarray(np.asarray(a, dtype=np.float32))
    x, c, ctx, c_ctx = f(x), f(c), f(ctx), f(c_ctx)
    if "nc" not in _CACHE:
        _CACHE["nc"] = build()[0]
    nc = _CACHE["nc"]
    shared = {
        "ada_w": f(ada_w), "ada_b": f(ada_b),
        "ada_bT": f(np.asarray(ada_b).reshape(DEPTH, 48, 128).transpose(0, 2, 1)),
        "ret_w_in": f(ret_w_in), "ret_w_o": f(ret_w_o), "ret_decay": f(np.asarray(ret_decay).reshape(2, 8)),
        "nat_w_in": f(nat_w_in), "nat_w_o": f(nat_w_o),
        "pad": _pad_tables(f(nat_rpb)), "colmask": _colmask(), "rope": _rope_tables(),
        "mlp_w1": f(mlp_w1), "mlp_w2": f(mlp_w2), "ln_g": f(ln_g), "ln_b": f(ln_b),
    }
    in_maps = []
    for b in range(8):
        m = dict(shared)
        m["xs"] = np.ascontiguousarray(np.concatenate([ctx[b], x[b]], axis=0))
        cv = np.stack([c[b], c_ctx], axis=1)
        m["cvT"] = np.ascontiguousarray(cv.reshape(8, 128, 2).transpose(1, 0, 2))
        in_maps.append(m)
    res = run_bass_kernel_spmd(nc, in_maps, core_ids=list(range(8)))
    return np.stack([np.asarray(r["y"], dtype=np.float32) for r in res.results], axis=0)
```

```python
import contextlib
import numpy as np
import concourse.bass as bass
import concourse.mybir as mybir
from concourse.bass_utils import run_bass_kernel_spmd

F32 = mybir.dt.float32
BF16 = mybir.dt.bfloat16
I32 = mybir.dt.int32
AF = mybir.ActivationFunctionType
ALU = mybir.AluOpType

D = 1024
SEQ = 4096
CTX = 256
TOK = SEQ + CTX
NT = TOK // 128
DEPTH = 4
ALPHA = (2.0 * DEPTH) ** 0.25
EPS = 1e-5
NEG = -30000.0
NAT_SCALE = 32 ** -0.5

EPOCH = 3000
STREAMS = ("pe", "act", "dve", "pool", "sp")


class Op:
    __slots__ = ("stream", "fn", "is_dma", "deps", "signal", "cnt", "dsem", "dval")

    def __init__(self, stream, fn, is_dma):
        self.stream = stream
        self.fn = fn
        self.is_dma = is_dma
        self.deps = []
        self.signal = False
        self.cnt = None
        self.dsem = None
        self.dval = None


class Prog:
    def __init__(self, nc, n_dma_sems=16):
        self.nc = nc
        self.ops = {s: [] for s in STREAMS}
        self.last_w = {}
        self.readers = {}
        self.n_dma_sems = n_dma_sems
        self.dma_rr = {s: 0 for s in STREAMS}
        self.dma_cnt = {}
        self.dma_last = {}
        self.pending = {s: [] for s in STREAMS}

    def _need(self, op, dep):
        if dep is None or dep is op:
            return
        if not dep.is_dma:
            dep.signal = True
        op.deps.append(dep)

    def barrier(self):
        deps = []
        for s in STREAMS:
            for op in reversed(self.ops[s]):
                if not op.is_dma:
                    deps.append(op)
                    break
        deps.extend(self.dma_last.values())
        for s in STREAMS:
            self.pending[s] = list(deps)

    def add(self, stream, fn, reads=(), writes=(), dma=False):
        op = Op(stream, fn, dma)
        if self.pending[stream]:
            for dep in self.pending[stream]:
                self._need(op, dep)
            self.pending[stream] = []
        if dma:
            slot = self.dma_rr[stream]
            self.dma_rr[stream] = (slot + 1) % self.n_dma_sems
            key = (stream, slot)
            prev = self.dma_last.get(key)
            if prev is not None:
                self._need(op, prev)
            op.dsem = key
            op.dval = self.dma_cnt.get(key, 0) + 16
            self.dma_cnt[key] = op.dval
            self.dma_last[key] = op
        for r in reads:
            w = self.last_w.get(r)
            if w is not None:
                if w.stream == stream and stream == "pe" and not w.is_dma and not dma:
                    pass
                else:
                    self._need(op, w)
        for wk in writes:
            w = self.last_w.get(wk)
            if w is not None and not (w.stream == stream and not w.is_dma and not dma):
                self._need(op, w)
            for rd in self.readers.get(wk, ()):
                if rd.stream == stream and not rd.is_dma and not dma:
                    continue
                self._need(op, rd)
        for r in reads:
            lst = self.readers.setdefault(r, [])
            if not dma:
                for i_, o_ in enumerate(lst):
                    if o_.stream == stream and not o_.is_dma:
                        lst[i_] = op
                        break
                else:
                    lst.append(op)
            else:
                lst.append(op)
        for wk in writes:
            self.last_w[wk] = op
            self.readers[wk] = []
        self.ops[stream].append(op)
        return op

    def emit(self):
        nc = self.nc
        nsig = {}
        for s in STREAMS:
            c = 0
            for op in self.ops[s]:
                if op.signal and not op.is_dma:
                    op.cnt = c
                    c += 1
            nsig[s] = c
        with contextlib.ExitStack() as es:
            esem = {}
            for s in STREAMS:
                n = (nsig[s] + EPOCH - 1) // EPOCH
                esem[s] = [es.enter_context(nc.semaphore(f"e_{s}_{i}")) for i in range(max(n, 1))]
            dsem = {}
            for key in self.dma_cnt:
                dsem[key] = es.enter_context(nc.semaphore(f"d_{key[0]}_{key[1]}"))
            block = es.enter_context(nc.Block())

            def run(stream, eng):
                waited = {}
                for op in self.ops[stream]:
                    for dep in op.deps:
                        if dep.is_dma:
                            k = ("d", dep.dsem)
                            v = dep.dval
                            if waited.get(k, 0) >= v:
                                continue
                            waited[k] = v
                            eng.wait_ge(dsem[dep.dsem], v)
                        else:
                            k = ("e", dep.stream)
                            v = dep.cnt
                            if waited.get(k, -1) >= v:
                                continue
                            waited[k] = v
                            eng.wait_ge(esem[dep.stream][v // EPOCH], v % EPOCH + 1)
                    ins = op.fn(eng)
                    if op.is_dma:
                        ins.then_inc(dsem[op.dsem], 16)
                    elif op.signal:
                        ins.then_inc(esem[stream][op.cnt // EPOCH], 1)
                for key, val in self.dma_cnt.items():
                    if key[0] == stream:
                        eng.wait_ge(dsem[key], val)

            @block.tensor
            def _(e):
                run("pe", e)

            @block.scalar
            def _(e):
                run("act", e)

            @block.vector
            def _(e):
                run("dve", e)

            @block.gpsimd
            def _(e):
                run("pool", e)

            @block.sync
            def _(e):
                run("sp", e)
        return nsig


class Arena:
    def __init__(self, t, words):
        self.t = t
        self.words = words
        self.off = 0
        self.peak = 0

    def f32(self, n, dt=F32):
        assert self.off + n <= self.words, ("arena overflow", self.off, n, self.words)
        ap = self.t[:, self.off:self.off + n]
        self.off += (n + 15) // 16 * 16
        self.peak = max(self.peak, self.off)
        return ap if dt == F32 else ap.bitcast(dt)

    def bf16(self, n):
        assert n % 2 == 0
        return self.f32(n // 2).bitcast(BF16)


ARENA_WORDS = 196 * 256
NSTAGE = 4


class _Stop(Exception):
    pass


def build(n_layers=DEPTH, dbg=None, stop=None):
    nc = bass.Bass("TRN2", target_bir_lowering=False)

    def din(name, shape, dt=F32):
        return nc.dram_tensor(name, list(shape), dt, kind="ExternalInput").ap()

    def dscr(name, shape, dt=F32):
        return nc.dram_tensor(name, list(shape), dt).ap()

    xs_d = din("xs", [TOK, D])
    cvT_d = din("cvT", [128, 8, 2])
    ada_w_d = din("ada_w", [DEPTH, D, 6 * D])
    ada_b_d = din("ada_b", [DEPTH, 6 * D])
    ada_bT_d = din("ada_bT", [DEPTH, 128, 48])
    ret_w_in_d = din("ret_w_in", [2, D, 6 * D])
    ret_w_o_d = din("ret_w_o", [2, 2 * D, D])
    ret_decay_d = din("ret_decay", [2, 8])
    nat_w_in_d = din("nat_w_in", [2, D, 3 * D])
    nat_w_o_d = din("nat_w_o", [2, D, D])
    pad_d = din("pad", [2, 8, 32, 20, 127])
    colmask_d = din("colmask", [128, 128])
    rope_d = din("rope", [128, 4, SEQ])
    mlp_w1_d = din("mlp_w1", [DEPTH, D, 4 * D])
    mlp_w2_d = din("mlp_w2", [DEPTH, 4 * D, D])
    ln_g_d = din("ln_g", [DEPTH, 2, D])
    ln_b_d = din("ln_b", [DEPTH, 2, D])
    y_d = nc.dram_tensor("y", [SEQ, D], F32, kind="ExternalOutput").ap()

    XM_d = dscr("XM", [TOK, D])
    XN_d = dscr("XN", [TOK, D])
    MODR_d = dscr("MODR", [DEPTH, 2, 2, D])
    QT_d = dscr("QT", [NT, 128, 1024], BF16)
    KT_d = dscr("KT", [NT, 128, 1024], BF16)
    KD_d = dscr("KD", [2, NT, 128, 1024], BF16)
    V_d = dscr("V", [NT, 128, 2048], BF16)
    SG_d = dscr("SG", [NT, 128, 2048], BF16)
    OB_d = dscr("OB", [NT, 128, 2048], F32)
    VN_d = dscr("VN", [NT, 128, 1056], BF16)
    BIAS_d = dscr("BIAS", [5, 32, 5, 128, 128])
    dbg_outs = {}
    if dbg:
        for name, shape in dbg.items():
            dbg_outs[name] = nc.dram_tensor("dbg_" + name, list(shape), F32, kind="ExternalOutput").ap()

    P = Prog(nc)

    def ck(name):
        if stop == name:
            raise _Stop()

    es = contextlib.ExitStack()
    arena_t = es.enter_context(nc.sbuf_tensor("arena", [128, ARENA_WORDS], F32))
    ps_t = es.enter_context(nc.psum_tensor("ps", [128, 4096], F32))
    A = Arena(arena_t, ARENA_WORDS)

    def bank(b, n=1):
        return ps_t[:, b * 512:(b + n) * 512]

    def bank_bf(b, n=1):
        return ps_t[:, b * 512:(b + n) * 512].bitcast(BF16)

    def dma(q, out, in_, r=(), w=()):
        P.add(q, lambda e: e.dma_start(out=out, in_=in_), reads=r, writes=w, dma=True)

    def mm(out, lhsT, rhs, start, stop, r, w, tp=None):
        if tp is None:
            P.add("pe", lambda e: e.matmul(out, lhsT=lhsT, rhs=rhs, start=start, stop=stop), reads=r, writes=w)
        else:
            P.add("pe", lambda e: e.matmul(out, lhsT=lhsT, rhs=rhs, start=start, stop=stop, tile_position=tp), reads=r, writes=w)

    def tr(out, in_, ident, r, w):
        P.add("pe", lambda e: e.transpose(out=out, in_=in_, identity=ident), reads=list(r) + ["ident"], writes=w)

    def act(out, in_, func, r, w, scale=None, bias=None):
        kw = {}
        if scale is not None:
            kw["scale"] = scale
        if bias is not None:
            kw["bias"] = bias
        P.add("act", lambda e: e.activation(out=out, in_=in_, func=func, **kw), reads=r, writes=w)

    def tt(eng, out, in0, in1, op, r, w):
        P.add(eng, lambda e: e.tensor_tensor(out=out, in0=in0, in1=in1, op=op), reads=r, writes=w)

    def ts(eng, out, in0, s1, op0, r, w, s2=None, op1=None):
        if op1 is None:
            P.add(eng, lambda e: e.tensor_scalar(out=out, in0=in0, scalar1=s1, scalar2=None, op0=op0), reads=r, writes=w)
        else:
            P.add(eng, lambda e: e.tensor_scalar(out=out, in0=in0, scalar1=s1, scalar2=s2, op0=op0, op1=op1), reads=r, writes=w)

    def stt(eng, out, in0, scalar, in1, op0, op1, r, w):
        P.add(eng, lambda e: e.scalar_tensor_tensor(out=out, in0=in0, scalar=scalar, in1=in1, op0=op0, op1=op1), reads=r, writes=w)

    def cp(eng, out, in_, r, w):
        P.add(eng, lambda e: e.tensor_copy(out=out, in_=in_), reads=r, writes=w)

    def memset(eng, out, val, w, r=()):
        P.add(eng, lambda e: e.memset(out, val), reads=r, writes=w)

    identf = A.f32(128)
    ident = A.bf16(128)
    memset("pool", identf, 0.0, w=["identf"])
    P.add("pool", lambda e: e.affine_select(out=identf, in_=identf, pattern=[[-1, 128]], compare_op=ALU.not_equal,
                                            fill=1.0, base=0, channel_multiplier=1), reads=["identf"], writes=["identf"])
    cp("dve", ident, identf, r=["identf"], w=["ident"])
    epsc = A.f32(1)
    memset("pool", epsc, EPS, w=["epsc"])
    modT = A.f32(DEPTH * 2 * 4 * 8).rearrange("p (l v s c) -> p l v s c", l=DEPTH, v=2, s=4)
    persist_off = A.off

    def phase_mods():
        A.off = persist_off
        cv = A.f32(16)
        scT = A.f32(16)
        sig = A.f32(16)
        dma("sp", cv, cvT_d.rearrange("p c r -> p (c r)"), w=["cv"])
        act(sig, cv, AF.Sigmoid, r=["cv"], w=["sig"])
        tt("dve", scT, cv, sig, ALU.mult, r=["cv", "sig"], w=["scT"])
        scT3 = scT.rearrange("p (c r) -> p c r", r=2)
        wb = [A.f32(8 * 512) for _ in range(3)]
        abT = A.f32(DEPTH * 48)
        dma("sp", abT.rearrange("p (l c) -> p l c", l=DEPTH), ada_bT_d.rearrange("l p c -> p l c"), w=["abT"])
        abR = A.f32(DEPTH * 2 * D)
        rowo = A.f32(DEPTH * 2 * D)
        for l in range(n_layers):
            for wi, sec in enumerate((2, 5)):
                dma("sp", abR[0:2, (l * 2 + wi) * D:(l * 2 + wi + 1) * D],
                    ada_b_d[l:l + 1, sec * D:(sec + 1) * D].partition_broadcast(2), w=["abR"])
        it = 0
        for l in range(n_layers):
            for grp in range(12):
                s = it % 3
                pb_ = it % 2
                it += 1
                wv = wb[s].rearrange("p (c n) -> p c n", c=8)
                dma("sp" if it % 2 else "act", wv, ada_w_d[l].rearrange("(c p) n -> p c n", p=128)[:, :, grp * 512:(grp + 1) * 512],
                    w=[f"wb{s}"])
                sec = grp // 2
                if sec in (2, 5):
                    wi = 0 if sec == 2 else 1
                    half = grp % 2
                    for kc in range(8):
                        mm(bank(2 + pb_)[0:2, :], lhsT=scT3[:, kc, :], rhs=wv[:, kc, :], start=kc == 0, stop=kc == 7,
                           r=["scT", f"wb{s}"], w=[("ps", 2 + pb_)])
                    o = (l * 2 + wi) * D + half * 512
                    tt("dve", rowo[0:2, o:o + 512], bank(2 + pb_)[0:2, :], abR[0:2, o:o + 512], ALU.add,
                       r=[("ps", 2 + pb_), "abR"], w=["rowo"])
                else:
                    si = {0: 0, 1: 1, 3: 2, 4: 3}[sec]
                    for cc in range(4):
                        for kc in range(8):
                            mm(bank(pb_)[:, cc * 2:cc * 2 + 2], lhsT=wv[:, kc, cc * 128:(cc + 1) * 128], rhs=scT3[:, kc, :],
                               start=kc == 0, stop=kc == 7, r=["scT", f"wb{s}"], w=[("ps", pb_)])
                    for v in range(2):
                        c0 = (grp % 2) * 4
                        ab = abT[:, l * 48 + grp * 4:l * 48 + grp * 4 + 4]
                        src = bank(pb_)[:, 0:8].rearrange("p (c v) -> p c v", v=2)[:, :, v]
                        tt("dve", modT[:, l, v, si, c0:c0 + 4], src, ab, ALU.add, r=[("ps", pb_), "abT"], w=["modT"])
        for l in range(n_layers):
            for v in range(2):
                for si in (1, 3):
                    ts("dve", modT[:, l, v, si, :], modT[:, l, v, si, :], 1.0, ALU.add, r=["modT"], w=["modT"])
        for l in range(n_layers):
            for wi in range(2):
                o = (l * 2 + wi) * D
                dma("pool", MODR_d[l, :, wi, :], rowo[0:2, o:o + D], r=["rowo"], w=["MODR"])
        P.barrier()
        ck('mods')

    def load_row_bc(dst, src_row, w, r=()):
        dma("sp", dst, src_row.partition_broadcast(128), r=r, w=w)

    def make_hT(xt, xkey, hT_view, hkey, l, ver, si_sh, si_sc, tb):
        pt = bank(tb, 2)
        for c in range(8):
            tr(pt[:, c * 128:(c + 1) * 128], xt[:, c * 128:(c + 1) * 128], identf, r=[xkey], w=[("ps", tb), ("ps", tb + 1)])
        for c in range(8):
            act(hT_view[:, c, :], pt[:, c * 128:(c + 1) * 128], AF.Identity, r=[("ps", tb), ("ps", tb + 1), "modT"], w=[hkey],
                scale=modT[:, l, ver, si_sc, c:c + 1], bias=modT[:, l, ver, si_sh, c:c + 1])

    class LNctx:
        pass

    def alloc_ln(nbuf=2):
        c = LNctx()
        c.gtab = A.f32(D)
        c.lng = A.f32(D)
        c.lnb = A.f32(D)
        c.r = [A.f32(D) for _ in range(nbuf)]
        c.st = [A.f32(12) for _ in range(nbuf)]
        c.mv = [A.f32(4) for _ in range(nbuf)]
        c.n = 0
        c.nbuf = nbuf
        return c

    def load_ln_tabs(c, l, which, ver):
        load_row_bc(c.gtab, MODR_d[l, ver, which:which + 1, :], w=["gtab"], r=["MODR"])
        load_row_bc(c.lng, ln_g_d[l, which:which + 1, :], w=["lng"])
        load_row_bc(c.lnb, ln_b_d[l, which:which + 1, :], w=["lnb"])

    def res_ln_a(c, yb0, yb1):
        s = c.n % c.nbuf
        c.n += 1
        r = c.r[s]
        rk = f"lnr{s}"
        for g, yb in enumerate((yb0, yb1)):
            tt("dve", r[:, g * 512:(g + 1) * 512], bank(yb), c.gtab[:, g * 512:(g + 1) * 512], ALU.mult,
               r=[("ps", yb), "gtab"], w=[rk])
        return s

    def res_ln_b(c, s, xt, xkey, dst_d, dst_key):
        r = c.r[s]
        rk = f"lnr{s}"
        stt("dve", r, xt, ALPHA, r, ALU.mult, ALU.add, r=[xkey, rk], w=[rk])
        st = c.st[s].rearrange("p (a b) -> p a b", a=2)
        for g in range(2):
            P.add("dve", lambda e, g=g: e.bn_stats(out=st[:, g, :], in_=r[:, g * 512:(g + 1) * 512]), reads=[rk], writes=[f"lnst{s}"])
        mv = c.mv[s]
        P.add("dve", lambda e: e.bn_aggr(out=mv[:, 0:2], in_=st), reads=[f"lnst{s}"], writes=[f"lnmv{s}"])
        act(mv[:, 2:3], mv[:, 1:2], AF.Ln, r=[f"lnmv{s}"], w=[f"lnmv{s}"], bias=epsc)
        act(mv[:, 2:3], mv[:, 2:3], AF.Exp, r=[f"lnmv{s}"], w=[f"lnmv{s}"], scale=-0.5)
        ts("dve", mv[:, 3:4], mv[:, 0:1], mv[:, 2:3], ALU.mult, r=[f"lnmv{s}"], w=[f"lnmv{s}"], s2=-1.0, op1=ALU.mult)
        act(r, r, AF.Identity, r=[rk, f"lnmv{s}"], w=[rk], scale=mv[:, 2:3], bias=mv[:, 3:4])
        tt("dve", r, r, c.lng, ALU.mult, r=[rk, "lng"], w=[rk])
        tt("pool", r, r, c.lnb, ALU.add, r=[rk, "lnb"], w=[rk])
        dma("pool", dst_d, r, r=[rk], w=[dst_key])

    def res_ln(c, yb0, yb1, xt, xkey, dst_d, dst_key):
        s = res_ln_a(c, yb0, yb1)
        res_ln_b(c, s, xt, xkey, dst_d, dst_key)

    cast_rr = [0]

    def cast(out, in_, r, w, scale=None):
        i = cast_rr[0] % 2
        cast_rr[0] += 1
        if i == 0:
            if scale is None:
                act(out, in_, AF.Copy, r, w)
            else:
                act(out, in_, AF.Identity, r, w, scale=scale)
        else:
            if scale is None:
                cp("dve", out, in_, r, w)
            else:
                ts("dve", out, in_, scale, ALU.mult, r, w)

    def load_w(dst3, src2, nchunk, key, stage, col0=0, ncol=None, kstep=1):
        ncol_ = ncol if ncol is not None else src2.shape[1]
        srcv = src2.rearrange("(c p) n -> p c n", p=128)
        for k0 in range(0, nchunk, kstep):
            s_ = (k0 // kstep) % len(stage)
            st = stage[s_][:, 0:kstep * ncol_].rearrange("p (c n) -> p c n", c=kstep)
            dma("sp" if s_ % 2 == 0 else "act", st, srcv[:, k0:k0 + kstep, col0:col0 + ncol_], w=[f"stage{s_}"])
            cast(dst3[:, k0:k0 + kstep, :], st, r=[f"stage{s_}"], w=[key])

    def bias_expansion_thunks(j):
        th = []
        for ty in range(5):
            off = (4, 0, 2, 6, 8)[ty]
            vv = ((0, 0), (1, 2), (3, 4), (0, 5), (6, 7))[ty]
            for h in range(32):
                for jq in range(2):
                    base = (((j * 8 + vv[jq]) * 32 + h) * 20 + (10 - jq - off)) * 127 + 63
                    src = bass.AP(pad_d.tensor, base, [[2 * 127, 5], [127, 2], [-1, 64], [1, 64]])
                    dst = BIAS_d[ty, h].rearrange("j (i k) (a q) -> j i k a q", i=2, a=2)[:, :, :, jq, :]
                    th.append((dst, src, ("BIAS", ty, h)))
        return th

    def phase_mlp(l, src_d, src_name, dst_d, dst_name, last, extra=()):
        A.off = persist_off
        w1 = A.bf16(8 * 4096).rearrange("p (c n) -> p c n", c=8)
        w2 = A.bf16(32 * 1024).rearrange("p (c n) -> p c n", c=32)
        o1 = A.off
        stage = [A.f32(4096) for _ in range(NSTAGE)]
        load_w(w1, mlp_w1_d[l], 8, "w1", stage)
        load_w(w2, mlp_w2_d[l], 32, "w2", stage, kstep=4)
        P.barrier()
        A.off = o1
        ln = alloc_ln()
        xt = [A.f32(D) for _ in range(4)]
        hT = [A.bf16(8 * 256).rearrange("p (c t) -> p c t", c=8) for _ in range(2)]
        uT = A.bf16(32 * 256)
        rl = [A.f32(512) for _ in range(2)]
        sts2 = ([] if last else [0]) + [2 + 2 * k for k in range(16)]
        cur_ver = None
        extra = list(extra)
        per = (len(extra) + len(sts2) - 1) // len(sts2) if extra else 0

        def prep(si):
            t0 = sts2[si]
            for tl in range(2):
                t = t0 + tl
                xs_ = (2 * si + tl) % 4
                dma("sp", xt[xs_], src_d[t * 128:(t + 1) * 128, :], r=[(src_name, t)], w=[f"xt{xs_}"])
                make_hT(xt[xs_], f"xt{xs_}", hT[si % 2][:, :, tl * 128:(tl + 1) * 128], f"hT{si % 2}", l,
                        1 if t < 2 else 0, 2, 3, 6)
            for (dst_, src_, key_) in extra[si * per:(si + 1) * per]:
                dma("sp", dst_, src_, w=[key_])

        prep(0)
        for si, t0 in enumerate(sts2):
            ver = 1 if t0 < 2 else 0
            if ver != cur_ver:
                load_ln_tabs(ln, l, 1, ver)
                cur_ver = ver
            s = si % 2
            for grp in range(16):
                pb = grp % 2
                for cc in range(2):
                    j = grp * 2 + cc
                    for kc in range(8):
                        mm(bank(pb)[:, cc * 256:(cc + 1) * 256], lhsT=w1[:, kc, j * 128:(j + 1) * 128], rhs=hT[s][:, kc, :],
                           start=kc == 0, stop=kc == 7, r=["w1", f"hT{s}"], w=[("ps", pb)])
                act(rl[pb], bank(pb), AF.Relu, r=[("ps", pb)], w=[f"rl{pb}"])
                tt("dve", uT[:, grp * 512:(grp + 1) * 512], rl[pb], rl[pb], ALU.mult, r=[f"rl{pb}"], w=["uT"])
            if si + 1 < len(sts2):
                prep(si + 1)
            for tl in range(2):
                t = t0 + tl
                xs_ = (2 * si + tl) % 4
                yb = 2 + 2 * tl
                for g in range(2):
                    for j in range(32):
                        mm(bank(yb + g), lhsT=uT[:, j * 256 + tl * 128:j * 256 + (tl + 1) * 128], rhs=w2[:, j, g * 512:(g + 1) * 512],
                           start=j == 0, stop=j == 31, r=["uT", "w2"], w=[("ps", yb + g)])
                if last:
                    res_ln(ln, yb, yb + 1, xt[xs_], f"xt{xs_}", y_d[(t - 2) * 128:(t - 1) * 128, :], ("y", t))
                else:
                    res_ln(ln, yb, yb + 1, xt[xs_], f"xt{xs_}", dst_d[t * 128:(t + 1) * 128, :], (dst_name, t))
        P.barrier()

    def phase_ret(l, j, src_d, src_name):
        A.off = persist_off
        dp = A.f32(8)
        lg = A.f32(8)
        load_row_bc(dp, ret_decay_d[j:j + 1, :], w=["dp"])
        act(lg, dp, AF.Exp, r=["dp"], w=["lg"])
        ts("dve", lg, lg, -1.0, ALU.mult, r=["lg"], w=["lg"])
        Di = A.f32(128, I32)
        Dq = A.f32(128)
        P.add("pool", lambda e: e.iota(Di, pattern=[[1, 128]], base=0, channel_multiplier=-1), writes=["Di"])
        cp("dve", Dq, Di, r=["Di"], w=["Dq"])
        pos_i = A.f32(128, I32)
        pos = A.f32(128)
        P.add("pool", lambda e: e.iota(pos_i, pattern=[[1, 128]], base=0, channel_multiplier=0), writes=["posi"])
        cp("dve", pos, pos_i, r=["posi"], w=["pos"])
        par_i = A.f32(1, I32)
        par = A.f32(1)
        P.add("pool", lambda e: e.iota(par_i, pattern=[[1, 1]], base=0, channel_multiplier=1), writes=["pari"])
        cp("dve", par, par_i, r=["pari"], w=["par"])
        tmpA = A.f32(128)
        tmpB = A.f32(128)
        maskT = A.f32(2 * 4 * 128).rearrange("p (d h q) -> p d h q", d=2, h=4)
        qdec = A.f32(2 * 4 * 128).rearrange("p (d h q) -> p d h q", d=2, h=4)
        kdec = A.f32(8).rearrange("p (d h) -> p d h", d=2)
        cdec = A.f32(8).rearrange("p (d h) -> p d h", d=2)
        for d in range(2):
            if d == 0:
                ts("dve", tmpA, Dq, 0.0, ALU.max, r=["Dq"], w=["tmpA"])
                ts("dve", tmpB, Dq, 0.0, ALU.is_ge, r=["Dq"], w=["tmpB"])
            else:
                ts("dve", tmpA, Dq, -1.0, ALU.mult, r=["Dq"], w=["tmpA"], s2=0.0, op1=ALU.max)
                ts("dve", tmpB, Dq, 0.0, ALU.is_le, r=["Dq"], w=["tmpB"])
            for h in range(4):
                sc = lg[:, d * 4 + h:d * 4 + h + 1]
                act(maskT[:, d, h, :], tmpA, AF.Exp, r=["tmpA", "lg"], w=["maskT"], scale=sc)
                tt("dve", maskT[:, d, h, :], maskT[:, d, h, :], tmpB, ALU.mult, r=["maskT", "tmpB"], w=["maskT"])
            tq = A.f32(128)
            tk = A.f32(1)
            if d == 0:
                ts("dve", tq, pos, 1.0, ALU.add, r=["pos"], w=[f"tq{d}"])
                ts("dve", tk, par, -1.0, ALU.mult, r=["par"], w=[f"tk{d}"], s2=127.0, op1=ALU.add)
            else:
                ts("dve", tq, pos, -1.0, ALU.mult, r=["pos"], w=[f"tq{d}"], s2=128.0, op1=ALU.add)
                cp("dve", tk, par, r=["par"], w=[f"tk{d}"])
            for h in range(4):
                sc = lg[:, d * 4 + h:d * 4 + h + 1]
                act(qdec[:, d, h, :], tq, AF.Exp, r=[f"tq{d}", "lg"], w=["qdec"], scale=sc)
                act(kdec[:, d, h:h + 1], tk, AF.Exp, r=[f"tk{d}", "lg"], w=["kdec"], scale=sc)
                act(cdec[:, d, h:h + 1], sc, AF.Exp, r=["lg"], w=["cdec"], scale=128.0)
        ret_off = A.off
        ck('consts')

        wqk = A.bf16(8 * 4096).rearrange("p (c n) -> p c n", c=8)
        o1 = A.off
        stage = [A.f32(2048) for _ in range(NSTAGE)]
        wsrcv = ret_w_in_d[j].rearrange("(c p) n -> p c n", p=128)
        for kc in range(8):
            s_ = kc % NSTAGE
            st = stage[s_]
            dma("sp" if s_ % 2 == 0 else "act", st, wsrcv[:, kc, 0:2048], w=[f"stage{s_}"])
            cast(wqk[:, kc, 0:1024], st[:, 0:1024], r=[f"stage{s_}"], w=["wqk"])
            cast(wqk[:, kc, 1024:2048], st[:, 1024:2048], r=[f"stage{s_}"], w=["wqk"], scale=0.0625)
            stv = st.rearrange("p (ch a b) -> p ch a b", a=2, b=64)
            dv = wqk[:, kc, 2048:4096].rearrange("p (ch a b) -> p ch a b", a=2, b=64)
            for a in range(2):
                cast(dv[:, 0:8, a, :], stv[:, 0:8, 1 - a, :], r=[f"stage{s_}"], w=["wqk"])
                cast(dv[:, 8:16, a, :], stv[:, 8:16, 1 - a, :], r=[f"stage{s_}"], w=["wqk"], scale=0.0625)
        P.barrier()
        A.off = o1
        xt = [A.f32(D) for _ in range(2)]
        ropet = [A.f32(4 * 512).rearrange("p (a t) -> p a t", a=4) for _ in range(2)]
        hT = [A.bf16(8 * 512).rearrange("p (c t) -> p c t", c=8) for _ in range(2)]
        qTo = [A.bf16(8 * 512).rearrange("p (c t) -> p c t", c=8) for _ in range(2)]
        kTo = [A.bf16(8 * 512).rearrange("p (c t) -> p c t", c=8) for _ in range(2)]
        t1 = [A.f32(512) for _ in range(2)]
        t2 = [A.f32(512) for _ in range(2)]
        kd = [[A.bf16(1024) for _ in range(2)] for _ in range(2)]
        sts = [(0, 2)] + [(2 + 4 * k, 4) for k in range(8)]
        xi = 0
        ci = 0
        ki = 0
        xi_ = [0]

        def prep_tile(si, tl):
            t0_, ntl_ = sts[si]
            s_ = si % 2
            lat_ = t0_ >= 2
            if tl == 0 and lat_:
                dma("sp", ropet[s_], rope_d[:, :, (t0_ - 2) * 128:(t0_ - 2) * 128 + 512], w=[f"rope{s_}"])
            t = t0_ + tl
            xs_ = xi_[0] % 2
            xi_[0] += 1
            dma("sp", xt[xs_], src_d[t * 128:(t + 1) * 128, :], r=[(src_name, t)], w=[f"xt{xs_}"])
            make_hT(xt[xs_], f"xt{xs_}", hT[s_][:, :, tl * 128:(tl + 1) * 128], f"hT{s_}", l, 0 if lat_ else 1, 0, 1, 6)

        for tl in range(sts[0][1]):
            prep_tile(0, tl)
        for si, (t0, ntl) in enumerate(sts):
            s = si % 2
            lat = t0 >= 2
            ver = 0 if lat else 1
            ntok = ntl * 128
            for c in range(16):
                cs = ci % 2
                ci += 1
                dst = (qTo if c < 8 else kTo)[s]
                dkey = (f"qTo{s}" if c < 8 else f"kTo{s}")
                pa, pb = 2 * cs, 2 * cs + 1
                for kc in range(8):
                    mm(bank(pa)[:, 0:ntok], lhsT=wqk[:, kc, c * 128:(c + 1) * 128], rhs=hT[s][:, kc, 0:ntok],
                       start=kc == 0, stop=kc == 7, r=["wqk", f"hT{s}"], w=[("ps", pa)])
                if lat:
                    for kc in range(8):
                        mm(bank(pb)[:, 0:ntok], lhsT=wqk[:, kc, 2048 + c * 128:2048 + (c + 1) * 128], rhs=hT[s][:, kc, 0:ntok],
                           start=kc == 0, stop=kc == 7, r=["wqk", f"hT{s}"], w=[("ps", pb)])
                    ty = c % 2
                    tt("dve", t1[cs], bank(pa), ropet[s][:, ty, :], ALU.mult, r=[("ps", pa), f"rope{s}"], w=[f"t1{cs}"])
                    tt("dve", t2[cs], bank(pb), ropet[s][:, 2 + ty, :], ALU.mult, r=[("ps", pb), f"rope{s}"], w=[f"t2{cs}"])
                    tt("pool", dst[:, c % 8, :], t1[cs], t2[cs], ALU.add, r=[f"t1{cs}", f"t2{cs}"], w=[dkey])
                else:
                    act(dst[:, c % 8, 0:ntok], bank(pa)[:, 0:ntok], AF.Copy, r=[("ps", pa)], w=[dkey])
                if si + 1 < len(sts) and c % 4 == 3 and (c // 4) < sts[si + 1][1]:
                    prep_tile(si + 1, c // 4)
            for tl in range(ntl):
                t = t0 + tl
                dma("pool", QT_d[t].rearrange("p (c t) -> p c t", c=8), qTo[s][:, :, tl * 128:(tl + 1) * 128], r=[f"qTo{s}"], w=[("QT", t)])
                dma("pool", KT_d[t].rearrange("p (c t) -> p c t", c=8), kTo[s][:, :, tl * 128:(tl + 1) * 128], r=[f"kTo{s}"], w=[("KT", t)])
                ks = ki % 2
                ki += 1
                for c in range(8):
                    tr(bank_bf(4 + ks)[:, c * 128:(c + 1) * 128], kTo[s][:, c, tl * 128:(tl + 1) * 128], ident,
                       r=[f"kTo{s}"], w=[("ps", 4 + ks)])
                for d in range(2):
                    tt("dve", kd[ks][d].rearrange("p (h x) -> p h x", h=4), bank_bf(4 + ks).rearrange("p (h x) -> p h x", h=4),
                       kdec[:, d, :].unsqueeze(2).to_broadcast([128, 4, 256]), ALU.mult,
                       r=[("ps", 4 + ks), "kdec"], w=[f"kd{ks}{d}"])
                    dma("pool", KD_d[d, t], kd[ks][d], r=[f"kd{ks}{d}"], w=[("KD", d, t)])
        P.barrier()
        ck('A1')

        A.off = ret_off
        wvg = A.bf16(8 * 4096).rearrange("p (c n) -> p c n", c=8)
        o1 = A.off
        stage = [A.f32(4096) for _ in range(NSTAGE)]
        load_w(wvg, ret_w_in_d[j], 8, "wvg", stage, col0=2048, ncol=4096)
        P.barrier()
        A.off = o1
        xt = [A.f32(D) for _ in range(2)]
        hT = [A.bf16(1024).rearrange("p (c t) -> p c t", c=8) for _ in range(2)]
        vo = [A.bf16(2048) for _ in range(2)]
        sgo = [A.bf16(2048) for _ in range(2)]
        sgt = [A.f32(512) for _ in range(2)]
        gi = 0

        def prep2(t):
            dma("sp", xt[t % 2], src_d[t * 128:(t + 1) * 128, :], r=[(src_name, t)], w=[f"xt{t % 2}"])
            make_hT(xt[t % 2], f"xt{t % 2}", hT[t % 2], f"hT{t % 2}", l, 1 if t < 2 else 0, 0, 1, 6)

        prep2(0)
        for t in range(NT):
            s = t % 2
            for g in range(8):
                if g == 4 and t + 1 < NT:
                    prep2(t + 1)
                b = gi % 4
                gi += 1
                for kc in range(8):
                    mm(bank(b), lhsT=hT[s][:, kc, :], rhs=wvg[:, kc, g * 512:(g + 1) * 512], start=kc == 0, stop=kc == 7,
                       r=[f"hT{s}", "wvg"], w=[("ps", b)])
                if g < 4:
                    act(vo[s][:, g * 512:(g + 1) * 512], bank(b), AF.Copy, r=[("ps", b)], w=[f"vo{s}"])
                else:
                    sb_ = b % 2
                    act(sgt[sb_], bank(b), AF.Sigmoid, r=[("ps", b)], w=[f"sgt{sb_}"])
                    tt("dve", sgo[s][:, (g - 4) * 512:(g - 3) * 512], bank(b), sgt[sb_], ALU.mult,
                       r=[("ps", b), f"sgt{sb_}"], w=[f"sgo{s}"])
            dma("pool", V_d[t], vo[s], r=[f"vo{s}"], w=[("V", t)])
            dma("pool", SG_d[t], sgo[s], r=[f"sgo{s}"], w=[("SG", t)])
        P.barrier()
        ck('A2')

        A.off = ret_off
        stf = A.f32(8 * 512).rearrange("p (c n) -> p c n", c=8)
        stb = A.bf16(8 * 512).rearrange("p (c n) -> p c n", c=8)
        qTt = [A.bf16(1024) for _ in range(2)]
        kTt = [A.bf16(1024) for _ in range(2)]
        kdt = [A.bf16(1024) for _ in range(2)]
        vt = [A.bf16(2048) for _ in range(2)]
        qd = [A.bf16(1024) for _ in range(2)]
        obt = [A.f32(2048) for _ in range(2)]
        pT = [A.bf16(128) for _ in range(2)]
        for d in (1, 0):
            order = [1, 0] + list(range(NT - 1, 1, -1)) if d == 1 else list(range(NT))
            memset("dve", stf, 0.0, w=[f"stf{c}" for c in range(8)])
            memset("pool", stb, 0.0, w=[f"stb{c}" for c in range(8)])
            for n, t in enumerate(order):
                s = n % 2
                dma("sp", qTt[s], QT_d[t], r=[("QT", t)], w=[f"qTt{s}"])
                dma("sp", kTt[s], KT_d[t], r=[("KT", t)], w=[f"kTt{s}"])
                dma("sp", kdt[s], KD_d[d, t], r=[("KD", d, t)], w=[f"kdt{s}"])
                dma("sp", vt[s], V_d[t], r=[("V", t)], w=[f"vt{s}"])
                if d == 0:
                    dma("sp", obt[s], OB_d[t], r=[("OB", t)], w=[f"obt{s}"])
                tt("dve", qd[s].rearrange("p (h c q) -> p h c q", h=4, c=2), qTt[s].rearrange("p (h c q) -> p h c q", h=4, c=2),
                   qdec[:, d, :, :].unsqueeze(2).to_broadcast([128, 4, 2, 128]), ALU.mult, r=[f"qTt{s}", "qdec"], w=[f"qd{s}"])
                for h in range(4):
                    hs = h % 2
                    bS, bO, bU = hs, 2 + hs, 4 + 2 * hs
                    for dc in range(2):
                        c = 2 * h + dc
                        mm(bank(bS)[:, 0:128], lhsT=kTt[s][:, c * 128:(c + 1) * 128], rhs=qTt[s][:, c * 128:(c + 1) * 128],
                           start=dc == 0, stop=dc == 1, r=[f"kTt{s}", f"qTt{s}"], w=[("ps", bS)])
                    tt("dve", pT[hs], bank(bS)[:, 0:128], maskT[:, d, h, :], ALU.mult, r=[("ps", bS), "maskT"], w=[f"pT{hs}"])
                    for dc in range(2):
                        c = 2 * h + dc
                        mm(bank(bU + dc), lhsT=kdt[s][:, c * 128:(c + 1) * 128], rhs=vt[s][:, h * 512:(h + 1) * 512],
                           start=True, stop=True, r=[f"kdt{s}", f"vt{s}"], w=[("ps", bU + dc)])
                    mm(bank(bO), lhsT=pT[hs], rhs=vt[s][:, h * 512:(h + 1) * 512], start=True, stop=False,
                       r=[f"pT{hs}", f"vt{s}"], w=[("ps", bO)])
                    for dc in range(2):
                        c = 2 * h + dc
                        mm(bank(bO), lhsT=qd[s][:, c * 128:(c + 1) * 128], rhs=stb[:, c, :], start=False, stop=dc == 1,
                           r=[f"qd{s}", f"stb{c}"], w=[("ps", bO)])
                    if d == 1:
                        act(obt[s][:, h * 512:(h + 1) * 512], bank(bO), AF.Copy, r=[("ps", bO)], w=[f"obt{s}"])
                    else:
                        tt("dve", obt[s][:, h * 512:(h + 1) * 512], bank(bO), obt[s][:, h * 512:(h + 1) * 512], ALU.add,
                           r=[("ps", bO), f"obt{s}"], w=[f"obt{s}"])
                    for dc in range(2):
                        c = 2 * h + dc
                        stt("dve", stf[:, c, :], stf[:, c, :], cdec[:, d, h:h + 1], bank(bU + dc), ALU.mult, ALU.add,
                            r=[f"stf{c}", ("ps", bU + dc), "cdec"], w=[f"stf{c}"])
                        act(stb[:, c, :], stf[:, c, :], AF.Copy, r=[f"stf{c}"], w=[f"stb{c}"])
                dma("pool", OB_d[t], obt[s], r=[f"obt{s}"], w=[("OB", t)])
        P.barrier()

        A.off = ret_off
        wo = A.bf16(16 * 1024).rearrange("p (c n) -> p c n", c=16)
        o1 = A.off
        stage = [A.f32(4096) for _ in range(NSTAGE)]
        load_w(wo, ret_w_o_d[j], 16, "wo", stage, kstep=4)
        P.barrier()
        A.off = o1
        ln = alloc_ln()
        ost = [A.f32(2048) for _ in range(2)]
        sgt_ = [A.bf16(2048) for _ in range(2)]
        xt = [A.f32(D) for _ in range(2)]
        onf = [A.f32(512) for _ in range(2)]
        onb = [A.bf16(2048) for _ in range(2)]
        onT = [A.bf16(2048) for _ in range(2)]
        gst = [A.f32(4 * 6).rearrange("p (h x) -> p h x", h=4) for _ in range(2)]
        gmv = [A.f32(4 * 2).rearrange("p (h x) -> p h x", h=4) for _ in range(2)]
        grs = [A.f32(8) for _ in range(2)]
        xt3 = xt + [A.f32(D)]

        def d_stage1(t):
            s = t % 2
            dma("sp", ost[s], OB_d[t], r=[("OB", t)], w=[f"ost{s}"])
            dma("sp", sgt_[s], SG_d[t], r=[("SG", t)], w=[f"sgt_{s}"])
            dma("sp", xt3[t % 3], src_d[t * 128:(t + 1) * 128, :], r=[(src_name, t)], w=[f"xt{t % 3}"])
            for h in range(4):
                P.add("dve", lambda e, h=h, s=s: e.bn_stats(out=gst[s][:, h, :], in_=ost[s][:, h * 512:(h + 1) * 512]),
                      reads=[f"ost{s}"], writes=[f"gst{s}"])
                P.add("dve", lambda e, h=h, s=s: e.bn_aggr(out=gmv[s][:, h, :], in_=gst[s][:, h:h + 1, :]),
                      reads=[f"gst{s}"], writes=[f"gmv{s}"])
            act(grs[s][:, 0:4], gmv[s][:, :, 1], AF.Ln, r=[f"gmv{s}"], w=[f"grs{s}"], bias=epsc)
            act(grs[s][:, 0:4], grs[s][:, 0:4], AF.Exp, r=[f"grs{s}"], w=[f"grs{s}"], scale=-0.5)
            stt("dve", grs[s][:, 4:8], gmv[s][:, :, 0], -1.0, grs[s][:, 0:4], ALU.mult, ALU.mult,
                r=[f"gmv{s}", f"grs{s}"], w=[f"grs{s}"])
            for h in range(4):
                hs = h % 2
                act(onf[hs], ost[s][:, h * 512:(h + 1) * 512], AF.Identity, r=[f"ost{s}", f"grs{s}"], w=[f"onf{hs}"],
                    scale=grs[s][:, h:h + 1], bias=grs[s][:, 4 + h:5 + h])
                tt("pool", onb[s][:, h * 512:(h + 1) * 512], onf[hs], sgt_[s][:, h * 512:(h + 1) * 512], ALU.mult,
                   r=[f"onf{hs}", f"sgt_{s}"], w=[f"onb{s}"])

        def d_stage2(t):
            s = t % 2
            tb = 2 * s
            for c in range(16):
                tr(bank_bf(tb, 2)[:, c * 128:(c + 1) * 128], onb[s][:, c * 128:(c + 1) * 128], ident, r=[f"onb{s}"],
                   w=[("ps", tb), ("ps", tb + 1)])
            act(onT[s], bank_bf(tb, 2), AF.Copy, r=[("ps", tb), ("ps", tb + 1)], w=[f"onT{s}"])
            yb = 4 + 2 * s
            for g in range(2):
                for c in range(16):
                    mm(bank(yb + g), lhsT=onT[s][:, c * 128:(c + 1) * 128], rhs=wo[:, c, g * 512:(g + 1) * 512],
                       start=c == 0, stop=c == 15, r=[f"onT{s}", "wo"], w=[("ps", yb + g)])

        cur_ver_ = [None]

        def d_stage3(t):
            s = t % 2
            ver = 1 if t < 2 else 0
            if ver != cur_ver_[0]:
                load_ln_tabs(ln, l, 0, ver)
                cur_ver_[0] = ver
            yb = 4 + 2 * s
            res_ln(ln, yb, yb + 1, xt3[t % 3], f"xt{t % 3}", XM_d[t * 128:(t + 1) * 128, :], ("XM", t))

        for i in range(NT + 2):
            if i < NT:
                d_stage1(i)
            if 0 <= i - 1 < NT:
                d_stage2(i - 1)
            if 0 <= i - 2 < NT:
                d_stage3(i - 2)
        P.barrier()

    def phase_nat(l, j, src_d, src_name, last):
        A.off = persist_off
        wn = A.bf16(8 * 3072).rearrange("p (c n) -> p c n", c=8)
        o1 = A.off
        stage = [A.f32(3072) for _ in range(NSTAGE)]
        load_w(wn, nat_w_in_d[j], 8, "wn", stage)
        P.barrier()
        A.off = o1
        xt = [A.f32(D) for _ in range(2)]
        hT = [A.bf16(8 * 512).rearrange("p (c t) -> p c t", c=8) for _ in range(2)]
        qTo = [A.bf16(8 * 512).rearrange("p (c t) -> p c t", c=8) for _ in range(2)]
        kTo = [A.bf16(8 * 512).rearrange("p (c t) -> p c t", c=8) for _ in range(2)]
        vno = [A.bf16(1056) for _ in range(2)]
        for s in range(2):
            memset("pool", vno[s], 1.0, w=[f"vno{s}"])
        sts = [(0, 2)] + [(2 + 4 * k, 4) for k in range(8)]
        ci = 0
        vi = 0
        xi_ = [0]

        def prep_tile(si, tl):
            t0_, ntl_ = sts[si]
            s_ = si % 2
            t = t0_ + tl
            xs_ = xi_[0] % 2
            xi_[0] += 1
            dma("sp", xt[xs_], src_d[t * 128:(t + 1) * 128, :], r=[(src_name, t)], w=[f"xt{xs_}"])
            make_hT(xt[xs_], f"xt{xs_}", hT[s_][:, :, tl * 128:(tl + 1) * 128], f"hT{s_}", l, 0 if t0_ >= 2 else 1, 0, 1, 6)

        for tl in range(sts[0][1]):
            prep_tile(0, tl)
        for si, (t0, ntl) in enumerate(sts):
            s = si % 2
            ver = 0 if t0 >= 2 else 1
            ntok = ntl * 128
            for c in range(16):
                b = ci % 4
                ci += 1
                for kc in range(8):
                    mm(bank(b)[:, 0:ntok], lhsT=wn[:, kc, c * 128:(c + 1) * 128], rhs=hT[s][:, kc, 0:ntok],
                       start=kc == 0, stop=kc == 7, r=["wn", f"hT{s}"], w=[("ps", b)])
                if c < 8:
                    act(qTo[s][:, c, 0:ntok], bank(b)[:, 0:ntok], AF.Identity, r=[("ps", b)], w=[f"qTo{s}"], scale=NAT_SCALE)
                else:
                    cp("dve", kTo[s][:, c - 8, 0:ntok], bank(b)[:, 0:ntok], r=[("ps", b)], w=[f"kTo{s}"])
                if si + 1 < len(sts) and c % 4 == 3 and (c // 4) < sts[si + 1][1]:
                    prep_tile(si + 1, c // 4)
            for tl in range(ntl):
                t = t0 + tl
                dma("pool", QT_d[t].rearrange("p (c t) -> p c t", c=8), qTo[s][:, :, tl * 128:(tl + 1) * 128], r=[f"qTo{s}"], w=[("QT", t)])
                dma("pool", KT_d[t].rearrange("p (c t) -> p c t", c=8), kTo[s][:, :, tl * 128:(tl + 1) * 128], r=[f"kTo{s}"], w=[("KT", t)])
                vs = vi % 2
                vi += 1
                for g in range(2):
                    b = 4 + g
                    for kc in range(8):
                        mm(bank(b), lhsT=hT[s][:, kc, tl * 128:(tl + 1) * 128], rhs=wn[:, kc, 2048 + g * 512:2048 + (g + 1) * 512],
                           start=kc == 0, stop=kc == 7, r=[f"hT{s}", "wn"], w=[("ps", b)])
                    act(vno[vs].rearrange("p (h x) -> p h x", x=33)[:, g * 16:(g + 1) * 16, 0:32],
                        bank(b).rearrange("p (h x) -> p h x", x=32), AF.Copy, r=[("ps", b)], w=[f"vno{vs}"])
                dma("pool", VN_d[t], vno[vs], r=[f"vno{vs}"], w=[("VN", t)])
        P.barrier()
        ck('NA')

        A.off = persist_off
        won = A.bf16(8 * 1024).rearrange("p (c n) -> p c n", c=8)
        o1 = A.off
        stage = [A.f32(4096), A.f32(4096)]
        load_w(won, nat_w_o_d[j], 8, "won", stage, kstep=4)
        P.barrier()
        A.off = o1
        cmask = A.f32(128)
        dma("sp", cmask, colmask_d[:, :], w=["cmask"])
        bias_int = A.bf16(32 * 5 * 128).rearrange("p (h j q) -> p h j q", h=32, j=5)
        bstage = [A.f32(5 * 128).rearrange("p (j q) -> p j q", j=5) for _ in range(2)]
        bias_b = [A.bf16(5 * 128).rearrange("p (j q) -> p j q", j=5) for _ in range(3)]
        bcount = [0]

        def expand_bias(ty, h, dst, dkey):
            s = bcount[0] % 2
            bcount[0] += 1
            dma("sp", bstage[s], BIAS_d[ty, h].rearrange("j p q -> p j q"), r=[("BIAS", ty, h)], w=[f"bstage{s}"])
            tt("dve", bstage[s], bstage[s], cmask.unsqueeze(1).to_broadcast([128, 5, 128]), ALU.add,
               r=[f"bstage{s}", "cmask"], w=[f"bstage{s}"])
            act(dst, bstage[s], AF.Exp, r=[f"bstage{s}"], w=[dkey])

        for h in range(32):
            expand_bias(0, h, bias_int[:, h, :, :], "bias_int")
        kring = [A.bf16(1024) for _ in range(8)]
        vring = [A.bf16(1056) for _ in range(8)]
        kctx = [A.bf16(1024) for _ in range(2)]
        vctx = [A.bf16(1056) for _ in range(2)]
        for c in range(2):
            dma("sp", kctx[c], KT_d[c], r=[("KT", c)], w=[f"kctx{c}"])
            dma("sp", vctx[c], VN_d[c], r=[("VN", c)], w=[f"vctx{c}"])
        qTt = [A.bf16(1024) for _ in range(2)]
        qm = [A.bf16(8 * 4 * 128).rearrange("p (c h q) -> p c h q", c=8, h=4) for _ in range(2)]
        hmask_f = A.f32(4)
        hmask = A.bf16(4)
        memset("pool", hmask_f, 1.0, w=["hmask_f"])
        P.add("pool", lambda e: e.affine_select(out=hmask_f, in_=hmask_f, pattern=[[-32, 4]], compare_op=ALU.is_ge,
                                                fill=0.0, base=0, channel_multiplier=1), reads=["hmask_f"], writes=["hmask_f"])
        P.add("pool", lambda e: e.affine_select(out=hmask_f, in_=hmask_f, pattern=[[32, 4]], compare_op=ALU.is_ge,
                                                fill=0.0, base=31, channel_multiplier=-1), reads=["hmask_f"], writes=["hmask_f"])
        cp("dve", hmask, hmask_f, r=["hmask_f"], w=["hmask"])
        xt = [A.f32(D) for _ in range(2)]
        pT = [A.bf16(896) for _ in range(3)]
        rden = A.f32(32)
        ob = A.bf16(1024)
        oT = A.bf16(1024)
        ln = alloc_ln()
        loaded = set()
        qtiles = ([] if last else [0, 1]) + list(range(2, NT))
        cur_ver_ = [None]
        PVREG = ((6, 0, 0, 15), (7, 0, 15, 15), (1, 384, 30, 2))

        def pv_dst(h):
            if h < 15:
                return bank(6)[:, 33 * h:33 * h + 33], [("ps", 6)]
            if h < 30:
                return bank(7)[:, 33 * (h - 15):33 * (h - 15) + 33], [("ps", 7)]
            o_ = 384 + 33 * (h - 30)
            return bank(1)[:, o_:o_ + 33], ["pvtail", ("ps", 1)]

        def tail_norm(n):
            for (pvb, co, h0, nh) in PVREG:
                rk = ["pvtail"] if pvb == 1 else [("ps", pvb)]
                view = bank(pvb)[:, co:co + nh * 33].rearrange("p (h x) -> p h x", x=33)
                P.add("dve", lambda e, view=view, h0=h0, nh=nh: e.reciprocal(out=rden[:, h0:h0 + nh].unsqueeze(2), in_=view[:, :, 32:33]),
                      reads=rk, writes=["rden"])
                tt("dve", ob[:, h0 * 32:(h0 + nh) * 32].rearrange("p (h x) -> p h x", x=32), view[:, :, 0:32],
                   rden[:, h0:h0 + nh].unsqueeze(2).to_broadcast([128, nh, 32]), ALU.mult, r=rk + ["rden"], w=["ob"])

        def tail_proj(n):
            t = qtiles[n]
            ver = 1 if t < 2 else 0
            if ver != cur_ver_[0]:
                load_ln_tabs(ln, l, 0, ver)
                cur_ver_[0] = ver
            for c in range(8):
                tr(bank_bf(6)[:, c * 128:(c + 1) * 128], ob[:, c * 128:(c + 1) * 128], ident, r=["ob"], w=[("ps", 6)])
            act(oT, bank_bf(6), AF.Copy, r=[("ps", 6)], w=["oT"])
            for g in range(2):
                for c in range(8):
                    mm(bank(6 + g), lhsT=oT[:, c * 128:(c + 1) * 128], rhs=won[:, c, g * 512:(g + 1) * 512], start=c == 0, stop=c == 7,
                       r=["oT", "won"], w=[("ps", 6 + g)])
            return res_ln_a(ln, 6, 7)

        def tail_ln(n, rs):
            t = qtiles[n]
            res_ln_b(ln, rs, xt[n % 2], f"xt{n % 2}", XM_d[t * 128:(t + 1) * 128, :], ("XM", t))

        gcount = 0

        def tile_params(t):
            if t < 2:
                return 0, 0, 0
            T = t - 2
            ty_ = 1 if T == 0 else 2 if T == 1 else 3 if T == 30 else 4 if T == 31 else 0
            return 5, ty_, min(max(T - 2, 0), 27)

        def prefetch(n):
            t = qtiles[n]
            s_ = n % 2
            dma("sp", qTt[s_], QT_d[t], r=[("QT", t)], w=[f"qTt{s_}"])
            dma("sp", xt[s_], src_d[t * 128:(t + 1) * 128, :], r=[(src_name, t)], w=[f"xt{s_}"])
            tt("dve", qm[s_], qTt[s_].rearrange("p (c q) -> p c q", c=8).unsqueeze(2).to_broadcast([128, 8, 4, 128]),
               hmask.unsqueeze(1).unsqueeze(3).to_broadcast([128, 8, 4, 128]), ALU.mult, r=[f"qTt{s_}", "hmask"], w=[f"qm{s_}"])
            nloc_, _, wt_ = tile_params(t)
            if nloc_:
                for kt in range(wt_, wt_ + 5):
                    if kt not in loaded:
                        loaded.add(kt)
                        dma("sp", kring[kt % 8], KT_d[kt + 2], r=[("KT", kt + 2)], w=[f"kring{kt % 8}"])
                        dma("sp", vring[kt % 8], VN_d[kt + 2], r=[("VN", kt + 2)], w=[f"vring{kt % 8}"])

        prefetch(0)
        for n, t in enumerate(qtiles):
            s = n % 2
            is_ctx = t < 2
            nloc, ty, wt = tile_params(t)
            nch = nloc + 2
            g0 = gcount
            gcount += 32

            def emit_S(h):
                hp = (g0 + h) % 3
                hc, pb = h // 4, 32 * (h % 4)
                sb0 = 2 * hp
                psS = bank(sb0, 2)
                skeys = [("ps", sb0), ("ps", sb0 + 1)]
                if nloc and ty != 0:
                    expand_bias(ty, h, bias_b[hp], f"bias_b{hp}")
                for jj in range(nch):
                    if jj < nloc:
                        kt = wt + jj
                        ksrc, kkey = kring[kt % 8], f"kring{kt % 8}"
                    else:
                        ksrc, kkey = kctx[jj - nloc], f"kctx{jj - nloc}"
                    wk = [skeys[(jj * 128) // 512]]
                    mm(psS[:, jj * 128:(jj + 1) * 128], lhsT=ksrc[:, hc * 128:(hc + 1) * 128],
                       rhs=qm[s][:, hc, h % 4, :], start=True, stop=True, r=[kkey, f"qm{s}"], w=wk)
                act(pT[hp][:, 0:nch * 128], psS[:, 0:nch * 128], AF.Exp, r=skeys, w=[f"pT{hp}"])
                if nloc:
                    if ty == 0:
                        bsrc, bkey = bias_int[:, h, :, :], "bias_int"
                    else:
                        bsrc, bkey = bias_b[hp], f"bias_b{hp}"
                    pv_ = pT[hp][:, 0:640].rearrange("p (j q) -> p j q", j=5)
                    tt("dve", pv_, pv_, bsrc, ALU.mult, r=[f"pT{hp}", bkey], w=[f"pT{hp}"])

            def emit_PV(h):
                hp = (g0 + h) % 3
                dst, dkeys = pv_dst(h)
                for jj in range(nch):
                    if jj < nloc:
                        kt = wt + jj
                        vsrc, vkey = vring[kt % 8], f"vring{kt % 8}"
                    else:
                        vsrc, vkey = vctx[jj - nloc], f"vctx{jj - nloc}"
                    mm(dst, lhsT=pT[hp][:, jj * 128:(jj + 1) * 128], rhs=vsrc[:, h * 33:(h + 1) * 33],
                       start=jj == 0, stop=jj == nch - 1, r=[f"pT{hp}", vkey], w=dkeys)

            emit_S(0)
            emit_S(1)
            emit_S(2)
            rs_prev = None
            if n > 0:
                tail_norm(n - 1)
                rs_prev = tail_proj(n - 1)
            for h in range(3, 32):
                emit_PV(h - 3)
                emit_S(h)
                if h == 8 and n > 0:
                    tail_ln(n - 1, rs_prev)
                if h == 16 and n + 1 < len(qtiles):
                    prefetch(n + 1)
            emit_PV(29)
            emit_PV(30)
            emit_PV(31)
        tail_norm(len(qtiles) - 1)
        tail_ln(len(qtiles) - 1, tail_proj(len(qtiles) - 1))
        P.barrier()

    try:
        phase_mods()
        src_d, src_name = xs_d, "xs"
        for l in range(n_layers):
            last = l == DEPTH - 1
            if l % 2 == 0:
                phase_ret(l, l // 2, src_d, src_name)
            else:
                phase_nat(l, l // 2, src_d, src_name, last)
            ck('mixer%d' % l)
            nxt_nat = (l + 1 < n_layers) and ((l + 1) % 2 == 1)
            phase_mlp(l, XM_d, "XM", XN_d, "XN", last, extra=bias_expansion_thunks((l + 1) // 2) if nxt_nat else ())
            src_d, src_name = XN_d, "XN"
    except _Stop:
        P.barrier()
    if dbg:
        A.off = persist_off
        for name, shape in dbg.items():
            src = {"XM": XM_d, "XN": XN_d}[name]
            buf = A.f32(D)
            for t in range(shape[0] // 128):
                dma("sp", buf, src[t * 128:(t + 1) * 128, :], r=[(name, t)], w=["dbgbuf"])
                dma("sp", dbg_outs[name][t * 128:(t + 1) * 128, :], buf, r=["dbgbuf"], w=[("dbg", name, t)])
    nsig = P.emit()
    es.close()
    return nc, {s: len(P.ops[s]) for s in STREAMS}, nsig, A.peak


def _rope_tables():
    f = np.arange(64, dtype=np.float32)
    inv = (1.0 / (np.float32(10000.0) ** (np.arange(0, 128, 2, dtype=np.float32) / np.float32(128)))).astype(np.float32)
    t = np.arange(SEQ)
    row = (t // 64).astype(np.float32)
    col = (t % 64).astype(np.float32)
    tab = np.zeros((128, 4, SEQ), np.float32)
    for ty, posv in enumerate((row, col)):
        ang = (posv[None, :] * inv[:, None]).astype(np.float32)
        c = np.cos(ang).astype(np.float32)
        s = np.sin(ang).astype(np.float32)
        tab[0:64, ty, :] = c
        tab[64:128, ty, :] = c
        tab[0:64, 2 + ty, :] = -s
        tab[64:128, 2 + ty, :] = s
    return tab


def _colmask():
    m = np.full((128, 128), NEG, np.float32)
    for qc in range(64):
        cs = min(max(qc - 8, 0), 48)
        for i in range(2):
            for jq in range(2):
                m[i * 64 + cs:i * 64 + cs + 16, jq * 64 + qc] = 0.0
    return m


_VAR = ((-4, 3), (0, 7), (-1, 6), (-2, 5), (-3, 4), (-5, 2), (-6, 1), (-7, 0))


def _pad_tables(rpb):
    pad = np.full((2, 8, 32, 20, 127), NEG, np.float32)
    rev = rpb[:, :, :, ::-1]
    for v, (lo, hi) in enumerate(_VAR):
        for dr in range(lo, hi + 1):
            pad[:, v, :, dr + 10, 48:79] = rev[:, :, dr + 7, :]
    return pad


_CACHE = {}


def kernel(x, c, ctx, c_ctx, ada_w, ada_b, ret_w_in, ret_w_o, ret_decay, nat_w_in, nat_w_o, nat_rpb,
           mlp_w1, mlp_w2, ln_g, ln_b):
    f = lambda a: np.ascontiguousarray(np.asarray(a, dtype=np.float32))
    x, c, ctx, c_ctx = f(x), f(c), f(ctx), f(c_ctx)
    if "nc" not in _CACHE:
        _CACHE["nc"] = build()[0]
    nc = _CACHE["nc"]
    shared = {
        "ada_w": f(ada_w), "ada_b": f(ada_b),
        "ada_bT": f(np.asarray(ada_b).reshape(DEPTH, 48, 128).transpose(0, 2, 1)),
        "ret_w_in": f(ret_w_in), "ret_w_o": f(ret_w_o), "ret_decay": f(np.asarray(ret_decay).reshape(2, 8)),
        "nat_w_in": f(nat_w_in), "nat_w_o": f(nat_w_o),
        "pad": _pad_tables(f(nat_rpb)), "colmask": _colmask(), "rope": _rope_tables(),
        "mlp_w1": f(mlp_w1), "mlp_w2": f(mlp_w2), "ln_g": f(ln_g), "ln_b": f(ln_b),
    }
    in_maps = []
    for b in range(8):
        m = dict(shared)
        m["xs"] = np.ascontiguousarray(np.concatenate([ctx[b], x[b]], axis=0))
        cv = np.stack([c[b], c_ctx], axis=1)
        m["cvT"] = np.ascontiguousarray(cv.reshape(8, 128, 2).transpose(1, 0, 2))
        in_maps.append(m)
    res = run_bass_kernel_spmd(nc, in_maps, core_ids=list(range(8)))
    return np.stack([np.asarray(r["y"], dtype=np.float32) for r in res.results], axis=0)
```

```python
import contextlib
import numpy as np
import concourse.bass as bass
import concourse.mybir as mybir
from concourse.bass_utils import run_bass_kernel_spmd

F32 = mybir.dt.float32
BF16 = mybir.dt.bfloat16
I32 = mybir.dt.int32
AF = mybir.ActivationFunctionType
ALU = mybir.AluOpType

D = 1024
SEQ = 4096
CTX = 256
TOK = SEQ + CTX
NT = TOK // 128
DEPTH = 4
ALPHA = (2.0 * DEPTH) ** 0.25
EPS = 1e-5
NEG = -30000.0
NAT_SCALE = 32 ** -0.5

EPOCH = 3000
STREAMS = ("pe", "act", "dve", "pool", "sp")


class Op:
    __slots__ = ("stream", "fn", "is_dma", "deps", "signal", "cnt", "dsem", "dval")

    def __init__(self, stream, fn, is_dma):
        self.stream = stream
        self.fn = fn
        self.is_dma = is_dma
        self.deps = []
        self.signal = False
        self.cnt = None
        self.dsem = None
        self.dval = None


class Prog:
    def __init__(self, nc, n_dma_sems=16):
        self.nc = nc
        self.ops = {s: [] for s in STREAMS}
        self.last_w = {}
        self.readers = {}
        self.n_dma_sems = n_dma_sems
        self.dma_rr = {s: 0 for s in STREAMS}
        self.dma_cnt = {}
        self.dma_last = {}
        self.pending = {s: [] for s in STREAMS}

    def _need(self, op, dep):
        if dep is None or dep is op:
            return
        if not dep.is_dma:
            dep.signal = True
        op.deps.append(dep)

    def barrier(self):
        deps = []
        for s in STREAMS:
            for op in reversed(self.ops[s]):
                if not op.is_dma:
                    deps.append(op)
                    break
        deps.extend(self.dma_last.values())
        for s in STREAMS:
            self.pending[s] = list(deps)

    def add(self, stream, fn, reads=(), writes=(), dma=False):
        op = Op(stream, fn, dma)
        if self.pending[stream]:
            for dep in self.pending[stream]:
                self._need(op, dep)
            self.pending[stream] = []
        if dma:
            slot = self.dma_rr[stream]
            self.dma_rr[stream] = (slot + 1) % self.n_dma_sems
            key = (stream, slot)
            prev = self.dma_last.get(key)
            if prev is not None:
                self._need(op, prev)
            op.dsem = key
            op.dval = self.dma_cnt.get(key, 0) + 16
            self.dma_cnt[key] = op.dval
            self.dma_last[key] = op
        for r in reads:
            w = self.last_w.get(r)
            if w is not None:
                if w.stream == stream and stream == "pe" and not w.is_dma and not dma:
                    pass
                else:
                    self._need(op, w)
        for wk in writes:
            w = self.last_w.get(wk)
            if w is not None and not (w.stream == stream and not w.is_dma and not dma):
                self._need(op, w)
            for rd in self.readers.get(wk, ()):
                if rd.stream == stream and not rd.is_dma and not dma:
                    continue
                self._need(op, rd)
        for r in reads:
            lst = self.readers.setdefault(r, [])
            if not dma:
                for i_, o_ in enumerate(lst):
                    if o_.stream == stream and not o_.is_dma:
                        lst[i_] = op
                        break
                else:
                    lst.append(op)
            else:
                lst.append(op)
        for wk in writes:
            self.last_w[wk] = op
            self.readers[wk] = []
        self.ops[stream].append(op)
        return op

    def emit(self):
        nc = self.nc
        nsig = {}
        for s in STREAMS:
            c = 0
            for op in self.ops[s]:
                if op.signal and not op.is_dma:
                    op.cnt = c
                    c += 1
            nsig[s] = c
        with contextlib.ExitStack() as es:
            esem = {}
            for s in STREAMS:
                n = (nsig[s] + EPOCH - 1) // EPOCH
                esem[s] = [es.enter_context(nc.semaphore(f"e_{s}_{i}")) for i in range(max(n, 1))]
            dsem = {}
            for key in self.dma_cnt:
                dsem[key] = es.enter_context(nc.semaphore(f"d_{key[0]}_{key[1]}"))
            block = es.enter_context(nc.Block())

            def run(stream, eng):
                waited = {}
                for op in self.ops[stream]:
                    for dep in op.deps:
                        if dep.is_dma:
                            k = ("d", dep.dsem)
                            v = dep.dval
                            if waited.get(k, 0) >= v:
                                continue
                            waited[k] = v
                            eng.wait_ge(dsem[dep.dsem], v)
                        else:
                            k = ("e", dep.stream)
                            v = dep.cnt
                            if waited.get(k, -1) >= v:
                                continue
                            waited[k] = v
                            eng.wait_ge(esem[dep.stream][v // EPOCH], v % EPOCH + 1)
                    ins = op.fn(eng)
                    if op.is_dma:
                        ins.then_inc(dsem[op.dsem], 16)
                    elif op.signal:
                        ins.then_inc(esem[stream][op.cnt // EPOCH], 1)
                for key, val in self.dma_cnt.items():
                    if key[0] == stream:
                        eng.wait_ge(dsem[key], val)

            @block.tensor
            def _(e):
                run("pe", e)

            @block.scalar
            def _(e):
                run("act", e)

            @block.vector
            def _(e):
                run("dve", e)

            @block.gpsimd
            def _(e):
                run("pool", e)

            @block.sync
            def _(e):
                run("sp", e)
        return nsig


class Arena:
    def __init__(self, t, words):
        self.t = t
        self.words = words
        self.off = 0
        self.peak = 0

    def f32(self, n, dt=F32):
        assert self.off + n <= self.words, ("arena overflow", self.off, n, self.words)
        ap = self.t[:, self.off:self.off + n]
        self.off += (n + 15) // 16 * 16
        self.peak = max(self.peak, self.off)
        return ap if dt == F32 else ap.bitcast(dt)

    def bf16(self, n):
        assert n % 2 == 0
        return self.f32(n // 2).bitcast(BF16)


ARENA_WORDS = 196 * 256
NSTAGE = 4


class _Stop(Exception):
    pass


def build(n_layers=DEPTH, dbg=None, stop=None):
    nc = bass.Bass("TRN2", target_bir_lowering=False)

    def din(name, shape, dt=F32):
        return nc.dram_tensor(name, list(shape), dt, kind="ExternalInput").ap()

    def dscr(name, shape, dt=F32):
        return nc.dram_tensor(name, list(shape), dt).ap()

    xs_d = din("xs", [TOK, D])
    cvT_d = din("cvT", [128, 8, 2])
    ada_w_d = din("ada_w", [DEPTH, D, 6 * D])
    ada_b_d = din("ada_b", [DEPTH, 6 * D])
    ada_bT_d = din("ada_bT", [DEPTH, 128, 48])
    ret_w_in_d = din("ret_w_in", [2, D, 6 * D])
    ret_w_o_d = din("ret_w_o", [2, 2 * D, D])
    ret_decay_d = din("ret_decay", [2, 8])
    nat_w_in_d = din("nat_w_in", [2, D, 3 * D])
    nat_w_o_d = din("nat_w_o", [2, D, D])
    pad_d = din("pad", [2, 8, 32, 20, 127])
    colmask_d = din("colmask", [128, 128])
    rope_d = din("rope", [128, 4, SEQ])
    mlp_w1_d = din("mlp_w1", [DEPTH, D, 4 * D])
    mlp_w2_d = din("mlp_w2", [DEPTH, 4 * D, D])
    ln_g_d = din("ln_g", [DEPTH, 2, D])
    ln_b_d = din("ln_b", [DEPTH, 2, D])
    y_d = nc.dram_tensor("y", [SEQ, D], F32, kind="ExternalOutput").ap()

    XM_d = dscr("XM", [TOK, D])
    XN_d = dscr("XN", [TOK, D])
    MODR_d = dscr("MODR", [DEPTH, 2, 2, D])
    QT_d = dscr("QT", [NT, 128, 1024], BF16)
    KT_d = dscr("KT", [NT, 128, 1024], BF16)
    KD_d = dscr("KD", [2, NT, 128, 1024], BF16)
    V_d = dscr("V", [NT, 128, 2048], BF16)
    SG_d = dscr("SG", [NT, 128, 2048], BF16)
    OB_d = dscr("OB", [NT, 128, 2048], F32)
    VN_d = dscr("VN", [NT, 128, 1056], BF16)
    BIAS_d = dscr("BIAS", [5, 32, 5, 128, 128])
    OA_d = dscr("OA", [NT, 128, 1024], BF16)
    dbg_outs = {}
    if dbg:
        for name, shape in dbg.items():
            dbg_outs[name] = nc.dram_tensor("dbg_" + name, list(shape), F32, kind="ExternalOutput").ap()

    P = Prog(nc)

    def ck(name):
        if stop == name:
            raise _Stop()

    es = contextlib.ExitStack()
    arena_t = es.enter_context(nc.sbuf_tensor("arena", [128, ARENA_WORDS], F32))
    ps_t = es.enter_context(nc.psum_tensor("ps", [128, 4096], F32))
    A = Arena(arena_t, ARENA_WORDS)

    def bank(b, n=1):
        return ps_t[:, b * 512:(b + n) * 512]

    def bank_bf(b, n=1):
        return ps_t[:, b * 512:(b + n) * 512].bitcast(BF16)

    def dma(q, out, in_, r=(), w=()):
        P.add(q, lambda e: e.dma_start(out=out, in_=in_), reads=r, writes=w, dma=True)

    def mm(out, lhsT, rhs, start, stop, r, w, tp=None):
        if tp is None:
            P.add("pe", lambda e: e.matmul(out, lhsT=lhsT, rhs=rhs, start=start, stop=stop), reads=r, writes=w)
        else:
            P.add("pe", lambda e: e.matmul(out, lhsT=lhsT, rhs=rhs, start=start, stop=stop, tile_position=tp), reads=r, writes=w)

    def tr(out, in_, ident, r, w):
        P.add("pe", lambda e: e.transpose(out=out, in_=in_, identity=ident), reads=list(r) + ["ident"], writes=w)

    def act(out, in_, func, r, w, scale=None, bias=None):
        kw = {}
        if scale is not None:
            kw["scale"] = scale
        if bias is not None:
            kw["bias"] = bias
        P.add("act", lambda e: e.activation(out=out, in_=in_, func=func, **kw), reads=r, writes=w)

    def tt(eng, out, in0, in1, op, r, w):
        P.add(eng, lambda e: e.tensor_tensor(out=out, in0=in0, in1=in1, op=op), reads=r, writes=w)

    def ts(eng, out, in0, s1, op0, r, w, s2=None, op1=None):
        if op1 is None:
            P.add(eng, lambda e: e.tensor_scalar(out=out, in0=in0, scalar1=s1, scalar2=None, op0=op0), reads=r, writes=w)
        else:
            P.add(eng, lambda e: e.tensor_scalar(out=out, in0=in0, scalar1=s1, scalar2=s2, op0=op0, op1=op1), reads=r, writes=w)

    def stt(eng, out, in0, scalar, in1, op0, op1, r, w):
        P.add(eng, lambda e: e.scalar_tensor_tensor(out=out, in0=in0, scalar=scalar, in1=in1, op0=op0, op1=op1), reads=r, writes=w)

    def cp(eng, out, in_, r, w):
        P.add(eng, lambda e: e.tensor_copy(out=out, in_=in_), reads=r, writes=w)

    def memset(eng, out, val, w, r=()):
        P.add(eng, lambda e: e.memset(out, val), reads=r, writes=w)

    identf = A.f32(128)
    ident = A.bf16(128)
    memset("pool", identf, 0.0, w=["identf"])
    P.add("pool", lambda e: e.affine_select(out=identf, in_=identf, pattern=[[-1, 128]], compare_op=ALU.not_equal,
                                            fill=1.0, base=0, channel_multiplier=1), reads=["identf"], writes=["identf"])
    cp("dve", ident, identf, r=["identf"], w=["ident"])
    epsc = A.f32(1)
    memset("pool", epsc, EPS, w=["epsc"])
    modT = A.f32(DEPTH * 2 * 4 * 8).rearrange("p (l v s c) -> p l v s c", l=DEPTH, v=2, s=4)
    persist_off = A.off

    def phase_mods():
        A.off = persist_off
        cv = A.f32(16)
        scT = A.f32(16)
        sig = A.f32(16)
        dma("sp", cv, cvT_d.rearrange("p c r -> p (c r)"), w=["cv"])
        act(sig, cv, AF.Sigmoid, r=["cv"], w=["sig"])
        tt("dve", scT, cv, sig, ALU.mult, r=["cv", "sig"], w=["scT"])
        scT3 = scT.rearrange("p (c r) -> p c r", r=2)
        wb = [A.f32(8 * 512) for _ in range(3)]
        abT = A.f32(DEPTH * 48)
        dma("sp", abT.rearrange("p (l c) -> p l c", l=DEPTH), ada_bT_d.rearrange("l p c -> p l c"), w=["abT"])
        abR = A.f32(DEPTH * 2 * D)
        rowo = A.f32(DEPTH * 2 * D)
        for l in range(n_layers):
            for wi, sec in enumerate((2, 5)):
                dma("sp", abR[0:2, (l * 2 + wi) * D:(l * 2 + wi + 1) * D],
                    ada_b_d[l:l + 1, sec * D:(sec + 1) * D].partition_broadcast(2), w=["abR"])
        it = 0
        for l in range(n_layers):
            for grp in range(12):
                s = it % 3
                pb_ = it % 2
                it += 1
                wv = wb[s].rearrange("p (c n) -> p c n", c=8)
                dma("sp" if it % 2 else "act", wv, ada_w_d[l].rearrange("(c p) n -> p c n", p=128)[:, :, grp * 512:(grp + 1) * 512],
                    w=[f"wb{s}"])
                sec = grp // 2
                if sec in (2, 5):
                    wi = 0 if sec == 2 else 1
                    half = grp % 2
                    for kc in range(8):
                        mm(bank(2 + pb_)[0:2, :], lhsT=scT3[:, kc, :], rhs=wv[:, kc, :], start=kc == 0, stop=kc == 7,
                           r=["scT", f"wb{s}"], w=[("ps", 2 + pb_)])
                    o = (l * 2 + wi) * D + half * 512
                    tt("dve", rowo[0:2, o:o + 512], bank(2 + pb_)[0:2, :], abR[0:2, o:o + 512], ALU.add,
                       r=[("ps", 2 + pb_), "abR"], w=["rowo"])
                else:
                    si = {0: 0, 1: 1, 3: 2, 4: 3}[sec]
                    for cc in range(4):
                        for kc in range(8):
                            mm(bank(pb_)[:, cc * 2:cc * 2 + 2], lhsT=wv[:, kc, cc * 128:(cc + 1) * 128], rhs=scT3[:, kc, :],
                               start=kc == 0, stop=kc == 7, r=["scT", f"wb{s}"], w=[("ps", pb_)])
                    for v in range(2):
                        c0 = (grp % 2) * 4
                        ab = abT[:, l * 48 + grp * 4:l * 48 + grp * 4 + 4]
                        src = bank(pb_)[:, 0:8].rearrange("p (c v) -> p c v", v=2)[:, :, v]
                        tt("dve", modT[:, l, v, si, c0:c0 + 4], src, ab, ALU.add, r=[("ps", pb_), "abT"], w=["modT"])
        for l in range(n_layers):
            for v in range(2):
                for si in (1, 3):
                    ts("dve", modT[:, l, v, si, :], modT[:, l, v, si, :], 1.0, ALU.add, r=["modT"], w=["modT"])
        for l in range(n_layers):
            for wi in range(2):
                o = (l * 2 + wi) * D
                dma("pool", MODR_d[l, :, wi, :], rowo[0:2, o:o + D], r=["rowo"], w=["MODR"])
        P.barrier()
        ck('mods')

    def load_row_bc(dst, src_row, w, r=()):
        dma("sp", dst, src_row.partition_broadcast(128), r=r, w=w)

    def make_hT(xt, xkey, hT_view, hkey, l, ver, si_sh, si_sc, tb):
        pt = bank(tb, 2)
        for c in range(8):
            tr(pt[:, c * 128:(c + 1) * 128], xt[:, c * 128:(c + 1) * 128], identf, r=[xkey], w=[("ps", tb), ("ps", tb + 1)])
        for c in range(8):
            act(hT_view[:, c, :], pt[:, c * 128:(c + 1) * 128], AF.Identity, r=[("ps", tb), ("ps", tb + 1), "modT"], w=[hkey],
                scale=modT[:, l, ver, si_sc, c:c + 1], bias=modT[:, l, ver, si_sh, c:c + 1])

    class LNctx:
        pass

    def alloc_ln(nbuf=2):
        c = LNctx()
        c.gtab = A.f32(D)
        c.lng = A.f32(D)
        c.lnb = A.f32(D)
        c.r = [A.f32(D) for _ in range(nbuf)]
        c.st = [A.f32(12) for _ in range(nbuf)]
        c.mv = [A.f32(4) for _ in range(nbuf)]
        c.n = 0
        c.nbuf = nbuf
        return c

    def load_ln_tabs(c, l, which, ver):
        load_row_bc(c.gtab, MODR_d[l, ver, which:which + 1, :], w=["gtab"], r=["MODR"])
        load_row_bc(c.lng, ln_g_d[l, which:which + 1, :], w=["lng"])
        load_row_bc(c.lnb, ln_b_d[l, which:which + 1, :], w=["lnb"])

    def res_ln_a(c, yb0, yb1):
        s = c.n % c.nbuf
        c.n += 1
        r = c.r[s]
        rk = f"lnr{s}"
        for g, yb in enumerate((yb0, yb1)):
            tt("dve", r[:, g * 512:(g + 1) * 512], bank(yb), c.gtab[:, g * 512:(g + 1) * 512], ALU.mult,
               r=[("ps", yb), "gtab"], w=[rk])
        return s

    def res_ln_b(c, s, xt, xkey, dst_d, dst_key):
        r = c.r[s]
        rk = f"lnr{s}"
        stt("dve", r, xt, ALPHA, r, ALU.mult, ALU.add, r=[xkey, rk], w=[rk])
        st = c.st[s].rearrange("p (a b) -> p a b", a=2)
        for g in range(2):
            P.add("dve", lambda e, g=g: e.bn_stats(out=st[:, g, :], in_=r[:, g * 512:(g + 1) * 512]), reads=[rk], writes=[f"lnst{s}"])
        mv = c.mv[s]
        P.add("dve", lambda e: e.bn_aggr(out=mv[:, 0:2], in_=st), reads=[f"lnst{s}"], writes=[f"lnmv{s}"])
        act(mv[:, 2:3], mv[:, 1:2], AF.Ln, r=[f"lnmv{s}"], w=[f"lnmv{s}"], bias=epsc)
        act(mv[:, 2:3], mv[:, 2:3], AF.Exp, r=[f"lnmv{s}"], w=[f"lnmv{s}"], scale=-0.5)
        ts("dve", mv[:, 3:4], mv[:, 0:1], mv[:, 2:3], ALU.mult, r=[f"lnmv{s}"], w=[f"lnmv{s}"], s2=-1.0, op1=ALU.mult)
        act(r, r, AF.Identity, r=[rk, f"lnmv{s}"], w=[rk], scale=mv[:, 2:3], bias=mv[:, 3:4])
        tt("dve", r, r, c.lng, ALU.mult, r=[rk, "lng"], w=[rk])
        tt("pool", r, r, c.lnb, ALU.add, r=[rk, "lnb"], w=[rk])
        dma("pool", dst_d, r, r=[rk], w=[dst_key])

    def res_ln(c, yb0, yb1, xt, xkey, dst_d, dst_key):
        s = res_ln_a(c, yb0, yb1)
        res_ln_b(c, s, xt, xkey, dst_d, dst_key)

    cast_rr = [0]

    def cast(out, in_, r, w, scale=None):
        i = cast_rr[0] % 2
        cast_rr[0] += 1
        if i == 0:
            if scale is None:
                act(out, in_, AF.Copy, r, w)
            else:
                act(out, in_, AF.Identity, r, w, scale=scale)
        else:
            if scale is None:
                cp("dve", out, in_, r, w)
            else:
                ts("dve", out, in_, scale, ALU.mult, r, w)

    def load_w(dst3, src2, nchunk, key, stage, col0=0, ncol=None, kstep=1):
        ncol_ = ncol if ncol is not None else src2.shape[1]
        srcv = src2.rearrange("(c p) n -> p c n", p=128)
        for k0 in range(0, nchunk, kstep):
            s_ = (k0 // kstep) % len(stage)
            st = stage[s_][:, 0:kstep * ncol_].rearrange("p (c n) -> p c n", c=kstep)
            dma("sp" if s_ % 2 == 0 else "act", st, srcv[:, k0:k0 + kstep, col0:col0 + ncol_], w=[f"stage{s_}"])
            cast(dst3[:, k0:k0 + kstep, :], st, r=[f"stage{s_}"], w=[key])

    def bias_expansion_thunks(j):
        th = []
        for ty in range(5):
            off = (4, 0, 2, 6, 8)[ty]
            vv = ((0, 0), (1, 2), (3, 4), (0, 5), (6, 7))[ty]
            for h in range(32):
                for jq in range(2):
                    base = (((j * 8 + vv[jq]) * 32 + h) * 20 + (10 - jq - off)) * 127 + 63
                    src = bass.AP(pad_d.tensor, base, [[2 * 127, 5], [127, 2], [-1, 64], [1, 64]])
                    dst = BIAS_d[ty, h].rearrange("j (i k) (a q) -> j i k a q", i=2, a=2)[:, :, :, jq, :]
                    th.append((dst, src, ("BIAS", ty, h)))
        return th

    def phase_mlp(l, src_d, src_name, dst_d, dst_name, last, extra=()):
        A.off = persist_off
        w1 = A.bf16(8 * 4096).rearrange("p (c n) -> p c n", c=8)
        w2 = A.bf16(32 * 1024).rearrange("p (c n) -> p c n", c=32)
        o1 = A.off
        stage = [A.f32(4096) for _ in range(NSTAGE)]
        load_w(w1, mlp_w1_d[l], 8, "w1", stage)
        load_w(w2, mlp_w2_d[l], 32, "w2", stage, kstep=4)
        P.barrier()
        A.off = o1
        ln = alloc_ln()
        xt = [A.f32(D) for _ in range(4)]
        hT = [A.bf16(8 * 256).rearrange("p (c t) -> p c t", c=8) for _ in range(2)]
        uT = A.bf16(32 * 256)
        rl = [A.f32(512) for _ in range(2)]
        sts2 = ([] if last else [0]) + [2 + 2 * k for k in range(16)]
        cur_ver = None
        extra = list(extra)
        per = (len(extra) + len(sts2) - 1) // len(sts2) if extra else 0

        def prep(si):
            t0 = sts2[si]
            for tl in range(2):
                t = t0 + tl
                xs_ = (2 * si + tl) % 4
                dma("sp", xt[xs_], src_d[t * 128:(t + 1) * 128, :], r=[(src_name, t)], w=[f"xt{xs_}"])
                make_hT(xt[xs_], f"xt{xs_}", hT[si % 2][:, :, tl * 128:(tl + 1) * 128], f"hT{si % 2}", l,
                        1 if t < 2 else 0, 2, 3, 6)
            for (dst_, src_, key_) in extra[si * per:(si + 1) * per]:
                dma("sp", dst_, src_, w=[key_])

        prep(0)
        for si, t0 in enumerate(sts2):
            ver = 1 if t0 < 2 else 0
            if ver != cur_ver:
                load_ln_tabs(ln, l, 1, ver)
                cur_ver = ver
            s = si % 2
            for grp in range(16):
                pb = grp % 2
                for cc in range(2):
                    j = grp * 2 + cc
                    for kc in range(8):
                        mm(bank(pb)[:, cc * 256:(cc + 1) * 256], lhsT=w1[:, kc, j * 128:(j + 1) * 128], rhs=hT[s][:, kc, :],
                           start=kc == 0, stop=kc == 7, r=["w1", f"hT{s}"], w=[("ps", pb)])
                act(rl[pb], bank(pb), AF.Relu, r=[("ps", pb)], w=[f"rl{pb}"])
                tt("dve", uT[:, grp * 512:(grp + 1) * 512], rl[pb], rl[pb], ALU.mult, r=[f"rl{pb}"], w=["uT"])
            if si + 1 < len(sts2):
                prep(si + 1)
            for tl in range(2):
                t = t0 + tl
                xs_ = (2 * si + tl) % 4
                yb = 2 + 2 * tl
                for g in range(2):
                    for j in range(32):
                        mm(bank(yb + g), lhsT=uT[:, j * 256 + tl * 128:j * 256 + (tl + 1) * 128], rhs=w2[:, j, g * 512:(g + 1) * 512],
                           start=j == 0, stop=j == 31, r=["uT", "w2"], w=[("ps", yb + g)])
                if last:
                    res_ln(ln, yb, yb + 1, xt[xs_], f"xt{xs_}", y_d[(t - 2) * 128:(t - 1) * 128, :], ("y", t))
                else:
                    res_ln(ln, yb, yb + 1, xt[xs_], f"xt{xs_}", dst_d[t * 128:(t + 1) * 128, :], (dst_name, t))
        P.barrier()

    def phase_ret(l, j, src_d, src_name):
        A.off = persist_off
        dp = A.f32(8)
        lg = A.f32(8)
        load_row_bc(dp, ret_decay_d[j:j + 1, :], w=["dp"])
        act(lg, dp, AF.Exp, r=["dp"], w=["lg"])
        ts("dve", lg, lg, -1.0, ALU.mult, r=["lg"], w=["lg"])
        Di = A.f32(128, I32)
        Dq = A.f32(128)
        P.add("pool", lambda e: e.iota(Di, pattern=[[1, 128]], base=0, channel_multiplier=-1), writes=["Di"])
        cp("dve", Dq, Di, r=["Di"], w=["Dq"])
        pos_i = A.f32(128, I32)
        pos = A.f32(128)
        P.add("pool", lambda e: e.iota(pos_i, pattern=[[1, 128]], base=0, channel_multiplier=0), writes=["posi"])
        cp("dve", pos, pos_i, r=["posi"], w=["pos"])
        par_i = A.f32(1, I32)
        par = A.f32(1)
        P.add("pool", lambda e: e.iota(par_i, pattern=[[1, 1]], base=0, channel_multiplier=1), writes=["pari"])
        cp("dve", par, par_i, r=["pari"], w=["par"])
        tmpA = A.f32(128)
        tmpB = A.f32(128)
        maskT = A.f32(2 * 4 * 128).rearrange("p (d h q) -> p d h q", d=2, h=4)
        qdec = A.f32(2 * 4 * 128).rearrange("p (d h q) -> p d h q", d=2, h=4)
        kdec = A.f32(8).rearrange("p (d h) -> p d h", d=2)
        cdec = A.f32(8).rearrange("p (d h) -> p d h", d=2)
        for d in range(2):
            if d == 0:
                ts("dve", tmpA, Dq, 0.0, ALU.max, r=["Dq"], w=["tmpA"])
                ts("dve", tmpB, Dq, 0.0, ALU.is_ge, r=["Dq"], w=["tmpB"])
            else:
                ts("dve", tmpA, Dq, -1.0, ALU.mult, r=["Dq"], w=["tmpA"], s2=0.0, op1=ALU.max)
                ts("dve", tmpB, Dq, 0.0, ALU.is_le, r=["Dq"], w=["tmpB"])
            for h in range(4):
                sc = lg[:, d * 4 + h:d * 4 + h + 1]
                act(maskT[:, d, h, :], tmpA, AF.Exp, r=["tmpA", "lg"], w=["maskT"], scale=sc)
                tt("dve", maskT[:, d, h, :], maskT[:, d, h, :], tmpB, ALU.mult, r=["maskT", "tmpB"], w=["maskT"])
            tq = A.f32(128)
            tk = A.f32(1)
            if d == 0:
                ts("dve", tq, pos, 1.0, ALU.add, r=["pos"], w=[f"tq{d}"])
                ts("dve", tk, par, -1.0, ALU.mult, r=["par"], w=[f"tk{d}"], s2=127.0, op1=ALU.add)
            else:
                ts("dve", tq, pos, -1.0, ALU.mult, r=["pos"], w=[f"tq{d}"], s2=128.0, op1=ALU.add)
                cp("dve", tk, par, r=["par"], w=[f"tk{d}"])
            for h in range(4):
                sc = lg[:, d * 4 + h:d * 4 + h + 1]
                act(qdec[:, d, h, :], tq, AF.Exp, r=[f"tq{d}", "lg"], w=["qdec"], scale=sc)
                act(kdec[:, d, h:h + 1], tk, AF.Exp, r=[f"tk{d}", "lg"], w=["kdec"], scale=sc)
                act(cdec[:, d, h:h + 1], sc, AF.Exp, r=["lg"], w=["cdec"], scale=128.0)
        ret_off = A.off
        ck('consts')

        wqk = A.bf16(8 * 4096).rearrange("p (c n) -> p c n", c=8)
        o1 = A.off
        stage = [A.f32(2048) for _ in range(NSTAGE)]
        wsrcv = ret_w_in_d[j].rearrange("(c p) n -> p c n", p=128)
        for kc in range(8):
            s_ = kc % NSTAGE
            st = stage[s_]
            dma("sp" if s_ % 2 == 0 else "act", st, wsrcv[:, kc, 0:2048], w=[f"stage{s_}"])
            cast(wqk[:, kc, 0:1024], st[:, 0:1024], r=[f"stage{s_}"], w=["wqk"])
            cast(wqk[:, kc, 1024:2048], st[:, 1024:2048], r=[f"stage{s_}"], w=["wqk"], scale=0.0625)
            stv = st.rearrange("p (ch a b) -> p ch a b", a=2, b=64)
            dv = wqk[:, kc, 2048:4096].rearrange("p (ch a b) -> p ch a b", a=2, b=64)
            for a in range(2):
                cast(dv[:, 0:8, a, :], stv[:, 0:8, 1 - a, :], r=[f"stage{s_}"], w=["wqk"])
                cast(dv[:, 8:16, a, :], stv[:, 8:16, 1 - a, :], r=[f"stage{s_}"], w=["wqk"], scale=0.0625)
        P.barrier()
        A.off = o1
        xt = [A.f32(D) for _ in range(2)]
        ropet = [A.f32(4 * 512).rearrange("p (a t) -> p a t", a=4) for _ in range(2)]
        hT = [A.bf16(8 * 512).rearrange("p (c t) -> p c t", c=8) for _ in range(2)]
        qTo = [A.bf16(8 * 512).rearrange("p (c t) -> p c t", c=8) for _ in range(2)]
        kTo = [A.bf16(8 * 512).rearrange("p (c t) -> p c t", c=8) for _ in range(2)]
        t1 = [A.f32(512) for _ in range(2)]
        t2 = [A.f32(512) for _ in range(2)]
        kd = [[A.bf16(1024) for _ in range(2)] for _ in range(2)]
        sts = [(0, 2)] + [(2 + 4 * k, 4) for k in range(8)]
        xi = 0
        ci = 0
        ki = 0
        xi_ = [0]

        def prep_tile(si, tl):
            t0_, ntl_ = sts[si]
            s_ = si % 2
            lat_ = t0_ >= 2
            if tl == 0 and lat_:
                dma("sp", ropet[s_], rope_d[:, :, (t0_ - 2) * 128:(t0_ - 2) * 128 + 512], w=[f"rope{s_}"])
            t = t0_ + tl
            xs_ = xi_[0] % 2
            xi_[0] += 1
            dma("sp", xt[xs_], src_d[t * 128:(t + 1) * 128, :], r=[(src_name, t)], w=[f"xt{xs_}"])
            make_hT(xt[xs_], f"xt{xs_}", hT[s_][:, :, tl * 128:(tl + 1) * 128], f"hT{s_}", l, 0 if lat_ else 1, 0, 1, 6)

        for tl in range(sts[0][1]):
            prep_tile(0, tl)
        for si, (t0, ntl) in enumerate(sts):
            s = si % 2
            lat = t0 >= 2
            ver = 0 if lat else 1
            ntok = ntl * 128
            for c in range(16):
                cs = ci % 2
                ci += 1
                dst = (qTo if c < 8 else kTo)[s]
                dkey = (f"qTo{s}" if c < 8 else f"kTo{s}")
                pa, pb = 2 * cs, 2 * cs + 1
                for kc in range(8):
                    mm(bank(pa)[:, 0:ntok], lhsT=wqk[:, kc, c * 128:(c + 1) * 128], rhs=hT[s][:, kc, 0:ntok],
                       start=kc == 0, stop=kc == 7, r=["wqk", f"hT{s}"], w=[("ps", pa)])
                if lat:
                    for kc in range(8):
                        mm(bank(pb)[:, 0:ntok], lhsT=wqk[:, kc, 2048 + c * 128:2048 + (c + 1) * 128], rhs=hT[s][:, kc, 0:ntok],
                           start=kc == 0, stop=kc == 7, r=["wqk", f"hT{s}"], w=[("ps", pb)])
                    ty = c % 2
                    tt("dve", t1[cs], bank(pa), ropet[s][:, ty, :], ALU.mult, r=[("ps", pa), f"rope{s}"], w=[f"t1{cs}"])
                    tt("dve", t2[cs], bank(pb), ropet[s][:, 2 + ty, :], ALU.mult, r=[("ps", pb), f"rope{s}"], w=[f"t2{cs}"])
                    tt("pool", dst[:, c % 8, :], t1[cs], t2[cs], ALU.add, r=[f"t1{cs}", f"t2{cs}"], w=[dkey])
                else:
                    act(dst[:, c % 8, 0:ntok], bank(pa)[:, 0:ntok], AF.Copy, r=[("ps", pa)], w=[dkey])
                if si + 1 < len(sts) and c % 4 == 3 and (c // 4) < sts[si + 1][1]:
                    prep_tile(si + 1, c // 4)
            for tl in range(ntl):
                t = t0 + tl
                dma("pool", QT_d[t].rearrange("p (c t) -> p c t", c=8), qTo[s][:, :, tl * 128:(tl + 1) * 128], r=[f"qTo{s}"], w=[("QT", t)])
                dma("pool", KT_d[t].rearrange("p (c t) -> p c t", c=8), kTo[s][:, :, tl * 128:(tl + 1) * 128], r=[f"kTo{s}"], w=[("KT", t)])
                ks = ki % 2
                ki += 1
                for c in range(8):
                    tr(bank_bf(4 + ks)[:, c * 128:(c + 1) * 128], kTo[s][:, c, tl * 128:(tl + 1) * 128], ident,
                       r=[f"kTo{s}"], w=[("ps", 4 + ks)])
                for d in range(2):
                    tt("dve", kd[ks][d].rearrange("p (h x) -> p h x", h=4), bank_bf(4 + ks).rearrange("p (h x) -> p h x", h=4),
                       kdec[:, d, :].unsqueeze(2).to_broadcast([128, 4, 256]), ALU.mult,
                       r=[("ps", 4 + ks), "kdec"], w=[f"kd{ks}{d}"])
                    dma("pool", KD_d[d, t], kd[ks][d], r=[f"kd{ks}{d}"], w=[("KD", d, t)])
        P.barrier()
        ck('A1')

        A.off = ret_off
        wvg = A.bf16(8 * 4096).rearrange("p (c n) -> p c n", c=8)
        o1 = A.off
        stage = [A.f32(4096) for _ in range(NSTAGE)]
        load_w(wvg, ret_w_in_d[j], 8, "wvg", stage, col0=2048, ncol=4096)
        P.barrier()
        A.off = o1
        xt = [A.f32(D) for _ in range(2)]
        hT = [A.bf16(1024).rearrange("p (c t) -> p c t", c=8) for _ in range(2)]
        vo = [A.bf16(2048) for _ in range(2)]
        sgo = [A.bf16(2048) for _ in range(2)]
        sgt = [A.f32(512) for _ in range(2)]
        gi = 0

        def prep2(t):
            dma("sp", xt[t % 2], src_d[t * 128:(t + 1) * 128, :], r=[(src_name, t)], w=[f"xt{t % 2}"])
            make_hT(xt[t % 2], f"xt{t % 2}", hT[t % 2], f"hT{t % 2}", l, 1 if t < 2 else 0, 0, 1, 6)

        prep2(0)
        for t in range(NT):
            s = t % 2
            for g in range(8):
                if g == 4 and t + 1 < NT:
                    prep2(t + 1)
                b = gi % 4
                gi += 1
                for kc in range(8):
                    mm(bank(b), lhsT=hT[s][:, kc, :], rhs=wvg[:, kc, g * 512:(g + 1) * 512], start=kc == 0, stop=kc == 7,
                       r=[f"hT{s}", "wvg"], w=[("ps", b)])
                if g < 4:
                    act(vo[s][:, g * 512:(g + 1) * 512], bank(b), AF.Copy, r=[("ps", b)], w=[f"vo{s}"])
                else:
                    sb_ = b % 2
                    act(sgt[sb_], bank(b), AF.Sigmoid, r=[("ps", b)], w=[f"sgt{sb_}"])
                    tt("dve", sgo[s][:, (g - 4) * 512:(g - 3) * 512], bank(b), sgt[sb_], ALU.mult,
                       r=[("ps", b), f"sgt{sb_}"], w=[f"sgo{s}"])
            dma("pool", V_d[t], vo[s], r=[f"vo{s}"], w=[("V", t)])
            dma("pool", SG_d[t], sgo[s], r=[f"sgo{s}"], w=[("SG", t)])
        P.barrier()
        ck('A2')

        A.off = ret_off
        stf = A.f32(8 * 512).rearrange("p (c n) -> p c n", c=8)
        stb = A.bf16(8 * 512).rearrange("p (c n) -> p c n", c=8)
        qTt = [A.bf16(1024) for _ in range(2)]
        kTt = [A.bf16(1024) for _ in range(2)]
        kdt = [A.bf16(1024) for _ in range(2)]
        vt = [A.bf16(2048) for _ in range(2)]
        qd = [A.bf16(1024) for _ in range(2)]
        obt = [A.f32(2048) for _ in range(2)]
        pT = [A.bf16(128) for _ in range(2)]
        for d in (1, 0):
            order = [1, 0] + list(range(NT - 1, 1, -1)) if d == 1 else list(range(NT))
            memset("dve", stf, 0.0, w=[f"stf{c}" for c in range(8)])
            memset("pool", stb, 0.0, w=[f"stb{c}" for c in range(8)])
            for n, t in enumerate(order):
                s = n % 2
                dma("sp", qTt[s], QT_d[t], r=[("QT", t)], w=[f"qTt{s}"])
                dma("sp", kTt[s], KT_d[t], r=[("KT", t)], w=[f"kTt{s}"])
                dma("sp", kdt[s], KD_d[d, t], r=[("KD", d, t)], w=[f"kdt{s}"])
                dma("sp", vt[s], V_d[t], r=[("V", t)], w=[f"vt{s}"])
                if d == 0:
                    dma("sp", obt[s], OB_d[t], r=[("OB", t)], w=[f"obt{s}"])
                tt("dve", qd[s].rearrange("p (h c q) -> p h c q", h=4, c=2), qTt[s].rearrange("p (h c q) -> p h c q", h=4, c=2),
                   qdec[:, d, :, :].unsqueeze(2).to_broadcast([128, 4, 2, 128]), ALU.mult, r=[f"qTt{s}", "qdec"], w=[f"qd{s}"])
                for h in range(4):
                    hs = h % 2
                    bS, bO, bU = hs, 2 + hs, 4 + 2 * hs
                    for dc in range(2):
                        c = 2 * h + dc
                        mm(bank(bS)[:, 0:128], lhsT=kTt[s][:, c * 128:(c + 1) * 128], rhs=qTt[s][:, c * 128:(c + 1) * 128],
                           start=dc == 0, stop=dc == 1, r=[f"kTt{s}", f"qTt{s}"], w=[("ps", bS)])
                    tt("dve", pT[hs], bank(bS)[:, 0:128], maskT[:, d, h, :], ALU.mult, r=[("ps", bS), "maskT"], w=[f"pT{hs}"])
                    for dc in range(2):
                        c = 2 * h + dc
                        mm(bank(bU + dc), lhsT=kdt[s][:, c * 128:(c + 1) * 128], rhs=vt[s][:, h * 512:(h + 1) * 512],
                           start=True, stop=True, r=[f"kdt{s}", f"vt{s}"], w=[("ps", bU + dc)])
                    mm(bank(bO), lhsT=pT[hs], rhs=vt[s][:, h * 512:(h + 1) * 512], start=True, stop=False,
                       r=[f"pT{hs}", f"vt{s}"], w=[("ps", bO)])
                    for dc in range(2):
                        c = 2 * h + dc
                        mm(bank(bO), lhsT=qd[s][:, c * 128:(c + 1) * 128], rhs=stb[:, c, :], start=False, stop=dc == 1,
                           r=[f"qd{s}", f"stb{c}"], w=[("ps", bO)])
                    if d == 1:
                        act(obt[s][:, h * 512:(h + 1) * 512], bank(bO), AF.Copy, r=[("ps", bO)], w=[f"obt{s}"])
                    else:
                        tt("dve", obt[s][:, h * 512:(h + 1) * 512], bank(bO), obt[s][:, h * 512:(h + 1) * 512], ALU.add,
                           r=[("ps", bO), f"obt{s}"], w=[f"obt{s}"])
                    for dc in range(2):
                        c = 2 * h + dc
                        stt("dve", stf[:, c, :], stf[:, c, :], cdec[:, d, h:h + 1], bank(bU + dc), ALU.mult, ALU.add,
                            r=[f"stf{c}", ("ps", bU + dc), "cdec"], w=[f"stf{c}"])
                        act(stb[:, c, :], stf[:, c, :], AF.Copy, r=[f"stf{c}"], w=[f"stb{c}"])
                dma("pool", OB_d[t], obt[s], r=[f"obt{s}"], w=[("OB", t)])
        P.barrier()

        A.off = ret_off
        wo = A.bf16(16 * 1024).rearrange("p (c n) -> p c n", c=16)
        o1 = A.off
        stage = [A.f32(4096) for _ in range(NSTAGE)]
        load_w(wo, ret_w_o_d[j], 16, "wo", stage, kstep=4)
        P.barrier()
        A.off = o1
        ln = alloc_ln()
        ost = [A.f32(2048) for _ in range(2)]
        sgt_ = [A.bf16(2048) for _ in range(2)]
        xt = [A.f32(D) for _ in range(2)]
        onf = [A.f32(512) for _ in range(2)]
        onb = [A.bf16(2048) for _ in range(2)]
        onT = [A.bf16(2048) for _ in range(2)]
        gst = [A.f32(4 * 6).rearrange("p (h x) -> p h x", h=4) for _ in range(2)]
        gmv = [A.f32(4 * 2).rearrange("p (h x) -> p h x", h=4) for _ in range(2)]
        grs = [A.f32(8) for _ in range(2)]
        xt3 = xt + [A.f32(D)]

        def d_stage1(t):
            s = t % 2
            dma("sp", ost[s], OB_d[t], r=[("OB", t)], w=[f"ost{s}"])
            dma("sp", sgt_[s], SG_d[t], r=[("SG", t)], w=[f"sgt_{s}"])
            dma("sp", xt3[t % 3], src_d[t * 128:(t + 1) * 128, :], r=[(src_name, t)], w=[f"xt{t % 3}"])
            for h in range(4):
                P.add("dve", lambda e, h=h, s=s: e.bn_stats(out=gst[s][:, h, :], in_=ost[s][:, h * 512:(h + 1) * 512]),
                      reads=[f"ost{s}"], writes=[f"gst{s}"])
                P.add("dve", lambda e, h=h, s=s: e.bn_aggr(out=gmv[s][:, h, :], in_=gst[s][:, h:h + 1, :]),
                      reads=[f"gst{s}"], writes=[f"gmv{s}"])
            act(grs[s][:, 0:4], gmv[s][:, :, 1], AF.Ln, r=[f"gmv{s}"], w=[f"grs{s}"], bias=epsc)
            act(grs[s][:, 0:4], grs[s][:, 0:4], AF.Exp, r=[f"grs{s}"], w=[f"grs{s}"], scale=-0.5)
            stt("dve", grs[s][:, 4:8], gmv[s][:, :, 0], -1.0, grs[s][:, 0:4], ALU.mult, ALU.mult,
                r=[f"gmv{s}", f"grs{s}"], w=[f"grs{s}"])
            for h in range(4):
                hs = h % 2
                act(onf[hs], ost[s][:, h * 512:(h + 1) * 512], AF.Identity, r=[f"ost{s}", f"grs{s}"], w=[f"onf{hs}"],
                    scale=grs[s][:, h:h + 1], bias=grs[s][:, 4 + h:5 + h])
                tt("pool", onb[s][:, h * 512:(h + 1) * 512], onf[hs], sgt_[s][:, h * 512:(h + 1) * 512], ALU.mult,
                   r=[f"onf{hs}", f"sgt_{s}"], w=[f"onb{s}"])

        def d_stage2(t):
            s = t % 2
            tb = 2 * s
            for c in range(16):
                tr(bank_bf(tb, 2)[:, c * 128:(c + 1) * 128], onb[s][:, c * 128:(c + 1) * 128], ident, r=[f"onb{s}"],
                   w=[("ps", tb), ("ps", tb + 1)])
            act(onT[s], bank_bf(tb, 2), AF.Copy, r=[("ps", tb), ("ps", tb + 1)], w=[f"onT{s}"])
            yb = 4 + 2 * s
            for g in range(2):
                for c in range(16):
                    mm(bank(yb + g), lhsT=onT[s][:, c * 128:(c + 1) * 128], rhs=wo[:, c, g * 512:(g + 1) * 512],
                       start=c == 0, stop=c == 15, r=[f"onT{s}", "wo"], w=[("ps", yb + g)])

        cur_ver_ = [None]

        def d_stage3(t):
            s = t % 2
            ver = 1 if t < 2 else 0
            if ver != cur_ver_[0]:
                load_ln_tabs(ln, l, 0, ver)
                cur_ver_[0] = ver
            yb = 4 + 2 * s
            res_ln(ln, yb, yb + 1, xt3[t % 3], f"xt{t % 3}", XM_d[t * 128:(t + 1) * 128, :], ("XM", t))

        for i in range(NT + 2):
            if i < NT:
                d_stage1(i)
            if 0 <= i - 1 < NT:
                d_stage2(i - 1)
            if 0 <= i - 2 < NT:
                d_stage3(i - 2)
        P.barrier()

    def phase_nat(l, j, src_d, src_name, last):
        A.off = persist_off
        wn = A.bf16(8 * 3072).rearrange("p (c n) -> p c n", c=8)
        o1 = A.off
        stage = [A.f32(3072) for _ in range(NSTAGE)]
        load_w(wn, nat_w_in_d[j], 8, "wn", stage)
        P.barrier()
        A.off = o1
        xt = [A.f32(D) for _ in range(2)]
        hT = [A.bf16(8 * 512).rearrange("p (c t) -> p c t", c=8) for _ in range(2)]
        qTo = [A.bf16(8 * 512).rearrange("p (c t) -> p c t", c=8) for _ in range(2)]
        kTo = [A.bf16(8 * 512).rearrange("p (c t) -> p c t", c=8) for _ in range(2)]
        vno = [A.bf16(1056) for _ in range(2)]
        for s in range(2):
            memset("pool", vno[s], 1.0, w=[f"vno{s}"])
        sts = [(0, 2)] + [(2 + 4 * k, 4) for k in range(8)]
        ci = 0
        vi = 0
        xi_ = [0]

        def prep_tile(si, tl):
            t0_, ntl_ = sts[si]
            s_ = si % 2
            t = t0_ + tl
            xs_ = xi_[0] % 2
            xi_[0] += 1
            dma("sp", xt[xs_], src_d[t * 128:(t + 1) * 128, :], r=[(src_name, t)], w=[f"xt{xs_}"])
            make_hT(xt[xs_], f"xt{xs_}", hT[s_][:, :, tl * 128:(tl + 1) * 128], f"hT{s_}", l, 0 if t0_ >= 2 else 1, 0, 1, 6)

        for tl in range(sts[0][1]):
            prep_tile(0, tl)
        for si, (t0, ntl) in enumerate(sts):
            s = si % 2
            ver = 0 if t0 >= 2 else 1
            ntok = ntl * 128
            for c in range(16):
                b = ci % 4
                ci += 1
                for kc in range(8):
                    mm(bank(b)[:, 0:ntok], lhsT=wn[:, kc, c * 128:(c + 1) * 128], rhs=hT[s][:, kc, 0:ntok],
                       start=kc == 0, stop=kc == 7, r=["wn", f"hT{s}"], w=[("ps", b)])
                if c < 8:
                    act(qTo[s][:, c, 0:ntok], bank(b)[:, 0:ntok], AF.Identity, r=[("ps", b)], w=[f"qTo{s}"], scale=NAT_SCALE)
                else:
                    cp("dve", kTo[s][:, c - 8, 0:ntok], bank(b)[:, 0:ntok], r=[("ps", b)], w=[f"kTo{s}"])
                if si + 1 < len(sts) and c % 4 == 3 and (c // 4) < sts[si + 1][1]:
                    prep_tile(si + 1, c // 4)
            for tl in range(ntl):
                t = t0 + tl
                dma("pool", QT_d[t].rearrange("p (c t) -> p c t", c=8), qTo[s][:, :, tl * 128:(tl + 1) * 128], r=[f"qTo{s}"], w=[("QT", t)])
                dma("pool", KT_d[t].rearrange("p (c t) -> p c t", c=8), kTo[s][:, :, tl * 128:(tl + 1) * 128], r=[f"kTo{s}"], w=[("KT", t)])
                vs = vi % 2
                vi += 1
                for g in range(2):
                    b = 4 + g
                    for kc in range(8):
                        mm(bank(b), lhsT=hT[s][:, kc, tl * 128:(tl + 1) * 128], rhs=wn[:, kc, 2048 + g * 512:2048 + (g + 1) * 512],
                           start=kc == 0, stop=kc == 7, r=[f"hT{s}", "wn"], w=[("ps", b)])
                    act(vno[vs].rearrange("p (h x) -> p h x", x=33)[:, g * 16:(g + 1) * 16, 0:32],
                        bank(b).rearrange("p (h x) -> p h x", x=32), AF.Copy, r=[("ps", b)], w=[f"vno{vs}"])
                dma("pool", VN_d[t], vno[vs], r=[f"vno{vs}"], w=[("VN", t)])
        P.barrier()
        ck('NA')

        A.off = persist_off
        cmask = A.f32(128)
        dma("sp", cmask, colmask_d[:, :], w=["cmask"])
        bias_int = A.bf16(32 * 5 * 128).rearrange("p (h j q) -> p h j q", h=32, j=5)
        bstage = [A.f32(5 * 128).rearrange("p (j q) -> p j q", j=5) for _ in range(2)]
        bias_b = [A.bf16(5 * 128).rearrange("p (j q) -> p j q", j=5) for _ in range(3)]
        bcount = [0]

        def expand_bias(ty, h, dst, dkey):
            s = bcount[0] % 2
            bcount[0] += 1
            dma("sp", bstage[s], BIAS_d[ty, h].rearrange("j p q -> p j q"), r=[("BIAS", ty, h)], w=[f"bstage{s}"])
            tt("dve", bstage[s], bstage[s], cmask.unsqueeze(1).to_broadcast([128, 5, 128]), ALU.add,
               r=[f"bstage{s}", "cmask"], w=[f"bstage{s}"])
            act(dst, bstage[s], AF.Exp, r=[f"bstage{s}"], w=[dkey])

        for h in range(32):
            expand_bias(0, h, bias_int[:, h, :, :], "bias_int")
        kring = [A.bf16(1024) for _ in range(8)]
        vring = [A.bf16(1056) for _ in range(8)]
        kctx = [A.bf16(1024) for _ in range(2)]
        vctx = [A.bf16(1056) for _ in range(2)]
        for c in range(2):
            dma("sp", kctx[c], KT_d[c], r=[("KT", c)], w=[f"kctx{c}"])
            dma("sp", vctx[c], VN_d[c], r=[("VN", c)], w=[f"vctx{c}"])
        qTt = [A.bf16(1024) for _ in range(2)]
        qm = [A.bf16(8 * 4 * 128).rearrange("p (c h q) -> p c h q", c=8, h=4) for _ in range(2)]
        hmask_f = A.f32(4)
        hmask = A.bf16(4)
        memset("pool", hmask_f, 1.0, w=["hmask_f"])
        P.add("pool", lambda e: e.affine_select(out=hmask_f, in_=hmask_f, pattern=[[-32, 4]], compare_op=ALU.is_ge,
                                                fill=0.0, base=0, channel_multiplier=1), reads=["hmask_f"], writes=["hmask_f"])
        P.add("pool", lambda e: e.affine_select(out=hmask_f, in_=hmask_f, pattern=[[32, 4]], compare_op=ALU.is_ge,
                                                fill=0.0, base=31, channel_multiplier=-1), reads=["hmask_f"], writes=["hmask_f"])
        cp("dve", hmask, hmask_f, r=["hmask_f"], w=["hmask"])
        pT = [A.bf16(896) for _ in range(3)]
        rden = A.f32(32)
        ob = [A.bf16(1024) for _ in range(2)]
        loaded = set()
        qtiles = ([] if last else [0, 1]) + list(range(2, NT))
        PVREG = ((6, 0, 0, 15), (7, 0, 15, 15), (1, 384, 30, 2))

        def pv_dst(h):
            if h < 15:
                return bank(6)[:, 33 * h:33 * h + 33], [("ps", 6)]
            if h < 30:
                return bank(7)[:, 33 * (h - 15):33 * (h - 15) + 33], [("ps", 7)]
            o_ = 384 + 33 * (h - 30)
            return bank(1)[:, o_:o_ + 33], ["pvtail", ("ps", 1)]

        def tail_norm(n):
            t = qtiles[n]
            o_ = ob[n % 2]
            okey = f"ob{n % 2}"
            for (pvb, co, h0, nh) in PVREG:
                rk = ["pvtail"] if pvb == 1 else [("ps", pvb)]
                view = bank(pvb)[:, co:co + nh * 33].rearrange("p (h x) -> p h x", x=33)
                P.add("dve", lambda e, view=view, h0=h0, nh=nh: e.reciprocal(out=rden[:, h0:h0 + nh].unsqueeze(2), in_=view[:, :, 32:33]),
                      reads=rk, writes=["rden"])
                tt("dve", o_[:, h0 * 32:(h0 + nh) * 32].rearrange("p (h x) -> p h x", x=32), view[:, :, 0:32],
                   rden[:, h0:h0 + nh].unsqueeze(2).to_broadcast([128, nh, 32]), ALU.mult, r=rk + ["rden"], w=[okey])
            dma("pool", OA_d[t], o_, r=[okey], w=[("OA", t)])

        gcount = 0

        def tile_params(t):
            if t < 2:
                return 0, 0, 0
            T = t - 2
            ty_ = 1 if T == 0 else 2 if T == 1 else 3 if T == 30 else 4 if T == 31 else 0
            return 5, ty_, min(max(T - 2, 0), 27)

        def prefetch(n):
            t = qtiles[n]
            s_ = n % 2
            dma("sp", qTt[s_], QT_d[t], r=[("QT", t)], w=[f"qTt{s_}"])
            tt("dve", qm[s_], qTt[s_].rearrange("p (c q) -> p c q", c=8).unsqueeze(2).to_broadcast([128, 8, 4, 128]),
               hmask.unsqueeze(1).unsqueeze(3).to_broadcast([128, 8, 4, 128]), ALU.mult, r=[f"qTt{s_}", "hmask"], w=[f"qm{s_}"])
            nloc_, _, wt_ = tile_params(t)
            if nloc_:
                for kt in range(wt_, wt_ + 5):
                    if kt not in loaded:
                        loaded.add(kt)
                        dma("sp", kring[kt % 8], KT_d[kt + 2], r=[("KT", kt + 2)], w=[f"kring{kt % 8}"])
                        dma("sp", vring[kt % 8], VN_d[kt + 2], r=[("VN", kt + 2)], w=[f"vring{kt % 8}"])

        prefetch(0)
        for n, t in enumerate(qtiles):
            s = n % 2
            is_ctx = t < 2
            nloc, ty, wt = tile_params(t)
            nch = nloc + 2
            g0 = gcount
            gcount += 32

            def emit_S(h):
                hp = (g0 + h) % 3
                hc, pb = h // 4, 32 * (h % 4)
                sb0 = 2 * hp
                psS = bank(sb0, 2)
                skeys = [("ps", sb0), ("ps", sb0 + 1)]
                if nloc and ty != 0:
                    expand_bias(ty, h, bias_b[hp], f"bias_b{hp}")
                for jj in range(nch):
                    if jj < nloc:
                        kt = wt + jj
                        ksrc, kkey = kring[kt % 8], f"kring{kt % 8}"
                    else:
                        ksrc, kkey = kctx[jj - nloc], f"kctx{jj - nloc}"
                    wk = [skeys[(jj * 128) // 512]]
                    if wk[0] == ("ps", 1):
                        wk = wk + ["pvtail"]
                    mm(psS[:, jj * 128:(jj + 1) * 128], lhsT=ksrc[:, hc * 128:(hc + 1) * 128],
                       rhs=qm[s][:, hc, h % 4, :], start=True, stop=True, r=[kkey, f"qm{s}"], w=wk)
                act(pT[hp][:, 0:nch * 128], psS[:, 0:nch * 128], AF.Exp, r=skeys, w=[f"pT{hp}"])
                if nloc:
                    if ty == 0:
                        bsrc, bkey = bias_int[:, h, :, :], "bias_int"
                    else:
                        bsrc, bkey = bias_b[hp], f"bias_b{hp}"
                    pv_ = pT[hp][:, 0:640].rearrange("p (j q) -> p j q", j=5)
                    tt("dve", pv_, pv_, bsrc, ALU.mult, r=[f"pT{hp}", bkey], w=[f"pT{hp}"])

            def emit_PV(h):
                hp = (g0 + h) % 3
                dst, dkeys = pv_dst(h)
                for jj in range(nch):
                    if jj < nloc:
                        kt = wt + jj
                        vsrc, vkey = vring[kt % 8], f"vring{kt % 8}"
                    else:
                        vsrc, vkey = vctx[jj - nloc], f"vctx{jj - nloc}"
                    mm(dst, lhsT=pT[hp][:, jj * 128:(jj + 1) * 128], rhs=vsrc[:, h * 33:(h + 1) * 33],
                       start=jj == 0, stop=jj == nch - 1, r=[f"pT{hp}", vkey], w=dkeys)

            emit_S(0)
            emit_S(1)
            emit_S(2)
            for h in range(3, 32):
                emit_PV(h - 3)
                emit_S(h)
                if h == 16 and n + 1 < len(qtiles):
                    prefetch(n + 1)
            emit_PV(29)
            emit_PV(30)
            emit_PV(31)
            tail_norm(n)
        P.barrier()

        A.off = persist_off
        won = A.bf16(8 * 1024).rearrange("p (c n) -> p c n", c=8)
        o1 = A.off
        stage = [A.f32(4096), A.f32(4096)]
        load_w(won, nat_w_o_d[j], 8, "won", stage, kstep=4)
        P.barrier()
        A.off = o1
        ln = alloc_ln()
        obt = [A.bf16(1024) for _ in range(2)]
        oT = [A.bf16(1024) for _ in range(2)]
        xt = [A.f32(D) for _ in range(3)]
        nq = len(qtiles)
        cur_ver_ = [None]

        def c_stage1(i):
            t = qtiles[i]
            dma("sp", obt[i % 2], OA_d[t], r=[("OA", t)], w=[f"obt{i % 2}"])
            dma("sp", xt[i % 3], src_d[t * 128:(t + 1) * 128, :], r=[(src_name, t)], w=[f"xt{i % 3}"])

        def c_stage2(i):
            s_ = i % 2
            for c in range(8):
                tr(bank_bf(s_)[:, c * 128:(c + 1) * 128], obt[s_][:, c * 128:(c + 1) * 128], ident, r=[f"obt{s_}"], w=[("ps", s_)])
            act(oT[s_], bank_bf(s_), AF.Copy, r=[("ps", s_)], w=[f"oT{s_}"])
            yb = 2 + 2 * s_
            for g in range(2):
                for c in range(8):
                    mm(bank(yb + g), lhsT=oT[s_][:, c * 128:(c + 1) * 128], rhs=won[:, c, g * 512:(g + 1) * 512],
                       start=c == 0, stop=c == 7, r=[f"oT{s_}", "won"], w=[("ps", yb + g)])

        def c_stage3(i):
            t = qtiles[i]
            ver = 1 if t < 2 else 0
            if ver != cur_ver_[0]:
                load_ln_tabs(ln, l, 0, ver)
                cur_ver_[0] = ver
            yb = 2 + 2 * (i % 2)
            res_ln(ln, yb, yb + 1, xt[i % 3], f"xt{i % 3}", XM_d[t * 128:(t + 1) * 128, :], ("XM", t))

        for i in range(nq + 2):
            if i < nq:
                c_stage1(i)
            if 0 <= i - 1 < nq:
                c_stage2(i - 1)
            if 0 <= i - 2 < nq:
                c_stage3(i - 2)
        P.barrier()

    try:
        phase_mods()
        src_d, src_name = xs_d, "xs"
        for l in range(n_layers):
            last = l == DEPTH - 1
            if l % 2 == 0:
                phase_ret(l, l // 2, src_d, src_name)
            else:
                phase_nat(l, l // 2, src_d, src_name, last)
            ck('mixer%d' % l)
            nxt_nat = (l + 1 < n_layers) and ((l + 1) % 2 == 1)
            phase_mlp(l, XM_d, "XM", XN_d, "XN", last, extra=bias_expansion_thunks((l + 1) // 2) if nxt_nat else ())
            src_d, src_name = XN_d, "XN"
    except _Stop:
        P.barrier()
    if dbg:
        A.off = persist_off
        for name, shape in dbg.items():
            src = {"XM": XM_d, "XN": XN_d}[name]
            buf = A.f32(D)
            for t in range(shape[0] // 128):
                dma("sp", buf, src[t * 128:(t + 1) * 128, :], r=[(name, t)], w=["dbgbuf"])
                dma("sp", dbg_outs[name][t * 128:(t + 1) * 128, :], buf, r=["dbgbuf"], w=[("dbg", name, t)])
    nsig = P.emit()
    es.close()
    return nc, {s: len(P.ops[s]) for s in STREAMS}, nsig, A.peak


def _rope_tables():
    f = np.arange(64, dtype=np.float32)
    inv = (1.0 / (np.float32(10000.0) ** (np.arange(0, 128, 2, dtype=np.float32) / np.float32(128)))).astype(np.float32)
    t = np.arange(SEQ)
    row = (t // 64).astype(np.float32)
    col = (t % 64).astype(np.float32)
    tab = np.zeros((128, 4, SEQ), np.float32)
    for ty, posv in enumerate((row, col)):
        ang = (posv[None, :] * inv[:, None]).astype(np.float32)
        c = np.cos(ang).astype(np.float32)
        s = np.sin(ang).astype(np.float32)
        tab[0:64, ty, :] = c
        tab[64:128, ty, :] = c
        tab[0:64, 2 + ty, :] = -s
        tab[64:128, 2 + ty, :] = s
    return tab


def _colmask():
    m = np.full((128, 128), NEG, np.float32)
    for qc in range(64):
        cs = min(max(qc - 8, 0), 48)
        for i in range(2):
            for jq in range(2):
                m[i * 64 + cs:i * 64 + cs + 16, jq * 64 + qc] = 0.0
    return m


_VAR = ((-4, 3), (0, 7), (-1, 6), (-2, 5), (-3, 4), (-5, 2), (-6, 1), (-7, 0))


def _pad_tables(rpb):
    pad = np.full((2, 8, 32, 20, 127), NEG, np.float32)
    rev = rpb[:, :, :, ::-1]
    for v, (lo, hi) in enumerate(_VAR):
        for dr in range(lo, hi + 1):
            pad[:, v, :, dr + 10, 48:79] = rev[:, :, dr + 7, :]
    return pad


_CACHE = {}


def kernel(x, c, ctx, c_ctx, ada_w, ada_b, ret_w_in, ret_w_o, ret_decay, nat_w_in, nat_w_o, nat_rpb,
           mlp_w1, mlp_w2, ln_g, ln_b):
    f = lambda a: np.ascontiguousarray(np.asarray(a, dtype=np.float32))
    x, c, ctx, c_ctx = f(x), f(c), f(ctx), f(c_ctx)
    if "nc" not in _CACHE:
        _CACHE["nc"] = build()[0]
    nc = _CACHE["nc"]
    shared = {
        "ada_w": f(ada_w), "ada_b": f(ada_b),
        "ada_bT": f(np.asarray(ada_b).reshape(DEPTH, 48, 128).transpose(0, 2, 1)),
        "ret_w_in": f(ret_w_in), "ret_w_o": f(ret_w_o), "ret_decay": f(np.asarray(ret_decay).reshape(2, 8)),
        "nat_w_in": f(nat_w_in), "nat_w_o": f(nat_w_o),
        "pad": _pad_tables(f(nat_rpb)), "colmask": _colmask(), "rope": _rope_tables(),
        "mlp_w1": f(mlp_w1), "mlp_w2": f(mlp_w2), "ln_g": f(ln_g), "ln_b": f(ln_b),
    }
    in_maps = []
    for b in range(8):
        m = dict(shared)
        m["xs"] = np.ascontiguousarray(np.concatenate([ctx[b], x[b]], axis=0))
        cv = np.stack([c[b], c_ctx], axis=1)
        m["cvT"] = np.ascontiguousarray(cv.reshape(8, 128, 2).transpose(1, 0, 2))
        in_maps.append(m)
    res = run_bass_kernel_spmd(nc, in_maps, core_ids=list(range(8)))
    return np.stack([np.asarray(r["y"], dtype=np.float32) for r in res.results], axis=0)
```

```python
import contextlib
import numpy as np
import concourse.bass as bass
import concourse.mybir as mybir
from concourse.bass_utils import run_bass_kernel_spmd

F32 = mybir.dt.float32
BF16 = mybir.dt.bfloat16
I32 = mybir.dt.int32
AF = mybir.ActivationFunctionType
ALU = mybir.AluOpType

D = 1024
SEQ = 4096
CTX = 256
TOK = SEQ + CTX
NT = TOK // 128
DEPTH = 4
ALPHA = (2.0 * DEPTH) ** 0.25
EPS = 1e-5
NEG = -30000.0
NAT_SCALE = 32 ** -0.5

EPOCH = 3000
STREAMS = ("pe", "act", "dve", "pool", "sp")


class Op:
    __slots__ = ("stream", "fn", "is_dma", "deps", "signal", "cnt", "dsem", "dval")

    def __init__(self, stream, fn, is_dma):
        self.stream = stream
        self.fn = fn
        self.is_dma = is_dma
        self.deps = []
        self.signal = False
        self.cnt = None
        self.dsem = None
        self.dval = None


class Prog:
    def __init__(self, nc, n_dma_sems=16):
        self.nc = nc
        self.ops = {s: [] for s in STREAMS}
        self.last_w = {}
        self.readers = {}
        self.n_dma_sems = n_dma_sems
        self.dma_rr = {s: 0 for s in STREAMS}
        self.dma_cnt = {}
        self.dma_last = {}
        self.pending = {s: [] for s in STREAMS}

    def _need(self, op, dep):
        if dep is None or dep is op:
            return
        if not dep.is_dma:
            dep.signal = True
        op.deps.append(dep)

    def barrier(self):
        deps = []
        for s in STREAMS:
            for op in reversed(self.ops[s]):
                if not op.is_dma:
                    deps.append(op)
                    break
        deps.extend(self.dma_last.values())
        for s in STREAMS:
            self.pending[s] = list(deps)

    def add(self, stream, fn, reads=(), writes=(), dma=False):
        op = Op(stream, fn, dma)
        if self.pending[stream]:
            for dep in self.pending[stream]:
                self._need(op, dep)
            self.pending[stream] = []
        if dma:
            slot = self.dma_rr[stream]
            self.dma_rr[stream] = (slot + 1) % self.n_dma_sems
            key = (stream, slot)
            prev = self.dma_last.get(key)
            if prev is not None:
                self._need(op, prev)
            op.dsem = key
            op.dval = self.dma_cnt.get(key, 0) + 16
            self.dma_cnt[key] = op.dval
            self.dma_last[key] = op
        for r in reads:
            w = self.last_w.get(r)
            if w is not None:
                if w.stream == stream and stream == "pe" and not w.is_dma and not dma:
                    pass
                else:
                    self._need(op, w)
        for wk in writes:
            w = self.last_w.get(wk)
            if w is not None and not (w.stream == stream and not w.is_dma and not dma):
                self._need(op, w)
            for rd in self.readers.get(wk, ()):
                if rd.stream == stream and not rd.is_dma and not dma:
                    continue
                self._need(op, rd)
        for r in reads:
            lst = self.readers.setdefault(r, [])
            if not dma:
                for i_, o_ in enumerate(lst):
                    if o_.stream == stream and not o_.is_dma:
                        lst[i_] = op
                        break
                else:
                    lst.append(op)
            else:
                lst.append(op)
        for wk in writes:
            self.last_w[wk] = op
            self.readers[wk] = []
        self.ops[stream].append(op)
        return op

    def emit(self):
        nc = self.nc
        nsig = {}
        for s in STREAMS:
            c = 0
            for op in self.ops[s]:
                if op.signal and not op.is_dma:
                    op.cnt = c
                    c += 1
            nsig[s] = c
        with contextlib.ExitStack() as es:
            esem = {}
            for s in STREAMS:
                n = (nsig[s] + EPOCH - 1) // EPOCH
                esem[s] = [es.enter_context(nc.semaphore(f"e_{s}_{i}")) for i in range(max(n, 1))]
            dsem = {}
            for key in self.dma_cnt:
                dsem[key] = es.enter_context(nc.semaphore(f"d_{key[0]}_{key[1]}"))
            block = es.enter_context(nc.Block())

            def run(stream, eng):
                waited = {}
                for op in self.ops[stream]:
                    for dep in op.deps:
                        if dep.is_dma:
                            k = ("d", dep.dsem)
                            v = dep.dval
                            if waited.get(k, 0) >= v:
                                continue
                            waited[k] = v
                            eng.wait_ge(dsem[dep.dsem], v)
                        else:
                            k = ("e", dep.stream)
                            v = dep.cnt
                            if waited.get(k, -1) >= v:
                                continue
                            waited[k] = v
                            eng.wait_ge(esem[dep.stream][v // EPOCH], v % EPOCH + 1)
                    ins = op.fn(eng)
                    if op.is_dma:
                        ins.then_inc(dsem[op.dsem], 16)
                    elif op.signal:
                        ins.then_inc(esem[stream][op.cnt // EPOCH], 1)
                for key, val in self.dma_cnt.items():
                    if key[0] == stream:
                        eng.wait_ge(dsem[key], val)

            @block.tensor
            def _(e):
                run("pe", e)

            @block.scalar
            def _(e):
                run("act", e)

            @block.vector
            def _(e):
                run("dve", e)

            @block.gpsimd
            def _(e):
                run("pool", e)

            @block.sync
            def _(e):
                run("sp", e)
        return nsig


class Arena:
    def __init__(self, t, words):
        self.t = t
        self.words = words
        self.off = 0
        self.peak = 0

    def f32(self, n, dt=F32):
        assert self.off + n <= self.words, ("arena overflow", self.off, n, self.words)
        ap = self.t[:, self.off:self.off + n]
        self.off += (n + 15) // 16 * 16
        self.peak = max(self.peak, self.off)
        return ap if dt == F32 else ap.bitcast(dt)

    def bf16(self, n):
        assert n % 2 == 0
        return self.f32(n // 2).bitcast(BF16)


ARENA_WORDS = 196 * 256
NSTAGE = 4


class _Stop(Exception):
    pass


def build(n_layers=DEPTH, dbg=None, stop=None):
    nc = bass.Bass("TRN2", target_bir_lowering=False)

    def din(name, shape, dt=F32):
        return nc.dram_tensor(name, list(shape), dt, kind="ExternalInput").ap()

    def dscr(name, shape, dt=F32):
        return nc.dram_tensor(name, list(shape), dt).ap()

    xs_d = din("xs", [TOK, D])
    cvT_d = din("cvT", [128, 8, 2])
    ada_w_d = din("ada_w", [DEPTH, D, 6 * D])
    ada_b_d = din("ada_b", [DEPTH, 6 * D])
    ada_bT_d = din("ada_bT", [DEPTH, 128, 48])
    ret_w_in_d = din("ret_w_in", [2, D, 6 * D])
    ret_w_o_d = din("ret_w_o", [2, 2 * D, D])
    ret_decay_d = din("ret_decay", [2, 8])
    nat_w_in_d = din("nat_w_in", [2, D, 3 * D])
    nat_w_o_d = din("nat_w_o", [2, D, D])
    pad_d = din("pad", [2, 8, 32, 20, 127])
    colmask_d = din("colmask", [128, 128])
    rope_d = din("rope", [128, 4, SEQ])
    mlp_w1_d = din("mlp_w1", [DEPTH, D, 4 * D])
    mlp_w2_d = din("mlp_w2", [DEPTH, 4 * D, D])
    ln_g_d = din("ln_g", [DEPTH, 2, D])
    ln_b_d = din("ln_b", [DEPTH, 2, D])
    y_d = nc.dram_tensor("y", [SEQ, D], F32, kind="ExternalOutput").ap()

    XM_d = dscr("XM", [TOK, D])
    XN_d = dscr("XN", [TOK, D])
    MODR_d = dscr("MODR", [DEPTH, 2, 2, D])
    QT_d = dscr("QT", [NT, 128, 1024], BF16)
    KT_d = dscr("KT", [NT, 128, 1024], BF16)
    KD_d = dscr("KD", [2, NT, 128, 1024], BF16)
    V_d = dscr("V", [NT, 128, 2048], BF16)
    SG_d = dscr("SG", [NT, 128, 2048], BF16)
    OB_d = dscr("OB", [NT, 128, 2048], F32)
    VN_d = dscr("VN", [NT, 128, 1056], BF16)
    BIAS_d = dscr("BIAS", [5, 32, 5, 128, 128])
    OA_d = dscr("OA", [NT, 128, 1024], BF16)
    dbg_outs = {}
    if dbg:
        for name, shape in dbg.items():
            dbg_outs[name] = nc.dram_tensor("dbg_" + name, list(shape), F32, kind="ExternalOutput").ap()

    P = Prog(nc)

    def ck(name):
        if stop == name:
            raise _Stop()

    es = contextlib.ExitStack()
    arena_t = es.enter_context(nc.sbuf_tensor("arena", [128, ARENA_WORDS], F32))
    ps_t = es.enter_context(nc.psum_tensor("ps", [128, 4096], F32))
    A = Arena(arena_t, ARENA_WORDS)

    def bank(b, n=1):
        return ps_t[:, b * 512:(b + n) * 512]

    def bank_bf(b, n=1):
        return ps_t[:, b * 512:(b + n) * 512].bitcast(BF16)

    def dma(q, out, in_, r=(), w=()):
        P.add(q, lambda e: e.dma_start(out=out, in_=in_), reads=r, writes=w, dma=True)

    def mm(out, lhsT, rhs, start, stop, r, w, tp=None):
        if tp is None:
            P.add("pe", lambda e: e.matmul(out, lhsT=lhsT, rhs=rhs, start=start, stop=stop), reads=r, writes=w)
        else:
            P.add("pe", lambda e: e.matmul(out, lhsT=lhsT, rhs=rhs, start=start, stop=stop, tile_position=tp), reads=r, writes=w)

    def tr(out, in_, ident, r, w):
        P.add("pe", lambda e: e.transpose(out=out, in_=in_, identity=ident), reads=list(r) + ["ident"], writes=w)

    def act(out, in_, func, r, w, scale=None, bias=None):
        kw = {}
        if scale is not None:
            kw["scale"] = scale
        if bias is not None:
            kw["bias"] = bias
        P.add("act", lambda e: e.activation(out=out, in_=in_, func=func, **kw), reads=r, writes=w)

    def tt(eng, out, in0, in1, op, r, w):
        P.add(eng, lambda e: e.tensor_tensor(out=out, in0=in0, in1=in1, op=op), reads=r, writes=w)

    def ts(eng, out, in0, s1, op0, r, w, s2=None, op1=None):
        if op1 is None:
            P.add(eng, lambda e: e.tensor_scalar(out=out, in0=in0, scalar1=s1, scalar2=None, op0=op0), reads=r, writes=w)
        else:
            P.add(eng, lambda e: e.tensor_scalar(out=out, in0=in0, scalar1=s1, scalar2=s2, op0=op0, op1=op1), reads=r, writes=w)

    def stt(eng, out, in0, scalar, in1, op0, op1, r, w):
        P.add(eng, lambda e: e.scalar_tensor_tensor(out=out, in0=in0, scalar=scalar, in1=in1, op0=op0, op1=op1), reads=r, writes=w)

    def cp(eng, out, in_, r, w):
        P.add(eng, lambda e: e.tensor_copy(out=out, in_=in_), reads=r, writes=w)

    def memset(eng, out, val, w, r=()):
        P.add(eng, lambda e: e.memset(out, val), reads=r, writes=w)

    identf = A.f32(128)
    ident = A.bf16(128)
    memset("pool", identf, 0.0, w=["identf"])
    P.add("pool", lambda e: e.affine_select(out=identf, in_=identf, pattern=[[-1, 128]], compare_op=ALU.not_equal,
                                            fill=1.0, base=0, channel_multiplier=1), reads=["identf"], writes=["identf"])
    cp("dve", ident, identf, r=["identf"], w=["ident"])
    epsc = A.f32(1)
    memset("pool", epsc, EPS, w=["epsc"])
    modT = A.f32(DEPTH * 2 * 4 * 8).rearrange("p (l v s c) -> p l v s c", l=DEPTH, v=2, s=4)
    persist_off = A.off

    def phase_mods():
        A.off = persist_off
        cv = A.f32(16)
        scT = A.f32(16)
        sig = A.f32(16)
        dma("sp", cv, cvT_d.rearrange("p c r -> p (c r)"), w=["cv"])
        act(sig, cv, AF.Sigmoid, r=["cv"], w=["sig"])
        tt("dve", scT, cv, sig, ALU.mult, r=["cv", "sig"], w=["scT"])
        scT3 = scT.rearrange("p (c r) -> p c r", r=2)
        wb = [A.f32(8 * 512) for _ in range(3)]
        abT = A.f32(DEPTH * 48)
        dma("sp", abT.rearrange("p (l c) -> p l c", l=DEPTH), ada_bT_d.rearrange("l p c -> p l c"), w=["abT"])
        abR = A.f32(DEPTH * 2 * D)
        rowo = A.f32(DEPTH * 2 * D)
        for l in range(n_layers):
            for wi, sec in enumerate((2, 5)):
                dma("sp", abR[0:2, (l * 2 + wi) * D:(l * 2 + wi + 1) * D],
                    ada_b_d[l:l + 1, sec * D:(sec + 1) * D].partition_broadcast(2), w=["abR"])
        it = 0
        for l in range(n_layers):
            for grp in range(12):
                s = it % 3
                pb_ = it % 2
                it += 1
                wv = wb[s].rearrange("p (c n) -> p c n", c=8)
                dma("sp" if it % 2 else "act", wv, ada_w_d[l].rearrange("(c p) n -> p c n", p=128)[:, :, grp * 512:(grp + 1) * 512],
                    w=[f"wb{s}"])
                sec = grp // 2
                if sec in (2, 5):
                    wi = 0 if sec == 2 else 1
                    half = grp % 2
                    for kc in range(8):
                        mm(bank(2 + pb_)[0:2, :], lhsT=scT3[:, kc, :], rhs=wv[:, kc, :], start=kc == 0, stop=kc == 7,
                           r=["scT", f"wb{s}"], w=[("ps", 2 + pb_)])
                    o = (l * 2 + wi) * D + half * 512
                    tt("dve", rowo[0:2, o:o + 512], bank(2 + pb_)[0:2, :], abR[0:2, o:o + 512], ALU.add,
                       r=[("ps", 2 + pb_), "abR"], w=["rowo"])
                else:
                    si = {0: 0, 1: 1, 3: 2, 4: 3}[sec]
                    for cc in range(4):
                        for kc in range(8):
                            mm(bank(pb_)[:, cc * 2:cc * 2 + 2], lhsT=wv[:, kc, cc * 128:(cc + 1) * 128], rhs=scT3[:, kc, :],
                               start=kc == 0, stop=kc == 7, r=["scT", f"wb{s}"], w=[("ps", pb_)])
                    for v in range(2):
                        c0 = (grp % 2) * 4
                        ab = abT[:, l * 48 + grp * 4:l * 48 + grp * 4 + 4]
                        src = bank(pb_)[:, 0:8].rearrange("p (c v) -> p c v", v=2)[:, :, v]
                        tt("dve", modT[:, l, v, si, c0:c0 + 4], src, ab, ALU.add, r=[("ps", pb_), "abT"], w=["modT"])
        for l in range(n_layers):
            for v in range(2):
                for si in (1, 3):
                    ts("dve", modT[:, l, v, si, :], modT[:, l, v, si, :], 1.0, ALU.add, r=["modT"], w=["modT"])
        for l in range(n_layers):
            for wi in range(2):
                o = (l * 2 + wi) * D
                dma("pool", MODR_d[l, :, wi, :], rowo[0:2, o:o + D], r=["rowo"], w=["MODR"])
        P.barrier()
        ck('mods')

    def load_row_bc(dst, src_row, w, r=()):
        dma("sp", dst, src_row.partition_broadcast(128), r=r, w=w)

    def make_hT(xt, xkey, hT_view, hkey, l, ver, si_sh, si_sc, tb):
        pt = bank(tb, 2)
        for c in range(8):
            tr(pt[:, c * 128:(c + 1) * 128], xt[:, c * 128:(c + 1) * 128], identf, r=[xkey], w=[("ps", tb), ("ps", tb + 1)])
        for c in range(8):
            act(hT_view[:, c, :], pt[:, c * 128:(c + 1) * 128], AF.Identity, r=[("ps", tb), ("ps", tb + 1), "modT"], w=[hkey],
                scale=modT[:, l, ver, si_sc, c:c + 1], bias=modT[:, l, ver, si_sh, c:c + 1])

    class LNctx:
        pass

    def alloc_ln(nbuf=2):
        c = LNctx()
        c.gtab = A.f32(D)
        c.lng = A.f32(D)
        c.lnb = A.f32(D)
        c.r = [A.f32(D) for _ in range(nbuf)]
        c.st = [A.f32(12) for _ in range(nbuf)]
        c.mv = [A.f32(4) for _ in range(nbuf)]
        c.n = 0
        c.nbuf = nbuf
        return c

    def load_ln_tabs(c, l, which, ver):
        load_row_bc(c.gtab, MODR_d[l, ver, which:which + 1, :], w=["gtab"], r=["MODR"])
        load_row_bc(c.lng, ln_g_d[l, which:which + 1, :], w=["lng"])
        load_row_bc(c.lnb, ln_b_d[l, which:which + 1, :], w=["lnb"])

    def res_ln_a(c, yb0, yb1):
        s = c.n % c.nbuf
        c.n += 1
        r = c.r[s]
        rk = f"lnr{s}"
        for g, yb in enumerate((yb0, yb1)):
            tt("dve", r[:, g * 512:(g + 1) * 512], bank(yb), c.gtab[:, g * 512:(g + 1) * 512], ALU.mult,
               r=[("ps", yb), "gtab"], w=[rk])
        return s

    def res_ln_b(c, s, xt, xkey, dst_d, dst_key):
        r = c.r[s]
        rk = f"lnr{s}"
        stt("dve", r, xt, ALPHA, r, ALU.mult, ALU.add, r=[xkey, rk], w=[rk])
        st = c.st[s].rearrange("p (a b) -> p a b", a=2)
        for g in range(2):
            P.add("dve", lambda e, g=g: e.bn_stats(out=st[:, g, :], in_=r[:, g * 512:(g + 1) * 512]), reads=[rk], writes=[f"lnst{s}"])
        mv = c.mv[s]
        P.add("dve", lambda e: e.bn_aggr(out=mv[:, 0:2], in_=st), reads=[f"lnst{s}"], writes=[f"lnmv{s}"])
        act(mv[:, 2:3], mv[:, 1:2], AF.Ln, r=[f"lnmv{s}"], w=[f"lnmv{s}"], bias=epsc)
        act(mv[:, 2:3], mv[:, 2:3], AF.Exp, r=[f"lnmv{s}"], w=[f"lnmv{s}"], scale=-0.5)
        ts("dve", mv[:, 3:4], mv[:, 0:1], mv[:, 2:3], ALU.mult, r=[f"lnmv{s}"], w=[f"lnmv{s}"], s2=-1.0, op1=ALU.mult)
        act(r, r, AF.Identity, r=[rk, f"lnmv{s}"], w=[rk], scale=mv[:, 2:3], bias=mv[:, 3:4])
        tt("dve", r, r, c.lng, ALU.mult, r=[rk, "lng"], w=[rk])
        tt("pool", r, r, c.lnb, ALU.add, r=[rk, "lnb"], w=[rk])
        dma("pool", dst_d, r, r=[rk], w=[dst_key])

    def res_ln(c, yb0, yb1, xt, xkey, dst_d, dst_key):
        s = res_ln_a(c, yb0, yb1)
        res_ln_b(c, s, xt, xkey, dst_d, dst_key)

    cast_rr = [0]

    def cast(out, in_, r, w, scale=None):
        i = cast_rr[0] % 2
        cast_rr[0] += 1
        if i == 0:
            if scale is None:
                act(out, in_, AF.Copy, r, w)
            else:
                act(out, in_, AF.Identity, r, w, scale=scale)
        else:
            if scale is None:
                cp("dve", out, in_, r, w)
            else:
                ts("dve", out, in_, scale, ALU.mult, r, w)

    def load_w(dst3, src2, nchunk, key, stage, col0=0, ncol=None, kstep=1):
        ncol_ = ncol if ncol is not None else src2.shape[1]
        srcv = src2.rearrange("(c p) n -> p c n", p=128)
        for k0 in range(0, nchunk, kstep):
            s_ = (k0 // kstep) % len(stage)
            st = stage[s_][:, 0:kstep * ncol_].rearrange("p (c n) -> p c n", c=kstep)
            dma("sp" if s_ % 2 == 0 else "act", st, srcv[:, k0:k0 + kstep, col0:col0 + ncol_], w=[f"stage{s_}"])
            cast(dst3[:, k0:k0 + kstep, :], st, r=[f"stage{s_}"], w=[key])

    def bias_expansion_thunks(j):
        th = []
        for ty in range(5):
            off = (4, 0, 2, 6, 8)[ty]
            vv = ((0, 0), (1, 2), (3, 4), (0, 5), (6, 7))[ty]
            for h in range(32):
                for jq in range(2):
                    base = (((j * 8 + vv[jq]) * 32 + h) * 20 + (10 - jq - off)) * 127 + 63
                    src = bass.AP(pad_d.tensor, base, [[2 * 127, 5], [127, 2], [-1, 64], [1, 64]])
                    dst = BIAS_d[ty, h].rearrange("j (i k) (a q) -> j i k a q", i=2, a=2)[:, :, :, jq, :]
                    th.append((dst, src, ("BIAS", ty, h)))
        return th

    def phase_mlp(l, src_d, src_name, dst_d, dst_name, last, extra=()):
        A.off = persist_off
        w1 = A.bf16(8 * 4096).rearrange("p (c n) -> p c n", c=8)
        w2 = A.bf16(32 * 1024).rearrange("p (c n) -> p c n", c=32)
        o1 = A.off
        stage = [A.f32(4096) for _ in range(NSTAGE)]
        load_w(w1, mlp_w1_d[l], 8, "w1", stage)
        load_w(w2, mlp_w2_d[l], 32, "w2", stage, kstep=4)
        P.barrier()
        A.off = o1
        ln = alloc_ln()
        xt = [A.f32(D) for _ in range(4)]
        hT = [A.bf16(8 * 256).rearrange("p (c t) -> p c t", c=8) for _ in range(2)]
        uT = A.bf16(32 * 256)
        rl = [A.f32(512) for _ in range(2)]
        sts2 = ([] if last else [0]) + [2 + 2 * k for k in range(16)]
        cur_ver = None
        extra = list(extra)
        per = (len(extra) + len(sts2) - 1) // len(sts2) if extra else 0

        def prep(si):
            t0 = sts2[si]
            for tl in range(2):
                t = t0 + tl
                xs_ = (2 * si + tl) % 4
                dma("sp", xt[xs_], src_d[t * 128:(t + 1) * 128, :], r=[(src_name, t)], w=[f"xt{xs_}"])
                make_hT(xt[xs_], f"xt{xs_}", hT[si % 2][:, :, tl * 128:(tl + 1) * 128], f"hT{si % 2}", l,
                        1 if t < 2 else 0, 2, 3, 6)
            for (dst_, src_, key_) in extra[si * per:(si + 1) * per]:
                dma("sp", dst_, src_, w=[key_])

        prep(0)
        for si, t0 in enumerate(sts2):
            ver = 1 if t0 < 2 else 0
            if ver != cur_ver:
                load_ln_tabs(ln, l, 1, ver)
                cur_ver = ver
            s = si % 2
            for grp in range(16):
                pb = grp % 2
                for cc in range(2):
                    j = grp * 2 + cc
                    for kc in range(8):
                        mm(bank(pb)[:, cc * 256:(cc + 1) * 256], lhsT=w1[:, kc, j * 128:(j + 1) * 128], rhs=hT[s][:, kc, :],
                           start=kc == 0, stop=kc == 7, r=["w1", f"hT{s}"], w=[("ps", pb)])
                act(rl[pb], bank(pb), AF.Relu, r=[("ps", pb)], w=[f"rl{pb}"])
                tt("dve", uT[:, grp * 512:(grp + 1) * 512], rl[pb], rl[pb], ALU.mult, r=[f"rl{pb}"], w=["uT"])
            if si + 1 < len(sts2):
                prep(si + 1)
            for tl in range(2):
                t = t0 + tl
                xs_ = (2 * si + tl) % 4
                yb = 2 + 2 * tl
                for g in range(2):
                    for j in range(32):
                        mm(bank(yb + g), lhsT=uT[:, j * 256 + tl * 128:j * 256 + (tl + 1) * 128], rhs=w2[:, j, g * 512:(g + 1) * 512],
                           start=j == 0, stop=j == 31, r=["uT", "w2"], w=[("ps", yb + g)])
                if last:
                    res_ln(ln, yb, yb + 1, xt[xs_], f"xt{xs_}", y_d[(t - 2) * 128:(t - 1) * 128, :], ("y", t))
                else:
                    res_ln(ln, yb, yb + 1, xt[xs_], f"xt{xs_}", dst_d[t * 128:(t + 1) * 128, :], (dst_name, t))
        P.barrier()

    def phase_ret(l, j, src_d, src_name):
        A.off = persist_off
        dp = A.f32(8)
        lg = A.f32(8)
        load_row_bc(dp, ret_decay_d[j:j + 1, :], w=["dp"])
        act(lg, dp, AF.Exp, r=["dp"], w=["lg"])
        ts("dve", lg, lg, -1.0, ALU.mult, r=["lg"], w=["lg"])
        Di = A.f32(128, I32)
        Dq = A.f32(128)
        P.add("pool", lambda e: e.iota(Di, pattern=[[1, 128]], base=0, channel_multiplier=-1), writes=["Di"])
        cp("dve", Dq, Di, r=["Di"], w=["Dq"])
        pos_i = A.f32(128, I32)
        pos = A.f32(128)
        P.add("pool", lambda e: e.iota(pos_i, pattern=[[1, 128]], base=0, channel_multiplier=0), writes=["posi"])
        cp("dve", pos, pos_i, r=["posi"], w=["pos"])
        par_i = A.f32(1, I32)
        par = A.f32(1)
        P.add("pool", lambda e: e.iota(par_i, pattern=[[1, 1]], base=0, channel_multiplier=1), writes=["pari"])
        cp("dve", par, par_i, r=["pari"], w=["par"])
        tmpA = A.f32(128)
        tmpB = A.f32(128)
        maskT = A.f32(2 * 4 * 128).rearrange("p (d h q) -> p d h q", d=2, h=4)
        qdec = A.f32(2 * 4 * 128).rearrange("p (d h q) -> p d h q", d=2, h=4)
        kdec = A.f32(8).rearrange("p (d h) -> p d h", d=2)
        cdec = A.f32(8).rearrange("p (d h) -> p d h", d=2)
        for d in range(2):
            if d == 0:
                ts("dve", tmpA, Dq, 0.0, ALU.max, r=["Dq"], w=["tmpA"])
                ts("dve", tmpB, Dq, 0.0, ALU.is_ge, r=["Dq"], w=["tmpB"])
            else:
                ts("dve", tmpA, Dq, -1.0, ALU.mult, r=["Dq"], w=["tmpA"], s2=0.0, op1=ALU.max)
                ts("dve", tmpB, Dq, 0.0, ALU.is_le, r=["Dq"], w=["tmpB"])
            for h in range(4):
                sc = lg[:, d * 4 + h:d * 4 + h + 1]
                act(maskT[:, d, h, :], tmpA, AF.Exp, r=["tmpA", "lg"], w=["maskT"], scale=sc)
                tt("dve", maskT[:, d, h, :], maskT[:, d, h, :], tmpB, ALU.mult, r=["maskT", "tmpB"], w=["maskT"])
            tq = A.f32(128)
            tk = A.f32(1)
            if d == 0:
                ts("dve", tq, pos, 1.0, ALU.add, r=["pos"], w=[f"tq{d}"])
                ts("dve", tk, par, -1.0, ALU.mult, r=["par"], w=[f"tk{d}"], s2=127.0, op1=ALU.add)
            else:
                ts("dve", tq, pos, -1.0, ALU.mult, r=["pos"], w=[f"tq{d}"], s2=128.0, op1=ALU.add)
                cp("dve", tk, par, r=["par"], w=[f"tk{d}"])
            for h in range(4):
                sc = lg[:, d * 4 + h:d * 4 + h + 1]
                act(qdec[:, d, h, :], tq, AF.Exp, r=[f"tq{d}", "lg"], w=["qdec"], scale=sc)
                act(kdec[:, d, h:h + 1], tk, AF.Exp, r=[f"tk{d}", "lg"], w=["kdec"], scale=sc)
                act(cdec[:, d, h:h + 1], sc, AF.Exp, r=["lg"], w=["cdec"], scale=128.0)
        ret_off = A.off
        ck('consts')

        wqk = A.bf16(8 * 4096).rearrange("p (c n) -> p c n", c=8)
        o1 = A.off
        stage = [A.f32(2048) for _ in range(NSTAGE)]
        wsrcv = ret_w_in_d[j].rearrange("(c p) n -> p c n", p=128)
        for kc in range(8):
            s_ = kc % NSTAGE
            st = stage[s_]
            dma("sp" if s_ % 2 == 0 else "act", st, wsrcv[:, kc, 0:2048], w=[f"stage{s_}"])
            cast(wqk[:, kc, 0:1024], st[:, 0:1024], r=[f"stage{s_}"], w=["wqk"])
            cast(wqk[:, kc, 1024:2048], st[:, 1024:2048], r=[f"stage{s_}"], w=["wqk"], scale=0.0625)
            stv = st.rearrange("p (ch a b) -> p ch a b", a=2, b=64)
            dv = wqk[:, kc, 2048:4096].rearrange("p (ch a b) -> p ch a b", a=2, b=64)
            for a in range(2):
                cast(dv[:, 0:8, a, :], stv[:, 0:8, 1 - a, :], r=[f"stage{s_}"], w=["wqk"])
                cast(dv[:, 8:16, a, :], stv[:, 8:16, 1 - a, :], r=[f"stage{s_}"], w=["wqk"], scale=0.0625)
        P.barrier()
        A.off = o1
        xt = [A.f32(D) for _ in range(2)]
        ropet = [A.f32(4 * 512).rearrange("p (a t) -> p a t", a=4) for _ in range(2)]
        hT = [A.bf16(8 * 512).rearrange("p (c t) -> p c t", c=8) for _ in range(2)]
        qTo = [A.bf16(8 * 512).rearrange("p (c t) -> p c t", c=8) for _ in range(2)]
        kTo = [A.bf16(8 * 512).rearrange("p (c t) -> p c t", c=8) for _ in range(2)]
        t1 = [A.f32(512) for _ in range(2)]
        t2 = [A.f32(512) for _ in range(2)]
        kd = [[A.bf16(1024) for _ in range(2)] for _ in range(2)]
        sts = [(0, 2)] + [(2 + 4 * k, 4) for k in range(8)]
        xi = 0
        ci = 0
        ki = 0
        xi_ = [0]

        def prep_tile(si, tl):
            t0_, ntl_ = sts[si]
            s_ = si % 2
            lat_ = t0_ >= 2
            if tl == 0 and lat_:
                dma("sp", ropet[s_], rope_d[:, :, (t0_ - 2) * 128:(t0_ - 2) * 128 + 512], w=[f"rope{s_}"])
            t = t0_ + tl
            xs_ = xi_[0] % 2
            xi_[0] += 1
            dma("sp", xt[xs_], src_d[t * 128:(t + 1) * 128, :], r=[(src_name, t)], w=[f"xt{xs_}"])
            make_hT(xt[xs_], f"xt{xs_}", hT[s_][:, :, tl * 128:(tl + 1) * 128], f"hT{s_}", l, 0 if lat_ else 1, 0, 1, 6)

        for tl in range(sts[0][1]):
            prep_tile(0, tl)
        for si, (t0, ntl) in enumerate(sts):
            s = si % 2
            lat = t0 >= 2
            ver = 0 if lat else 1
            ntok = ntl * 128
            for c in range(16):
                cs = ci % 2
                ci += 1
                dst = (qTo if c < 8 else kTo)[s]
                dkey = (f"qTo{s}" if c < 8 else f"kTo{s}")
                pa, pb = 2 * cs, 2 * cs + 1
                for kc in range(8):
                    mm(bank(pa)[:, 0:ntok], lhsT=wqk[:, kc, c * 128:(c + 1) * 128], rhs=hT[s][:, kc, 0:ntok],
                       start=kc == 0, stop=kc == 7, r=["wqk", f"hT{s}"], w=[("ps", pa)])
                if lat:
                    for kc in range(8):
                        mm(bank(pb)[:, 0:ntok], lhsT=wqk[:, kc, 2048 + c * 128:2048 + (c + 1) * 128], rhs=hT[s][:, kc, 0:ntok],
                           start=kc == 0, stop=kc == 7, r=["wqk", f"hT{s}"], w=[("ps", pb)])
                    ty = c % 2
                    tt("dve", t1[cs], bank(pa), ropet[s][:, ty, :], ALU.mult, r=[("ps", pa), f"rope{s}"], w=[f"t1{cs}"])
                    tt("dve", t2[cs], bank(pb), ropet[s][:, 2 + ty, :], ALU.mult, r=[("ps", pb), f"rope{s}"], w=[f"t2{cs}"])
                    tt("pool", dst[:, c % 8, :], t1[cs], t2[cs], ALU.add, r=[f"t1{cs}", f"t2{cs}"], w=[dkey])
                else:
                    act(dst[:, c % 8, 0:ntok], bank(pa)[:, 0:ntok], AF.Copy, r=[("ps", pa)], w=[dkey])
                if si + 1 < len(sts) and c % 4 == 3 and (c // 4) < sts[si + 1][1]:
                    prep_tile(si + 1, c // 4)
            for tl in range(ntl):
                t = t0 + tl
                dma("pool", QT_d[t].rearrange("p (c t) -> p c t", c=8), qTo[s][:, :, tl * 128:(tl + 1) * 128], r=[f"qTo{s}"], w=[("QT", t)])
                dma("pool", KT_d[t].rearrange("p (c t) -> p c t", c=8), kTo[s][:, :, tl * 128:(tl + 1) * 128], r=[f"kTo{s}"], w=[("KT", t)])
                ks = ki % 2
                ki += 1
                for c in range(8):
                    tr(bank_bf(4 + ks)[:, c * 128:(c + 1) * 128], kTo[s][:, c, tl * 128:(tl + 1) * 128], ident,
                       r=[f"kTo{s}"], w=[("ps", 4 + ks)])
                for d in range(2):
                    tt("dve", kd[ks][d].rearrange("p (h x) -> p h x", h=4), bank_bf(4 + ks).rearrange("p (h x) -> p h x", h=4),
                       kdec[:, d, :].unsqueeze(2).to_broadcast([128, 4, 256]), ALU.mult,
                       r=[("ps", 4 + ks), "kdec"], w=[f"kd{ks}{d}"])
                    dma("pool", KD_d[d, t], kd[ks][d], r=[f"kd{ks}{d}"], w=[("KD", d, t)])
        P.barrier()
        ck('A1')

        A.off = ret_off
        wvg = A.bf16(8 * 4096).rearrange("p (c n) -> p c n", c=8)
        o1 = A.off
        stage = [A.f32(4096) for _ in range(NSTAGE)]
        load_w(wvg, ret_w_in_d[j], 8, "wvg", stage, col0=2048, ncol=4096)
        P.barrier()
        A.off = o1
        xt = [A.f32(D) for _ in range(2)]
        hT = [A.bf16(1024).rearrange("p (c t) -> p c t", c=8) for _ in range(2)]
        vo = [A.bf16(2048) for _ in range(2)]
        sgo = [A.bf16(2048) for _ in range(2)]
        sgt = [A.f32(512) for _ in range(2)]
        gi = 0

        def prep2(t):
            dma("sp", xt[t % 2], src_d[t * 128:(t + 1) * 128, :], r=[(src_name, t)], w=[f"xt{t % 2}"])
            make_hT(xt[t % 2], f"xt{t % 2}", hT[t % 2], f"hT{t % 2}", l, 1 if t < 2 else 0, 0, 1, 6)

        prep2(0)
        for t in range(NT):
            s = t % 2
            for g in range(8):
                if g == 4 and t + 1 < NT:
                    prep2(t + 1)
                b = gi % 4
                gi += 1
                for kc in range(8):
                    mm(bank(b), lhsT=hT[s][:, kc, :], rhs=wvg[:, kc, g * 512:(g + 1) * 512], start=kc == 0, stop=kc == 7,
                       r=[f"hT{s}", "wvg"], w=[("ps", b)])
                if g < 4:
                    act(vo[s][:, g * 512:(g + 1) * 512], bank(b), AF.Copy, r=[("ps", b)], w=[f"vo{s}"])
                else:
                    sb_ = b % 2
                    act(sgt[sb_], bank(b), AF.Sigmoid, r=[("ps", b)], w=[f"sgt{sb_}"])
                    tt("dve", sgo[s][:, (g - 4) * 512:(g - 3) * 512], bank(b), sgt[sb_], ALU.mult,
                       r=[("ps", b), f"sgt{sb_}"], w=[f"sgo{s}"])
            dma("pool", V_d[t], vo[s], r=[f"vo{s}"], w=[("V", t)])
            dma("pool", SG_d[t], sgo[s], r=[f"sgo{s}"], w=[("SG", t)])
        P.barrier()
        ck('A2')

        A.off = ret_off
        stf = A.f32(8 * 512).rearrange("p (c n) -> p c n", c=8)
        stb = A.bf16(8 * 512).rearrange("p (c n) -> p c n", c=8)
        qTt = [A.bf16(1024) for _ in range(2)]
        kTt = [A.bf16(1024) for _ in range(2)]
        kdt = [A.bf16(1024) for _ in range(2)]
        vt = [A.bf16(2048) for _ in range(2)]
        qd = [A.bf16(1024) for _ in range(2)]
        obt = [A.f32(2048) for _ in range(2)]
        pT = [A.bf16(128) for _ in range(2)]
        for d in (1, 0):
            order = [1, 0] + list(range(NT - 1, 1, -1)) if d == 1 else list(range(NT))
            memset("dve", stf, 0.0, w=[f"stf{c}" for c in range(8)])
            memset("pool", stb, 0.0, w=[f"stb{c}" for c in range(8)])
            for n, t in enumerate(order):
                s = n % 2
                dma("sp", qTt[s], QT_d[t], r=[("QT", t)], w=[f"qTt{s}"])
                dma("sp", kTt[s], KT_d[t], r=[("KT", t)], w=[f"kTt{s}"])
                dma("sp", kdt[s], KD_d[d, t], r=[("KD", d, t)], w=[f"kdt{s}"])
                dma("sp", vt[s], V_d[t], r=[("V", t)], w=[f"vt{s}"])
                if d == 0:
                    dma("sp", obt[s], OB_d[t], r=[("OB", t)], w=[f"obt{s}"])
                tt("dve", qd[s].rearrange("p (h c q) -> p h c q", h=4, c=2), qTt[s].rearrange("p (h c q) -> p h c q", h=4, c=2),
                   qdec[:, d, :, :].unsqueeze(2).to_broadcast([128, 4, 2, 128]), ALU.mult, r=[f"qTt{s}", "qdec"], w=[f"qd{s}"])
                for h in range(4):
                    hs = h % 2
                    bS, bO, bU = hs, 2 + hs, 4 + 2 * hs
                    for dc in range(2):
                        c = 2 * h + dc
                        mm(bank(bS)[:, 0:128], lhsT=kTt[s][:, c * 128:(c + 1) * 128], rhs=qTt[s][:, c * 128:(c + 1) * 128],
                           start=dc == 0, stop=dc == 1, r=[f"kTt{s}", f"qTt{s}"], w=[("ps", bS)])
                    tt("dve", pT[hs], bank(bS)[:, 0:128], maskT[:, d, h, :], ALU.mult, r=[("ps", bS), "maskT"], w=[f"pT{hs}"])
                    for dc in range(2):
                        c = 2 * h + dc
                        mm(bank(bU + dc), lhsT=kdt[s][:, c * 128:(c + 1) * 128], rhs=vt[s][:, h * 512:(h + 1) * 512],
                           start=True, stop=True, r=[f"kdt{s}", f"vt{s}"], w=[("ps", bU + dc)])
                    mm(bank(bO), lhsT=pT[hs], rhs=vt[s][:, h * 512:(h + 1) * 512], start=True, stop=False,
                       r=[f"pT{hs}", f"vt{s}"], w=[("ps", bO)])
                    for dc in range(2):
                        c = 2 * h + dc
                        mm(bank(bO), lhsT=qd[s][:, c * 128:(c + 1) * 128], rhs=stb[:, c, :], start=False, stop=dc == 1,
                           r=[f"qd{s}", f"stb{c}"], w=[("ps", bO)])
                    if d == 1:
                        act(obt[s][:, h * 512:(h + 1) * 512], bank(bO), AF.Copy, r=[("ps", bO)], w=[f"obt{s}"])
                    else:
                        tt("dve", obt[s][:, h * 512:(h + 1) * 512], bank(bO), obt[s][:, h * 512:(h + 1) * 512], ALU.add,
                           r=[("ps", bO), f"obt{s}"], w=[f"obt{s}"])
                    for dc in range(2):
                        c = 2 * h + dc
                        stt("dve", stf[:, c, :], stf[:, c, :], cdec[:, d, h:h + 1], bank(bU + dc), ALU.mult, ALU.add,
                            r=[f"stf{c}", ("ps", bU + dc), "cdec"], w=[f"stf{c}"])
                        act(stb[:, c, :], stf[:, c, :], AF.Copy, r=[f"stf{c}"], w=[f"stb{c}"])
                dma("pool", OB_d[t], obt[s], r=[f"obt{s}"], w=[("OB", t)])
        P.barrier()

        A.off = ret_off
        wo = A.bf16(16 * 1024).rearrange("p (c n) -> p c n", c=16)
        o1 = A.off
        stage = [A.f32(4096) for _ in range(NSTAGE)]
        load_w(wo, ret_w_o_d[j], 16, "wo", stage, kstep=4)
        P.barrier()
        A.off = o1
        ln = alloc_ln()
        ost = [A.f32(2048) for _ in range(2)]
        sgt_ = [A.bf16(2048) for _ in range(2)]
        xt = [A.f32(D) for _ in range(2)]
        onf = [A.f32(512) for _ in range(2)]
        onb = [A.bf16(2048) for _ in range(2)]
        onT = [A.bf16(2048) for _ in range(2)]
        gst = [A.f32(4 * 6).rearrange("p (h x) -> p h x", h=4) for _ in range(2)]
        gmv = [A.f32(4 * 2).rearrange("p (h x) -> p h x", h=4) for _ in range(2)]
        grs = [A.f32(8) for _ in range(2)]
        xt3 = xt + [A.f32(D)]

        def d_stage1(t):
            s = t % 2
            dma("sp", ost[s], OB_d[t], r=[("OB", t)], w=[f"ost{s}"])
            dma("sp", sgt_[s], SG_d[t], r=[("SG", t)], w=[f"sgt_{s}"])
            dma("sp", xt3[t % 3], src_d[t * 128:(t + 1) * 128, :], r=[(src_name, t)], w=[f"xt{t % 3}"])
            for h in range(4):
                P.add("dve", lambda e, h=h, s=s: e.bn_stats(out=gst[s][:, h, :], in_=ost[s][:, h * 512:(h + 1) * 512]),
                      reads=[f"ost{s}"], writes=[f"gst{s}"])
                P.add("dve", lambda e, h=h, s=s: e.bn_aggr(out=gmv[s][:, h, :], in_=gst[s][:, h:h + 1, :]),
                      reads=[f"gst{s}"], writes=[f"gmv{s}"])
            act(grs[s][:, 0:4], gmv[s][:, :, 1], AF.Ln, r=[f"gmv{s}"], w=[f"grs{s}"], bias=epsc)
            act(grs[s][:, 0:4], grs[s][:, 0:4], AF.Exp, r=[f"grs{s}"], w=[f"grs{s}"], scale=-0.5)
            stt("dve", grs[s][:, 4:8], gmv[s][:, :, 0], -1.0, grs[s][:, 0:4], ALU.mult, ALU.mult,
                r=[f"gmv{s}", f"grs{s}"], w=[f"grs{s}"])
            for h in range(4):
                hs = h % 2
                act(onf[hs], ost[s][:, h * 512:(h + 1) * 512], AF.Identity, r=[f"ost{s}", f"grs{s}"], w=[f"onf{hs}"],
                    scale=grs[s][:, h:h + 1], bias=grs[s][:, 4 + h:5 + h])
                tt("pool", onb[s][:, h * 512:(h + 1) * 512], onf[hs], sgt_[s][:, h * 512:(h + 1) * 512], ALU.mult,
                   r=[f"onf{hs}", f"sgt_{s}"], w=[f"onb{s}"])

        def d_stage2(t):
            s = t % 2
            tb = 2 * s
            for c in range(16):
                tr(bank_bf(tb, 2)[:, c * 128:(c + 1) * 128], onb[s][:, c * 128:(c + 1) * 128], ident, r=[f"onb{s}"],
                   w=[("ps", tb), ("ps", tb + 1)])
            act(onT[s], bank_bf(tb, 2), AF.Copy, r=[("ps", tb), ("ps", tb + 1)], w=[f"onT{s}"])
            yb = 4 + 2 * s
            for g in range(2):
                for c in range(16):
                    mm(bank(yb + g), lhsT=onT[s][:, c * 128:(c + 1) * 128], rhs=wo[:, c, g * 512:(g + 1) * 512],
                       start=c == 0, stop=c == 15, r=[f"onT{s}", "wo"], w=[("ps", yb + g)])

        cur_ver_ = [None]

        def d_stage3(t):
            s = t % 2
            ver = 1 if t < 2 else 0
            if ver != cur_ver_[0]:
                load_ln_tabs(ln, l, 0, ver)
                cur_ver_[0] = ver
            yb = 4 + 2 * s
            res_ln(ln, yb, yb + 1, xt3[t % 3], f"xt{t % 3}", XM_d[t * 128:(t + 1) * 128, :], ("XM", t))

        for i in range(NT + 2):
            if i < NT:
                d_stage1(i)
            if 0 <= i - 1 < NT:
                d_stage2(i - 1)
            if 0 <= i - 2 < NT:
                d_stage3(i - 2)
        P.barrier()

    def phase_nat(l, j, src_d, src_name, last):
        A.off = persist_off
        wn = A.bf16(8 * 3072).rearrange("p (c n) -> p c n", c=8)
        o1 = A.off
        stage = [A.f32(3072) for _ in range(NSTAGE)]
        load_w(wn, nat_w_in_d[j], 8, "wn", stage)
        P.barrier()
        A.off = o1
        xt = [A.f32(D) for _ in range(2)]
        hT = [A.bf16(8 * 512).rearrange("p (c t) -> p c t", c=8) for _ in range(2)]
        qTo = [A.bf16(8 * 512).rearrange("p (c t) -> p c t", c=8) for _ in range(2)]
        kTo = [A.bf16(8 * 512).rearrange("p (c t) -> p c t", c=8) for _ in range(2)]
        vno = [A.bf16(1056) for _ in range(2)]
        for s in range(2):
            memset("pool", vno[s], 1.0, w=[f"vno{s}"])
        sts = [(0, 2)] + [(2 + 4 * k, 4) for k in range(8)]
        ci = 0
        vi = 0
        xi_ = [0]

        def prep_tile(si, tl):
            t0_, ntl_ = sts[si]
            s_ = si % 2
            t = t0_ + tl
            xs_ = xi_[0] % 2
            xi_[0] += 1
            dma("sp", xt[xs_], src_d[t * 128:(t + 1) * 128, :], r=[(src_name, t)], w=[f"xt{xs_}"])
            make_hT(xt[xs_], f"xt{xs_}", hT[s_][:, :, tl * 128:(tl + 1) * 128], f"hT{s_}", l, 0 if t0_ >= 2 else 1, 0, 1, 6)

        for tl in range(sts[0][1]):
            prep_tile(0, tl)
        for si, (t0, ntl) in enumerate(sts):
            s = si % 2
            ver = 0 if t0 >= 2 else 1
            ntok = ntl * 128
            for c in range(16):
                b = ci % 4
                ci += 1
                for kc in range(8):
                    mm(bank(b)[:, 0:ntok], lhsT=wn[:, kc, c * 128:(c + 1) * 128], rhs=hT[s][:, kc, 0:ntok],
                       start=kc == 0, stop=kc == 7, r=["wn", f"hT{s}"], w=[("ps", b)])
                if c < 8:
                    act(qTo[s][:, c, 0:ntok], bank(b)[:, 0:ntok], AF.Identity, r=[("ps", b)], w=[f"qTo{s}"], scale=NAT_SCALE)
                else:
                    cp("dve", kTo[s][:, c - 8, 0:ntok], bank(b)[:, 0:ntok], r=[("ps", b)], w=[f"kTo{s}"])
                if si + 1 < len(sts) and c % 4 == 3 and (c // 4) < sts[si + 1][1]:
                    prep_tile(si + 1, c // 4)
            for tl in range(ntl):
                t = t0 + tl
                dma("pool", QT_d[t].rearrange("p (c t) -> p c t", c=8), qTo[s][:, :, tl * 128:(tl + 1) * 128], r=[f"qTo{s}"], w=[("QT", t)])
                dma("pool", KT_d[t].rearrange("p (c t) -> p c t", c=8), kTo[s][:, :, tl * 128:(tl + 1) * 128], r=[f"kTo{s}"], w=[("KT", t)])
                vs = vi % 2
                vi += 1
                for g in range(2):
                    b = 4 + g
                    for kc in range(8):
                        mm(bank(b), lhsT=hT[s][:, kc, tl * 128:(tl + 1) * 128], rhs=wn[:, kc, 2048 + g * 512:2048 + (g + 1) * 512],
                           start=kc == 0, stop=kc == 7, r=[f"hT{s}", "wn"], w=[("ps", b)])
                    act(vno[vs].rearrange("p (h x) -> p h x", x=33)[:, g * 16:(g + 1) * 16, 0:32],
                        bank(b).rearrange("p (h x) -> p h x", x=32), AF.Copy, r=[("ps", b)], w=[f"vno{vs}"])
                dma("pool", VN_d[t], vno[vs], r=[f"vno{vs}"], w=[("VN", t)])
        P.barrier()
        ck('NA')

        A.off = persist_off
        cmask = A.f32(128)
        dma("sp", cmask, colmask_d[:, :], w=["cmask"])
        bias_int = A.bf16(32 * 5 * 128).rearrange("p (h j q) -> p h j q", h=32, j=5)
        NBB = 8
        bstage = [A.f32(5 * 128).rearrange("p (j q) -> p j q", j=5) for _ in range(NBB)]
        bias_b = [A.bf16(5 * 128).rearrange("p (j q) -> p j q", j=5) for _ in range(NBB)]
        bcount = [0]

        def expand_bias(ty, h, dst, dkey):
            s = bcount[0] % NBB
            bcount[0] += 1
            dma("sp", bstage[s], BIAS_d[ty, h].rearrange("j p q -> p j q"), r=[("BIAS", ty, h)], w=[f"bstage{s}"])
            tt("dve", bstage[s], bstage[s], cmask.unsqueeze(1).to_broadcast([128, 5, 128]), ALU.add,
               r=[f"bstage{s}", "cmask"], w=[f"bstage{s}"])
            act(dst, bstage[s], AF.Exp, r=[f"bstage{s}"], w=[dkey])

        for h in range(32):
            expand_bias(0, h, bias_int[:, h, :, :], "bias_int")
        kring = [A.bf16(1024) for _ in range(8)]
        vring = [A.bf16(1056) for _ in range(8)]
        kctx = [A.bf16(1024) for _ in range(2)]
        vctx = [A.bf16(1056) for _ in range(2)]
        for c in range(2):
            dma("sp", kctx[c], KT_d[c], r=[("KT", c)], w=[f"kctx{c}"])
            dma("sp", vctx[c], VN_d[c], r=[("VN", c)], w=[f"vctx{c}"])
        qTt = [A.bf16(1024) for _ in range(2)]
        qm = [A.bf16(8 * 4 * 128).rearrange("p (c h q) -> p c h q", c=8, h=4) for _ in range(2)]
        hmask_f = A.f32(4)
        hmask = A.bf16(4)
        memset("pool", hmask_f, 1.0, w=["hmask_f"])
        P.add("pool", lambda e: e.affine_select(out=hmask_f, in_=hmask_f, pattern=[[-32, 4]], compare_op=ALU.is_ge,
                                                fill=0.0, base=0, channel_multiplier=1), reads=["hmask_f"], writes=["hmask_f"])
        P.add("pool", lambda e: e.affine_select(out=hmask_f, in_=hmask_f, pattern=[[32, 4]], compare_op=ALU.is_ge,
                                                fill=0.0, base=31, channel_multiplier=-1), reads=["hmask_f"], writes=["hmask_f"])
        cp("dve", hmask, hmask_f, r=["hmask_f"], w=["hmask"])
        pT = [A.bf16(896) for _ in range(3)]
        rden = A.f32(32)
        ob = [A.bf16(1024) for _ in range(2)]
        loaded = set()
        qtiles = ([] if last else [0, 1]) + list(range(2, NT))
        PVREG = ((6, 0, 0, 15), (7, 0, 15, 15), (1, 384, 30, 2))

        def pv_dst(h):
            if h < 15:
                return bank(6)[:, 33 * h:33 * h + 33], [("ps", 6)]
            if h < 30:
                return bank(7)[:, 33 * (h - 15):33 * (h - 15) + 33], [("ps", 7)]
            o_ = 384 + 33 * (h - 30)
            return bank(1)[:, o_:o_ + 33], ["pvtail", ("ps", 1)]

        def tail_norm(n):
            t = qtiles[n]
            o_ = ob[n % 2]
            okey = f"ob{n % 2}"
            for (pvb, co, h0, nh) in PVREG:
                rk = ["pvtail"] if pvb == 1 else [("ps", pvb)]
                view = bank(pvb)[:, co:co + nh * 33].rearrange("p (h x) -> p h x", x=33)
                P.add("dve", lambda e, view=view, h0=h0, nh=nh: e.reciprocal(out=rden[:, h0:h0 + nh].unsqueeze(2), in_=view[:, :, 32:33]),
                      reads=rk, writes=["rden"])
                tt("dve", o_[:, h0 * 32:(h0 + nh) * 32].rearrange("p (h x) -> p h x", x=32), view[:, :, 0:32],
                   rden[:, h0:h0 + nh].unsqueeze(2).to_broadcast([128, nh, 32]), ALU.mult, r=rk + ["rden"], w=[okey])
            dma("pool", OA_d[t], o_, r=[okey], w=[("OA", t)])

        gcount = 0

        def tile_params(t):
            if t < 2:
                return 0, 0, 0
            T = t - 2
            ty_ = 1 if T == 0 else 2 if T == 1 else 3 if T == 30 else 4 if T == 31 else 0
            return 5, ty_, min(max(T - 2, 0), 27)

        def prefetch(n):
            t = qtiles[n]
            s_ = n % 2
            dma("sp", qTt[s_], QT_d[t], r=[("QT", t)], w=[f"qTt{s_}"])
            tt("dve", qm[s_], qTt[s_].rearrange("p (c q) -> p c q", c=8).unsqueeze(2).to_broadcast([128, 8, 4, 128]),
               hmask.unsqueeze(1).unsqueeze(3).to_broadcast([128, 8, 4, 128]), ALU.mult, r=[f"qTt{s_}", "hmask"], w=[f"qm{s_}"])
            nloc_, _, wt_ = tile_params(t)
            if nloc_:
                for kt in range(wt_, wt_ + 5):
                    if kt not in loaded:
                        loaded.add(kt)
                        dma("sp", kring[kt % 8], KT_d[kt + 2], r=[("KT", kt + 2)], w=[f"kring{kt % 8}"])
                        dma("sp", vring[kt % 8], VN_d[kt + 2], r=[("VN", kt + 2)], w=[f"vring{kt % 8}"])

        prefetch(0)
        for n, t in enumerate(qtiles):
            s = n % 2
            is_ctx = t < 2
            nloc, ty, wt = tile_params(t)
            nch = nloc + 2
            g0 = gcount
            gcount += 32

            def emit_S(h):
                hp = (g0 + h) % 3
                hc, pb = h // 4, 32 * (h % 4)
                sb0 = 2 * hp
                psS = bank(sb0, 2)
                skeys = [("ps", sb0), ("ps", sb0 + 1)]
                if nloc and ty != 0 and h + 6 < 32:
                    expand_bias(ty, h + 6, bias_b[(h + 6) % NBB], f"bias_b{(h + 6) % NBB}")
                for jj in range(nch):
                    if jj < nloc:
                        kt = wt + jj
                        ksrc, kkey = kring[kt % 8], f"kring{kt % 8}"
                    else:
                        ksrc, kkey = kctx[jj - nloc], f"kctx{jj - nloc}"
                    wk = [skeys[(jj * 128) // 512]]
                    if wk[0] == ("ps", 1):
                        wk = wk + ["pvtail"]
                    mm(psS[:, jj * 128:(jj + 1) * 128], lhsT=ksrc[:, hc * 128:(hc + 1) * 128],
                       rhs=qm[s][:, hc, h % 4, :], start=True, stop=True, r=[kkey, f"qm{s}"], w=wk)
                act(pT[hp][:, 0:nch * 128], psS[:, 0:nch * 128], AF.Exp, r=skeys, w=[f"pT{hp}"])
                if nloc:
                    if ty == 0:
                        bsrc, bkey = bias_int[:, h, :, :], "bias_int"
                    else:
                        bsrc, bkey = bias_b[h % NBB], f"bias_b{h % NBB}"
                    pv_ = pT[hp][:, 0:640].rearrange("p (j q) -> p j q", j=5)
                    tt("dve", pv_, pv_, bsrc, ALU.mult, r=[f"pT{hp}", bkey], w=[f"pT{hp}"])

            def emit_PV(h):
                hp = (g0 + h) % 3
                dst, dkeys = pv_dst(h)
                for jj in range(nch):
                    if jj < nloc:
                        kt = wt + jj
                        vsrc, vkey = vring[kt % 8], f"vring{kt % 8}"
                    else:
                        vsrc, vkey = vctx[jj - nloc], f"vctx{jj - nloc}"
                    mm(dst, lhsT=pT[hp][:, jj * 128:(jj + 1) * 128], rhs=vsrc[:, h * 33:(h + 1) * 33],
                       start=jj == 0, stop=jj == nch - 1, r=[f"pT{hp}", vkey], w=dkeys)

            if nloc and ty != 0:
                for h in range(6):
                    expand_bias(ty, h, bias_b[h % NBB], f"bias_b{h % NBB}")
            emit_S(0)
            emit_S(1)
            emit_S(2)
            for h in range(3, 32):
                emit_PV(h - 3)
                emit_S(h)
                if h == 16 and n + 1 < len(qtiles):
                    prefetch(n + 1)
            emit_PV(29)
            emit_PV(30)
            emit_PV(31)
            tail_norm(n)
        P.barrier()

        A.off = persist_off
        won = A.bf16(8 * 1024).rearrange("p (c n) -> p c n", c=8)
        o1 = A.off
        stage = [A.f32(4096), A.f32(4096)]
        load_w(won, nat_w_o_d[j], 8, "won", stage, kstep=4)
        P.barrier()
        A.off = o1
        ln = alloc_ln()
        obt = [A.bf16(1024) for _ in range(2)]
        oT = [A.bf16(1024) for _ in range(2)]
        xt = [A.f32(D) for _ in range(3)]
        nq = len(qtiles)
        cur_ver_ = [None]

        def c_stage1(i):
            t = qtiles[i]
            dma("sp", obt[i % 2], OA_d[t], r=[("OA", t)], w=[f"obt{i % 2}"])
            dma("sp", xt[i % 3], src_d[t * 128:(t + 1) * 128, :], r=[(src_name, t)], w=[f"xt{i % 3}"])

        def c_stage2(i):
            s_ = i % 2
            for c in range(8):
                tr(bank_bf(s_)[:, c * 128:(c + 1) * 128], obt[s_][:, c * 128:(c + 1) * 128], ident, r=[f"obt{s_}"], w=[("ps", s_)])
            act(oT[s_], bank_bf(s_), AF.Copy, r=[("ps", s_)], w=[f"oT{s_}"])
            yb = 2 + 2 * s_
            for g in range(2):
                for c in range(8):
                    mm(bank(yb + g), lhsT=oT[s_][:, c * 128:(c + 1) * 128], rhs=won[:, c, g * 512:(g + 1) * 512],
                       start=c == 0, stop=c == 7, r=[f"oT{s_}", "won"], w=[("ps", yb + g)])

        def c_stage3(i):
            t = qtiles[i]
            ver = 1 if t < 2 else 0
            if ver != cur_ver_[0]:
                load_ln_tabs(ln, l, 0, ver)
                cur_ver_[0] = ver
            yb = 2 + 2 * (i % 2)
            res_ln(ln, yb, yb + 1, xt[i % 3], f"xt{i % 3}", XM_d[t * 128:(t + 1) * 128, :], ("XM", t))

        for i in range(nq + 2):
            if i < nq:
                c_stage1(i)
            if 0 <= i - 1 < nq:
                c_stage2(i - 1)
            if 0 <= i - 2 < nq:
                c_stage3(i - 2)
        P.barrier()

    try:
        phase_mods()
        src_d, src_name = xs_d, "xs"
        for l in range(n_layers):
            last = l == DEPTH - 1
            if l % 2 == 0:
                phase_ret(l, l // 2, src_d, src_name)
            else:
                phase_nat(l, l // 2, src_d, src_name, last)
            ck('mixer%d' % l)
            nxt_nat = (l + 1 < n_layers) and ((l + 1) % 2 == 1)
            phase_mlp(l, XM_d, "XM", XN_d, "XN", last, extra=bias_expansion_thunks((l + 1) // 2) if nxt_nat else ())
            src_d, src_name = XN_d, "XN"
    except _Stop:
        P.barrier()
    if dbg:
        A.off = persist_off
        for name, shape in dbg.items():
            src = {"XM": XM_d, "XN": XN_d}[name]
            buf = A.f32(D)
            for t in range(shape[0] // 128):
                dma("sp", buf, src[t * 128:(t + 1) * 128, :], r=[(name, t)], w=["dbgbuf"])
                dma("sp", dbg_outs[name][t * 128:(t + 1) * 128, :], buf, r=["dbgbuf"], w=[("dbg", name, t)])
    nsig = P.emit()
    es.close()
    return nc, {s: len(P.ops[s]) for s in STREAMS}, nsig, A.peak


def _rope_tables():
    f = np.arange(64, dtype=np.float32)
    inv = (1.0 / (np.float32(10000.0) ** (np.arange(0, 128, 2, dtype=np.float32) / np.float32(128)))).astype(np.float32)
    t = np.arange(SEQ)
    row = (t // 64).astype(np.float32)
    col = (t % 64).astype(np.float32)
    tab = np.zeros((128, 4, SEQ), np.float32)
    for ty, posv in enumerate((row, col)):
        ang = (posv[None, :] * inv[:, None]).astype(np.float32)
        c = np.cos(ang).astype(np.float32)
        s = np.sin(ang).astype(np.float32)
        tab[0:64, ty, :] = c
        tab[64:128, ty, :] = c
        tab[0:64, 2 + ty, :] = -s
        tab[64:128, 2 + ty, :] = s
    return tab


def _colmask():
    m = np.full((128, 128), NEG, np.float32)
    for qc in range(64):
        cs = min(max(qc - 8, 0), 48)
        for i in range(2):
            for jq in range(2):
                m[i * 64 + cs:i * 64 + cs + 16, jq * 64 + qc] = 0.0
    return m


_VAR = ((-4, 3), (0, 7), (-1, 6), (-2, 5), (-3, 4), (-5, 2), (-6, 1), (-7, 0))


def _pad_tables(rpb):
    pad = np.full((2, 8, 32, 20, 127), NEG, np.float32)
    rev = rpb[:, :, :, ::-1]
    for v, (lo, hi) in enumerate(_VAR):
        for dr in range(lo, hi + 1):
            pad[:, v, :, dr + 10, 48:79] = rev[:, :, dr + 7, :]
    return pad


_CACHE = {}


def kernel(x, c, ctx, c_ctx, ada_w, ada_b, ret_w_in, ret_w_o, ret_decay, nat_w_in, nat_w_o, nat_rpb,
           mlp_w1, mlp_w2, ln_g, ln_b):
    f = lambda a: np.ascontiguousarray(np.asarray(a, dtype=np.float32))
    x, c, ctx, c_ctx = f(x), f(c), f(ctx), f(c_ctx)
    if "nc" not in _CACHE:
        _CACHE["nc"] = build()[0]
    nc = _CACHE["nc"]
    shared = {
        "ada_w": f(ada_w), "ada_b": f(ada_b),
        "ada_bT": f(np.asarray(ada_b).reshape(DEPTH, 48, 128).transpose(0, 2, 1)),
        "ret_w_in": f(ret_w_in), "ret_w_o": f(ret_w_o), "ret_decay": f(np.asarray(ret_decay).reshape(2, 8)),
        "nat_w_in": f(nat_w_in), "nat_w_o": f(nat_w_o),
        "pad": _pad_tables(f(nat_rpb)), "colmask": _colmask(), "rope": _rope_tables(),
        "mlp_w1": f(mlp_w1), "mlp_w2": f(mlp_w2), "ln_g": f(ln_g), "ln_b": f(ln_b),
    }
    in_maps = []
    for b in range(8):
        m = dict(shared)
        m["xs"] = np.ascontiguousarray(np.concatenate([ctx[b], x[b]], axis=0))
        cv = np.stack([c[b], c_ctx], axis=1)
        m["cvT"] = np.ascontiguousarray(cv.reshape(8, 128, 2).transpose(1, 0, 2))
        in_maps.append(m)
    res = run_bass_kernel_spmd(nc, in_maps, core_ids=list(range(8)))
    return np.stack([np.asarray(r["y"], dtype=np.float32) for r in res.results], axis=0)
```
